# Optimizing a Trainium2 kernel written in Bass

```python
import jax, jax.numpy as jnp
from jax import lax
import numpy as np

D_MODEL = 1024
BATCH = 2
SEQ = 16384
DEPTH = 2
DEC_BATCH = 4
DEC_SEQ = 4096
PAST_LEN = 128

N_HEADS = 8
N_KV_HEADS = 2
HEAD_DIM = D_MODEL // N_HEADS
GROUP = N_HEADS // N_KV_HEADS
QKV_DIM = (N_HEADS + 2 * N_KV_HEADS) * HEAD_DIM
ROPE_THETA = 10000.0
ROPE_SEG = HEAD_DIM // 2
ROPE_FREQS = ROPE_SEG // 2
GRID_W = 64
Q_BLOCK = 128
POOL_WINDOWS = (2, 4, 8, 16)
N_POOL_GROUPS = len(POOL_WINDOWS)
POOL_GROUP_DIM = D_MODEL // N_POOL_GROUPS
D_FF = 2816
CONV_W = 3
EPS = 1e-6
N_MIXERS = 2
N_ATTN_LAYERS = (DEPTH + 1) // 2
N_POOL_LAYERS = DEPTH // 2

kernel_name = 'hybrid_attn_pool_convffn_encoder'


def rmsnorm(x, g):
    xf = x.astype(jnp.float32)
    y = xf * lax.rsqrt(jnp.mean(xf * xf, axis=-1, keepdims=True) + EPS)
    return (y * g.astype(jnp.float32)).astype(x.dtype)


def axial_rope_tables(seq_len):
    rows = seq_len // GRID_W
    row = jnp.repeat(jnp.arange(rows, dtype=jnp.float32), GRID_W)
    col = jnp.tile(jnp.arange(GRID_W, dtype=jnp.float32), rows)
    inv = ROPE_THETA ** (-jnp.arange(ROPE_FREQS, dtype=jnp.float32) / ROPE_FREQS)
    ang = jnp.stack([row[:, None] * inv, col[:, None] * inv], axis=1)
    return jnp.cos(ang), jnp.sin(ang)


def apply_rope(x, cos, sin):
    b, s, h, d = x.shape
    xs = x.astype(jnp.float32).reshape(b, s, h, 2, 2, ROPE_FREQS)
    x1, x2 = xs[..., 0, :], xs[..., 1, :]
    c = cos[None, :, None]
    sn = sin[None, :, None]
    out = jnp.stack([x1 * c - x2 * sn, x2 * c + x1 * sn], axis=-2)
    return out.reshape(b, s, h, d).astype(x.dtype)


def attention_mixer(h, w_qkv, q_gain, k_gain, w_o):
    b, s, _ = h.shape
    qkv = h @ w_qkv
    q = qkv[..., :N_HEADS * HEAD_DIM].reshape(b, s, N_HEADS, HEAD_DIM)
    k = qkv[..., N_HEADS * HEAD_DIM:(N_HEADS + N_KV_HEADS) * HEAD_DIM].reshape(b, s, N_KV_HEADS, HEAD_DIM)
    v = qkv[..., (N_HEADS + N_KV_HEADS) * HEAD_DIM:].reshape(b, s, N_KV_HEADS, HEAD_DIM)
    q = rmsnorm(q, q_gain)
    k = rmsnorm(k, k_gain)
    cos, sin = axial_rope_tables(s)
    q = apply_rope(q, cos, sin)
    k = apply_rope(k, cos, sin)
    nb = s // Q_BLOCK
    qb = q.reshape(b, nb, Q_BLOCK, N_KV_HEADS, GROUP, HEAD_DIM).transpose(1, 0, 2, 3, 4, 5)
    scale = HEAD_DIM ** -0.5

    def block(qi):
        sc = jnp.einsum('bqkgd,bskd->bkgqs', qi, k, preferred_element_type=jnp.float32) * scale
        p = jax.nn.softmax(sc, axis=-1).astype(v.dtype)
        return jnp.einsum('bkgqs,bskd->bqkgd', p, v)

    ob = lax.map(block, qb)
    o = ob.transpose(1, 0, 2, 3, 4, 5).reshape(b, s, N_HEADS * HEAD_DIM)
    return o @ w_o


def pool_mixer(h, w_group, scale):
    b, s, d = h.shape
    hf = h.astype(jnp.float32).reshape(b, s, N_POOL_GROUPS, POOL_GROUP_DIM)
    csum = jnp.concatenate([jnp.zeros((b, 1, N_POOL_GROUPS, POOL_GROUP_DIM), jnp.float32),
                            jnp.cumsum(hf, axis=1)], axis=1)
    t = jnp.arange(s)
    means = []
    for gi, w in enumerate(POOL_WINDOWS):
        lo = jnp.clip(t - w // 2, 0, s)
        hi = jnp.clip(t + w // 2, 0, s)
        cnt = (hi - lo).astype(jnp.float32)
        means.append((csum[:, hi, gi] - csum[:, lo, gi]) / cnt[None, :, None])
    pooled = jnp.stack(means, axis=2) - hf
    mixed = jnp.einsum('bsgc,gcd->bsgd', pooled.astype(h.dtype), w_group).reshape(b, s, d)
    return mixed * scale


def conv_ffn(h, w_up, conv_w, conv_b, w_down):
    u = h @ w_up
    up = jnp.pad(u, ((0, 0), (1, 1), (0, 0)))
    c = up[:, :-2] * conv_w[0] + up[:, 1:-1] * conv_w[1] + up[:, 2:] * conv_w[2] + conv_b
    gate, val = jnp.split(c, 2, axis=-1)
    return (jax.nn.silu(gate) * val) @ w_down


def trunk(x, attn_norm, w_qkv, q_gain, k_gain, w_o, pool_norm, w_pool, pool_scale,
          ffn_norm, w_up, conv_w, conv_b, w_down):
    for i in range(DEPTH):
        j = i // N_MIXERS
        if i % N_MIXERS == 0:
            x = x + attention_mixer(rmsnorm(x, attn_norm[j]), w_qkv[j], q_gain[j], k_gain[j], w_o[j])
        else:
            x = x + pool_mixer(rmsnorm(x, pool_norm[j]), w_pool[j], pool_scale[j])
        x = x + conv_ffn(rmsnorm(x, ffn_norm[i]), w_up[i], conv_w[i], conv_b[i], w_down[i])
    return x


def setup_inputs(seed: int = 0) -> dict:
    key = jax.random.key(seed)
    ks = jax.random.split(key, 20)
    f32 = jnp.float32
    nrm = lambda k, shp, s: jax.random.normal(k, shp, f32) * s
    return {
        'x_prompt': nrm(ks[0], (BATCH, SEQ, D_MODEL), 1.0),
        'x_sample': nrm(ks[1], (DEC_BATCH, DEC_SEQ, D_MODEL), 1.0),
        'attn_norm': 1.0 + nrm(ks[2], (N_ATTN_LAYERS, D_MODEL), 0.05),
        'w_qkv': nrm(ks[3], (N_ATTN_LAYERS, D_MODEL, QKV_DIM), D_MODEL ** -0.5),
        'q_gain': 1.0 + nrm(ks[4], (N_ATTN_LAYERS, HEAD_DIM), 0.05),
        'k_gain': 1.0 + nrm(ks[5], (N_ATTN_LAYERS, HEAD_DIM), 0.05),
        'w_o': nrm(ks[6], (N_ATTN_LAYERS, N_HEADS * HEAD_DIM, D_MODEL), (N_HEADS * HEAD_DIM) ** -0.5),
        'pool_norm': 1.0 + nrm(ks[7], (N_POOL_LAYERS, D_MODEL), 0.05),
        'w_pool': nrm(ks[8], (N_POOL_LAYERS, N_POOL_GROUPS, POOL_GROUP_DIM, POOL_GROUP_DIM), POOL_GROUP_DIM ** -0.5),
        'pool_scale': 0.5 + nrm(ks[9], (N_POOL_LAYERS, D_MODEL), 0.1),
        'ffn_norm': 1.0 + nrm(ks[10], (DEPTH, D_MODEL), 0.05),
        'w_up': nrm(ks[11], (DEPTH, D_MODEL, 2 * D_FF), D_MODEL ** -0.5),
        'conv_w': nrm(ks[12], (DEPTH, CONV_W, 2 * D_FF), CONV_W ** -0.5),
        'conv_b': nrm(ks[13], (DEPTH, 2 * D_FF), 0.02),
        'w_down': nrm(ks[14], (DEPTH, D_FF, D_MODEL), D_FF ** -0.5),
    }


def reference(x_prompt, x_sample, attn_norm, w_qkv, q_gain, k_gain, w_o, pool_norm, w_pool,
              pool_scale, ffn_norm, w_up, conv_w, conv_b, w_down):
    y_prompt = trunk(x_prompt, attn_norm, w_qkv, q_gain, k_gain, w_o, pool_norm, w_pool,
                     pool_scale, ffn_norm, w_up, conv_w, conv_b, w_down)
    y_sample = trunk(x_sample, attn_norm, w_qkv, q_gain, k_gain, w_o, pool_norm, w_pool,
                     pool_scale, ffn_norm, w_up, conv_w, conv_b, w_down)
    return (y_prompt, y_sample)
```

```python
import sys
import numpy as np
from contextlib import ExitStack
import concourse.bass as bass
import concourse.mybir as mybir
from concourse.bass_utils import run_bass_kernel_spmd

F32 = mybir.dt.float32
BF16 = mybir.dt.bfloat16
AF = mybir.ActivationFunctionType
ALU = mybir.AluOpType

D = 1024
NH = 8
NKV = 2
HD = 128
DFF = 2816
NFC = DFF // 128
EPS = 1e-6
GRID_W = 64
HL, HR = 10, 9


class _Rec:
    def __getattr__(self, name):
        def f(*a, **k):
            self.call = (name, a, k)
            return self
        return f


class Sched:
    CE = ('pe', 'act', 'dve', 'pool')

    def __init__(self, nc, es):
        self.nc = nc
        self.es = es
        self.eng = dict(pe=nc.tensor, act=nc.scalar, dve=nc.vector, pool=nc.gpsimd, sp=nc.sync)
        self.ops = []
        self.tags = {}
        self.sem = {e: es.enter_context(nc.semaphore("sem_" + e)) for e in self.CE}
        self.slot_sem = {}
        self.cnt = {e: 0 for e in self.CE}
        self.slot_cnt = {}
        self.waited = {e: {} for e in self.eng}
        self.pend = {e: {} for e in self.eng}
        self.stats = dict(n_ops=0, n_wait=0)

    def op(self, eng, fn, r=(), w=(), dma=None):
        rec = _Rec()
        fn(rec)
        name, a, k = rec.call
        self.tags.setdefault(eng, []).append(sys._getframe(1).f_lineno)
        self.ops.append((eng, (lambda E, name=name, a=a, k=k: getattr(E, name)(*a, **k)), tuple(r), tuple(w), dma))

    def _wait(self, eng, key, val):
        if val <= 0 or self.waited[eng].get(key, 0) >= val:
            return
        self.waited[eng][key] = val
        s = self.slot_sem[key[1]] if key[0] == 's' else self.sem[key[1]]
        self.eng[eng].wait_ge(s, val)
        self.stats['n_wait'] += 1

    def flush(self):
        nc = self.nc
        ops = self.ops
        self.ops = []
        n = len(ops)
        lastw = {}
        readers = {}
        deps = [None] * n
        last_on = {}
        for i, (eng, fn, r, w, dma) in enumerate(ops):
            d = set()
            for k in r:
                j = lastw.get(k)
                if j is not None:
                    d.add(j)
                if isinstance(k, tuple) and k[0] in ('ps', 'psb'):
                    for j in readers.get(k, ()):
                        if ops[j][0] != eng:
                            d.add(j)
            for k in w:
                j = lastw.get(k)
                if j is not None:
                    d.add(j)
                for j in readers.get(k, ()):
                    d.add(j)
            d.discard(i)
            if eng == 'pe':
                d = {j for j in d if ops[j][0] != 'pe'}
            deps[i] = d
            for k in r:
                readers.setdefault(k, []).append(i)
            for k in w:
                lastw[k] = i
                readers[k] = []
            if dma is None:
                last_on[eng] = i
        signal = [False] * n
        for i in range(n):
            for j in deps[i]:
                if ops[j][4] is None:
                    signal[j] = True
        for e, i in last_on.items():
            signal[i] = True
        sigval = [0] * n
        for i, (eng, fn, r, w, dma) in enumerate(ops):
            if dma is not None and dma not in self.slot_sem:
                self.slot_sem[dma] = self.es.enter_context(nc.semaphore("dq%d" % len(self.slot_sem)))
                self.slot_cnt[dma] = 0
            if self.pend[eng]:
                for key, val in self.pend[eng].items():
                    self._wait(eng, key, val)
                self.pend[eng] = {}
            need = {}
            for j in deps[i]:
                if ops[j][4] is not None:
                    key = ('s', ops[j][4])
                    val = self.slot_cnt[ops[j][4]]
                else:
                    key = ('e', ops[j][0])
                    val = sigval[j]
                if need.get(key, 0) < val:
                    need[key] = val
            for key, val in need.items():
                self._wait(eng, key, val)
            ins = fn(self.eng[eng])
            if dma is not None:
                self.slot_cnt[dma] += 16
                ins.then_inc(self.slot_sem[dma], 16)
            elif signal[i]:
                self.cnt[eng] += 1
                ins.then_inc(self.sem[eng], 1)
                sigval[i] = self.cnt[eng]
        self.stats['n_ops'] += n
        for e in self.eng:
            for ce in self.CE:
                if self.pend[e].get(('e', ce), 0) < self.cnt[ce]:
                    self.pend[e][('e', ce)] = self.cnt[ce]
            for s, v in self.slot_cnt.items():
                if self.pend[e].get(('s', s), 0) < v:
                    self.pend[e][('s', s)] = v

    def finish(self):
        self.flush()
        for key, val in self.pend['sp'].items():
            self._wait('sp', key, val)
        self.pend['sp'] = {}


class Cfg:
    def __init__(self, SP=16384, SS=4096, NOP=4096, NOS=2048, nwp=9, nws=5, CK=16):
        self.SP, self.SS, self.NOP, self.NOS = SP, SS, NOP, NOS
        self.CK = CK
        self.seqs = []
        for (S, NO, nw, tag) in ((SP, NOP, nwp, 'p'), (SS, NOS, nws, 's')):
            so = -(-NO // nw)
            wins = []
            o = 0
            while o < NO:
                no = min(so, NO - o)
                wins.append((o, no))
                o += no
            assert max(w[1] for w in wins) + HL + HR <= 512
            self.seqs.append(dict(S=S, NO=NO, NL=NO + HL + HR, wins=wins, tag=tag))


def vec_layout():
    off = {}
    c = 0
    for name, ncol in (('attn_norm', 8), ('pool_norm', 8), ('pool_scale', 8), ('ffn_norm', 16),
                       ('q_gain', 1), ('k_gain', 1), ('conv_w', 2 * 3 * 44), ('conv_b', 2 * 44)):
        off[name] = c
        c += ncol
    return off, c


def build(cfg):
    nc = bass.Bass("TRN2", target_bir_lowering=False)
    es_all = ExitStack()
    S = Sched(nc, es_all)
    voff, NV = vec_layout()

    def din(name, shape, dt=F32):
        return nc.dram_tensor(name, list(shape), dt, kind="ExternalInput").ap()

    def dscr(name, shape, dt=BF16):
        return nc.dram_tensor(name, list(shape), dt, kind="Internal").ap()

    sq = cfg.seqs
    xkv = [din("xkv_" + s['tag'], [s['S'], D]) for s in sq]
    xq = [din("xq_" + s['tag'], [s['NL'], D]) for s in sq]
    cosk = din("cosk", [128, cfg.SP])
    sink = din("sink", [128, cfg.SP])
    cosq = [din("cosq_" + s['tag'], [128, s['NL']]) for s in sq]
    sinq = [din("sinq_" + s['tag'], [128, s['NL']]) for s in sq]
    maskd = [din("mask_" + s['tag'], [128, s['NL']]) for s in sq]
    consts = din("consts", [128, 3 * 128])
    vecs = din("vecs", [128, NV])
    gbc = din("gbc", [128, D])
    w_qkv = din("w_qkv", [D, 1536])
    w_o = din("w_o", [D, D])
    w_pool = din("w_pool", [4 * 256, 256])
    w_up = din("w_up", [2, D, 2 * DFF])
    w_down = din("w_down", [2, DFF, D])
    yout = [nc.dram_tensor("y_" + s['tag'], [s['NO'], D], F32, kind="ExternalOutput").ap() for s in sq]

    wq_s = dscr("wq_s", [8, 128, 1024])
    wk_s = dscr("wk_s", [2, 128, 1024])
    wv_s = dscr("wv_s", [128, 2048])
    wo_s = dscr("wo_s", [8, 128, 1024])
    wup_s = dscr("wup_s", [2, NFC, 128, 2, 1024])
    wd_s = dscr("wd_s", [2, 8, 128, NFC * 128])
    wp_s = dscr("wp_s", [128, 2048])
    kT_s = [dscr("kT_" + s['tag'], [2, 128, s['S']]) for s in sq]
    v_s = [dscr("v_" + s['tag'], [2, 128, s['S'] // 128, 128]) for s in sq]

    def sb(es, name, shape, dt=F32):
        return es.enter_context(nc.sbuf_tensor(name, list(shape), dt))

    cst = sb(es_all, "cst", [128, 384])
    vec = sb(es_all, "vec", [128, NV])
    onesb = sb(es_all, "onesb", [128, 128], BF16)
    identb = sb(es_all, "identb", [128, 128], BF16)
    ident = cst[:, 0:128]
    perm = cst[:, 128:256]
    ones = cst[:, 256:384]
    S.op('sp', lambda e: e.dma_start(out=cst[:], in_=consts[:, :]), w=['cst'], dma='cst')
    S.op('sp', lambda e: e.dma_start(out=vec[:], in_=vecs[:, :]), w=['vec'], dma='vec')
    S.op('dve', lambda e: e.tensor_copy(out=onesb[:], in_=ones), r=['cst'], w=['onesb'])
    S.op('dve', lambda e: e.tensor_copy(out=identb[:], in_=ident), r=['cst'], w=['identb'])

    for l_ in range(2):
        for k_ in range(3):
            c0_ = voff['conv_w'] + l_ * 132 + k_ * 44 + NFC
            S.op('dve', lambda e, c0_=c0_: e.tensor_scalar(out=vec[:, c0_:c0_ + NFC], in0=vec[:, c0_:c0_ + NFC], scalar1=0.5,
                                                           scalar2=None, op0=ALU.mult), r=['vec'], w=['vec'])
        c0_ = voff['conv_b'] + l_ * 44 + NFC
        S.op('dve', lambda e, c0_=c0_: e.tensor_scalar(out=vec[:, c0_:c0_ + NFC], in0=vec[:, c0_:c0_ + NFC], scalar1=0.5,
                                                       scalar2=None, op0=ALU.mult), r=['vec'], w=['vec'])

    def vcol(name, i=0):
        c = voff[name] + i
        return vec[:, c:c + 1]

    PS = {}

    def bank(b):
        return PS['t'][:, b, :]

    def bk(b):
        return ('ps', b)

    es_w = ExitStack()
    STG = 4096
    NST = 3
    stf = [sb(es_w, "stf%d" % i, [128, STG]) for i in range(NST)]
    stb = [sb(es_w, "stb%d" % i, [128, STG], BF16) for i in range(NST)]
    wstate = dict(i=0)
    cast_engs = ['pool', 'pool', 'pool']

    def prep(src, dst, nrc, cw, blocked, dkey=None):
        i = wstate['i']
        wstate['i'] += 1
        f, b = stf[i % NST], stb[i % NST]
        kf, kb = ('stf', i % NST), ('stb', i % NST)
        nel = nrc * cw
        fv = f[:, 0:nel].rearrange("p (c n) -> p c n", c=nrc)
        S.op('sp', lambda e: e.dma_start(out=fv, in_=src), w=[kf], dma=kf)
        ce = cast_engs[i % 3]
        if blocked:
            nb = cw // 128
            iv = f[:, 0:nel].rearrange("p (c b j) -> p b c j", c=nrc, b=nb)
            ov = b[:, 0:nel].rearrange("p (b c j) -> p b c j", c=nrc, b=nb)
            dv = b[:, 0:nel].rearrange("p (b x) -> p b x", b=nb)
        else:
            iv = f[:, 0:nel]
            ov = b[:, 0:nel]
            dv = b[:, 0:nel]
        if ce == 'act':
            if blocked:
                for bb in range(nb):
                    S.op('act', lambda e, bb=bb: e.activation(out=ov[:, bb], in_=iv[:, bb], func=AF.Copy),
                         r=[kf], w=[kb])
            else:
                S.op('act', lambda e: e.activation(out=ov, in_=iv, func=AF.Copy), r=[kf], w=[kb])
        else:
            if blocked:
                for bb in range(nb):
                    S.op(ce, lambda e, bb=bb: e.tensor_copy(out=ov[:, bb], in_=iv[:, bb]), r=[kf], w=[kb])
            else:
                S.op(ce, lambda e: e.tensor_copy(out=ov, in_=iv), r=[kf], w=[kb])
        S.op('sp', lambda e: e.dma_start(out=dst, in_=dv), r=[kb], w=([dkey] if dkey else []), dma=('wst', i % NST))

    def rows(w2d, nrc):
        return w2d.rearrange("(c p) n -> p c n", p=128)

    prep_steps = []

    def P(*args, **kw):
        prep_steps.append(lambda: prep(*args, **kw))
    qv = rows(w_qkv, 8)
    prep(qv[:, :, 1024:1280], wk_s[:, :, :].rearrange("b p x -> p b x"), 8, 256, True, dkey='wk_s')
    prep(qv[:, :, 1280:1536], wv_s[:, :], 8, 256, False, dkey='wv_s')
    for sl in range(2):
        P(qv[:, :, sl * 512:(sl + 1) * 512], wq_s[sl * 4:(sl + 1) * 4].rearrange("b p x -> p b x"), 8, 512, True)
    ov_ = rows(w_o, 8)
    for sl in range(2):
        P(ov_[:, :, sl * 512:(sl + 1) * 512], wo_s[sl * 4:(sl + 1) * 4].rearrange("b p x -> p b x"), 8, 512, True)
    P(rows(w_pool, 8), wp_s[:, :], 8, 256, False)
    for l in range(2):
        uv = rows(w_up[l], 8)
        for half in range(2):
            for b0 in range(0, NFC, 4):
                nb = min(4, NFC - b0)
                c0 = half * DFF + b0 * 128
                P(uv[:, :, c0:c0 + nb * 128],
                  wup_s[l, b0:b0 + nb, :, half, :].rearrange("b p x -> p b x"), 8, nb * 128, True)
        dv_ = rows(w_down[l], NFC)
        for m in range(8):
            P(dv_[:, :, m * 128:(m + 1) * 128], wd_s[l, m:m + 1].rearrange("b p x -> p b x"), NFC, 128, True)

    def rope_head(T, src_ps_bank, b_ss, b_pm, gain_col, cos_t, sin_t, N, out_bf, out_keys, post_scale, tagk, stages=(0, 1, 2), rope_key=None):
        kg, sqt, t1, t2, rs = T['kg'], T['sq'], T['t1'], T['t2'], T['rs']
        src = bank(src_ps_bank)[:, 0:N]
        rk_ = rope_key if rope_key is not None else tagk + 'rope'
        if 0 in stages:
            S.op('act', lambda e: e.activation(out=kg[:, 0:N], in_=src, func=AF.Copy, scale=gain_col),
                 r=[bk(src_ps_bank), 'vec'], w=[tagk + 'kg'])
            S.op('act', lambda e: e.activation(out=sqt[:, 0:N], in_=src, func=AF.Square),
                 r=[bk(src_ps_bank)], w=[tagk + 'sq'])
        if 1 in stages:
            S.op('pe', lambda e: e.matmul(bank(b_ss)[:, 0:N], lhsT=ones, rhs=sqt[:, 0:N], start=True, stop=True),
                 r=['cst', tagk + 'sq'], w=[bk(b_ss)])
            S.op('dve', lambda e: e.tensor_scalar(out=rs[:, 0:N], in0=bank(b_ss)[:, 0:N], scalar1=1.0 / HD, scalar2=EPS,
                                                  op0=ALU.mult, op1=ALU.add), r=[bk(b_ss)], w=[tagk + 'rs'])
            S.op('act', lambda e: e.activation(out=rs[:, 0:N], in_=rs[:, 0:N], func=AF.Sqrt),
                 r=[tagk + 'rs'], w=[tagk + 'rs'])
            S.op('dve', lambda e: e.reciprocal(out=rs[:, 0:N], in_=rs[:, 0:N]), r=[tagk + 'rs'], w=[tagk + 'rs'])
        if 2 in stages:
            S.op('pe', lambda e: e.matmul(bank(b_pm)[:, 0:N], lhsT=perm, rhs=kg[:, 0:N], start=True, stop=True),
                 r=['cst', tagk + 'kg'], w=[bk(b_pm)])
            S.op('dve', lambda e: e.tensor_tensor(out=t2[:, 0:N], in0=bank(b_pm)[:, 0:N], in1=sin_t, op=ALU.mult),
                 r=[bk(b_pm), rk_], w=[tagk + 't2'])
            S.op('pool', lambda e: e.tensor_tensor(out=t1[:, 0:N], in0=kg[:, 0:N], in1=cos_t, op=ALU.mult),
                 r=[tagk + 'kg', rk_], w=[tagk + 't1'])
            S.op('pool', lambda e: e.tensor_tensor(out=t1[:, 0:N], in0=t1[:, 0:N], in1=t2[:, 0:N], op=ALU.add),
                 r=[tagk + 't1', tagk + 't2'], w=[tagk + 't1'])
            S.op('dve', lambda e: e.scalar_tensor_tensor(out=out_bf, in0=t1[:, 0:N], scalar=post_scale, in1=rs[:, 0:N],
                                                         op0=ALU.mult, op1=ALU.mult),
                 r=[tagk + 't1', tagk + 'rs'], w=out_keys)

    es_a = es_w
    PS['t'] = es_a.enter_context(nc.psum_tensor("psA", [128, 6, 512], F32))
    psb = es_a.enter_context(nc.psum_tensor("psb", [128, 2, 1024], BF16))
    xa = [sb(es_a, "xa%d" % i, [128, 4, D]) for i in range(2)]
    xn = [sb(es_a, "xn%d" % i, [128, 4, D], BF16) for i in range(2)]
    hTa = [sb(es_a, "hTa%d" % i, [128, 8, 512], BF16) for i in range(2)]
    wk_t = sb(es_a, "wk_t", [128, 2, 8, 128], BF16)
    wv_t = sb(es_a, "wv_t", [128, 8, 256], BF16)
    gbc_t = sb(es_a, "gbc_t", [128, D])
    ropeA = [sb(es_a, "ropeA%d" % i, [128, 2, 512]) for i in range(2)]
    TA2 = [{k: sb(es_a, "TA%d_" % g_ + k, [128, 512]) for k in ('kg', 'sq', 't1', 't2', 'rs')} for g_ in range(2)]
    ssa = [sb(es_a, "ssa%d" % i, [128, 8]) for i in range(2)]
    kout = [sb(es_a, "kout%d" % i, [128, 512], BF16) for i in range(4)]
    vout = [sb(es_a, "vout%d" % i, [128, 4, 256], BF16) for i in range(2)]

    S.op('sp', lambda e: e.dma_start(out=wk_t[:].rearrange("p a c j -> p a (c j)"),
                                     in_=wk_s[:, :, :].rearrange("a p x -> p a x")), r=['wk_s'], w=['wk_t'], dma='wk_t')
    S.op('sp', lambda e: e.dma_start(out=wv_t[:].rearrange("p c j -> p (c j)"), in_=wv_s[:, :]),
         r=['wv_s'], w=['wv_t'], dma='wv_t')
    S.op('sp', lambda e: e.dma_start(out=gbc_t[:], in_=gbc[:, :]), w=['gbc_t'], dma='gbc_t')

    tiles = [(si, ti) for si, s in enumerate(sq) for ti in range(s['S'] // 512)]

    def frontA(t):
        si, ti = tiles[t]
        t0 = ti * 512
        pb = t % 2
        xat, xnt, hTt, rp, sst = xa[pb], xn[pb], hTa[pb], ropeA[pb], ssa[pb]
        kx = ('xa', pb)
        S.op('sp', lambda e: e.dma_start(out=xat[:], in_=xkv[si][t0:t0 + 512, :].rearrange("(c p) d -> p c d", p=128)),
             w=[kx], dma=kx)
        kr = ('ropeA', pb)
        S.op('sp', lambda e: e.dma_start(out=rp[:, 0, :], in_=cosk[:, t0:t0 + 512]), w=[kr], dma=kr)
        S.op('sp', lambda e: e.dma_start(out=rp[:, 1, :], in_=sink[:, t0:t0 + 512]), w=[kr], dma=kr)
        for c in range(4):
            S.op('act', lambda e, c=c: e.activation(out=xnt[:, c, :], in_=xat[:, c, :], func=AF.Square,
                                                    accum_out=sst[:, c:c + 1]),
                 r=[kx], w=[('ssa', pb, c), ('xn', pb, c)])
        S.op('dve', lambda e: e.tensor_scalar(out=sst[:, 4:8], in0=sst[:, 0:4], scalar1=1.0 / D, scalar2=EPS,
                                              op0=ALU.mult, op1=ALU.add),
             r=[('ssa', pb, c) for c in range(4)], w=[('ssa2', pb)])
        S.op('act', lambda e: e.activation(out=sst[:, 4:8], in_=sst[:, 4:8], func=AF.Sqrt), r=[('ssa2', pb)], w=[('ssa2', pb)])
        S.op('dve', lambda e: e.reciprocal(out=sst[:, 4:8], in_=sst[:, 4:8]), r=[('ssa2', pb)], w=[('ssa2', pb)])
        for c in range(4):
            S.op('dve', lambda e, c=c: e.scalar_tensor_tensor(
                out=xnt[:, c, :], in0=xat[:, c, :], scalar=sst[:, 4 + c:5 + c], in1=gbc_t[:],
                op0=ALU.mult, op1=ALU.mult), r=[kx, ('ssa2', pb), 'gbc_t'], w=[('xn', pb, c)])
        for f in range(8):
            for c in range(4):
                S.op('pe', lambda e, f=f, c=c: e.transpose(psb[:, f % 2, c * 128:(c + 1) * 128],
                                                           xnt[:, c, f * 128:(f + 1) * 128], identb[:]),
                     r=[('xn', pb, c), 'identb'], w=[('psb', f % 2)])
            if f % 2 == 0:
                S.op('act', lambda e, f=f: e.activation(out=hTt[:, f, :], in_=psb[:, f % 2, 0:512], func=AF.Copy),
                     r=[('psb', f % 2)], w=[('hTa', pb, f)])
            else:
                S.op('dve', lambda e, f=f: e.tensor_copy(out=hTt[:, f, :], in_=psb[:, f % 2, 0:512]),
                     r=[('psb', f % 2)], w=[('hTa', pb, f)])

    def backA(t):
        si, ti = tiles[t]
        t0 = ti * 512
        pb = t % 2
        hTt, rp = hTa[pb], ropeA[pb]
        vo = vout[pb]
        vok = ('vout', pb)

        def vpart(cs):
            for c in cs:
                pbv = 2 + c % 2
                for f in range(8):
                    S.op('pe', lambda e, c=c, f=f, pbv=pbv: e.matmul(bank(pbv)[:, 0:256], lhsT=hTt[:, f, c * 128:(c + 1) * 128],
                                                                     rhs=wv_t[:, f, :], start=(f == 0), stop=(f == 7)),
                         r=['wv_t', ('hTa', pb, f)], w=[bk(pbv)])
                if c % 2 == 0:
                    S.op('act', lambda e, c=c, pbv=pbv: e.activation(out=vo[:, c, :], in_=bank(pbv)[:, 0:256], func=AF.Copy),
                         r=[bk(pbv)], w=[vok])
                else:
                    S.op('dve', lambda e, c=c, pbv=pbv: e.tensor_copy(out=vo[:, c, :], in_=bank(pbv)[:, 0:256]),
                         r=[bk(pbv)], w=[vok])

        def rope(g, stages):
            ko = kout[(t * 2 + g) % 4]
            kok = ('kout', (t * 2 + g) % 4)
            rope_head(TA2[g], g, 4 + g, 4 + g, vcol('k_gain'), rp[:, 0, :], rp[:, 1, :], 512, ko[:], [kok], 1.0, 'A%d' % g,
                      stages=stages, rope_key=('ropeA', pb))
            if 2 in stages:
                S.op('sp', lambda e: e.dma_start(out=kT_s[si][g, :, t0:t0 + 512], in_=ko[:]), r=[kok], dma=kok)

        for g in range(NKV):
            for f in range(8):
                S.op('pe', lambda e, g=g, f=f: e.matmul(bank(g)[:, :], lhsT=wk_t[:, g, f, :], rhs=hTt[:, f, :],
                                                        start=(f == 0), stop=(f == 7)),
                     r=['wk_t', ('hTa', pb, f)], w=[bk(g)])
        for g in range(NKV):
            rope(g, (0,))
        vpart((0, 1))
        for g in range(NKV):
            rope(g, (1,))
        vpart((2, 3))
        for g in range(NKV):
            rope(g, (2,))
        for g in range(NKV):
            S.op('sp', lambda e, g=g: e.dma_start(out=v_s[si][g, :, ti * 4:(ti + 1) * 4, :], in_=vo[:, :, g * 128:(g + 1) * 128]),
                 r=[vok], dma=vok)

    frontA(0)
    per_tile = -(-len(prep_steps) // max(1, len(tiles) - 4))
    for t in range(len(tiles)):
        if t + 1 < len(tiles):
            frontA(t + 1)
        for _ in range(per_tile):
            if prep_steps:
                prep_steps.pop(0)()
        backA(t)
    while prep_steps:
        prep_steps.pop(0)()
    S.flush()
    es_a.close()

    es_b = ExitStack()
    PS['t'] = es_b.enter_context(nc.psum_tensor("psB", [128, 8, 512], F32))
    xT = sb(es_b, "xT", [128, 8, 512])
    hT = sb(es_b, "hT", [128, 8, 512], BF16)
    hF = sb(es_b, "hF", [128, 512])
    oT = sb(es_b, "oT", [128, 8, 512], BF16)
    aT = sb(es_b, "aT", [128, NFC, 512], BF16)
    plT = oT
    xw = [sb(es_b, "xw%d" % i, [128, D]) for i in range(2)]
    qT = [sb(es_b, "qT%d" % i, [128, 512], BF16) for i in range(2)]
    NPT = 5
    pT = [sb(es_b, "pT%d" % i, [128, 1024], BF16) for i in range(NPT)]
    RK = 3
    kring = [sb(es_b, "kr%d" % i, [128, cfg.CK * 128], BF16) for i in range(RK)]
    vring = [sb(es_b, "vr%d" % i, [128, cfg.CK, 128], BF16) for i in range(RK)]
    TB = {k: sb(es_b, "TB_" + k, [128, 512]) for k in ('kg', 'sq', 't1', 't2', 'rs')}
    ropeB = sb(es_b, "ropeB", [128, 2, 512])
    maskt = sb(es_b, "maskt", [128, 512])
    icnt = sb(es_b, "icnt", [128, 4, 512])
    rstd = sb(es_b, "rstd", [128, 512])
    osb = sb(es_b, "osb", [128, 512])
    lsb = sb(es_b, "lsb", [128, 512])
    tmpr = [sb(es_b, "tmpr%d" % i, [128, 512]) for i in range(2)]
    sqn = tmpr
    ffA = [sb(es_b, "ffA%d" % i, [128, 2, 512]) for i in range(4)]
    ffAf = [t_[:].rearrange("p a n -> p (a n)") for t_ in ffA]
    cg = [ffA[0][:, p_, :] for p_ in range(2)]
    cv = [ffA[1][:, p_, :] for p_ in range(2)]
    th = [ffA[2][:, p_, :] for p_ in range(2)]
    a0 = [ffA[3][:, p_, :] for p_ in range(2)]
    pa = [TB['kg'], TB['sq'], TB['t1']]
    wq_t = [sb(es_b, "wq_t%d" % i, [128, 8, 128], BF16) for i in range(2)]
    wo_t = [sb(es_b, "wo_t%d" % i, [128, 8, 128], BF16) for i in range(2)]
    wup_t = [sb(es_b, "wup_t%d" % i, [128, 2, 8, 128], BF16) for i in range(4)]
    wd_t = [sb(es_b, "wd_t%d" % i, [128, NFC, 128], BF16) for i in range(2)]
    wp_t = sb(es_b, "wp_t", [128, 8, 256], BF16)
    S.op('sp', lambda e: e.dma_start(out=wp_t[:].rearrange("p c j -> p (c j)"), in_=wp_s[:, :]), w=['wp_t'], dma='wp_t')

    cnt = dict(wq=0, wo=0, wup=0, wd=0, xw=0, kv=0, q=0, p=0, sqn=0, tmpr=0, ff=0)

    def fm_norm(N, gname, gi, out_fn, out_keys_fn, chunks=range(8)):
        for c in range(8):
            i = cnt['sqn'] % 2
            cnt['sqn'] += 1
            S.op('act', lambda e, c=c, i=i: e.activation(out=sqn[i][:, 0:N], in_=xT[:, c, 0:N], func=AF.Square),
                 r=[('xT', c)], w=[('tmpr', i)])
            S.op('pe', lambda e, c=c, i=i: e.matmul(bank(6)[:, 0:N], lhsT=ones, rhs=sqn[i][:, 0:N],
                                                    start=(c == 0), stop=(c == 7)), r=['cst', ('tmpr', i)], w=[bk(6)])
        S.op('dve', lambda e: e.tensor_scalar(out=rstd[:, 0:N], in0=bank(6)[:, 0:N], scalar1=1.0 / D, scalar2=EPS,
                                              op0=ALU.mult, op1=ALU.add), r=[bk(6)], w=['rstd'])
        S.op('act', lambda e: e.activation(out=rstd[:, 0:N], in_=rstd[:, 0:N], func=AF.Sqrt), r=['rstd'], w=['rstd'])
        S.op('dve', lambda e: e.reciprocal(out=rstd[:, 0:N], in_=rstd[:, 0:N]), r=['rstd'], w=['rstd'])

    def norm_chunk(N, c, gname, gi, out_ap, out_keys):
        S.op('dve', lambda e: e.scalar_tensor_tensor(out=out_ap, in0=xT[:, c, 0:N], scalar=vcol(gname, gi * 8 + c),
                                                     in1=rstd[:, 0:N], op0=ALU.mult, op1=ALU.mult),
             r=[('xT', c), 'rstd', 'vec'], w=out_keys)

    def resid_add(N, m, psbank, scale_col):
        i = cnt['tmpr'] % 2
        cnt['tmpr'] += 1
        t = tmpr[i]
        S.op('dve', lambda e: e.scalar_tensor_tensor(out=t[:, 0:N], in0=bank(psbank)[:, 0:N],
                                                     scalar=(1.0 if scale_col is None else scale_col),
                                                     in1=maskt[:, 0:N], op0=ALU.mult, op1=ALU.mult),
             r=[bk(psbank), 'maskt', 'vec'], w=[('tmpr', i)])
        S.op('pool', lambda e: e.tensor_tensor(out=xT[:, m, 0:N], in0=xT[:, m, 0:N], in1=t[:, 0:N], op=ALU.add),
             r=[('xT', m), ('tmpr', i)], w=[('xT', m)])

    def ffn(N, l):
        fm_norm(N, 'ffn_norm', l, None, None)
        for c in range(8):
            norm_chunk(N, c, 'ffn_norm', l, hT[:, c, 0:N], [('hT', c)])

        def load_wup(j):
            i = cnt['wup'] % 4
            cnt['wup'] += 1
            S.op('sp', lambda e: e.dma_start(out=wup_t[i][:].rearrange("p a c j -> p (a c j)"),
                                             in_=wup_s[l, j].rearrange("p a x -> p (a x)")),
                 w=[('wup', i)], dma=('wup', i))
            return i
        slots = {}
        for j in range(min(3, NFC)):
            slots[j] = load_wup(j)
        cwb = voff['conv_w'] + l * 3 * 44
        cbb = voff['conv_b'] + l * 44
        for j in range(NFC):
            if j + 3 < NFC:
                slots[j + 3] = load_wup(j + 3)
            wi = slots[j]
            par = cnt['ff'] % 2
            cnt['ff'] += 1
            bg, bv = (2, 3) if par == 0 else (4, 0)
            for half, bnk in ((0, bg), (1, bv)):
                for c in range(8):
                    S.op('pe', lambda e, half=half, bnk=bnk, c=c, wi=wi: e.matmul(
                        bank(bnk)[:, 0:N], lhsT=wup_t[wi][:, half, c, :], rhs=hT[:, c, 0:N],
                        start=(c == 0), stop=(c == 7)), r=[('wup', wi), ('hT', c)], w=[bk(bnk)])

            def taps(half):
                ch = half * NFC + j
                return [vec[:, cwb + k_ * 44 + ch: cwb + k_ * 44 + ch + 1] for k_ in range(3)] + \
                       [vec[:, cbb + ch: cbb + ch + 1]]
            kg_, kv_, kt_, ka_ = ('ffA', 0, par), ('ffA', 1, par), ('ffA', 2, par), ('ffA', 3, par)
            cgt, cvt, tht, a0t = cg[par], cv[par], th[par], a0[par]
            w0, w1, w2, bb = taps(0)
            S.op('act', lambda e: e.activation(out=cgt[:, 0:N], in_=bank(bg)[:, 0:N], func=AF.Identity, scale=w1, bias=bb),
                 r=[bk(bg), 'vec'], w=[kg_])
            S.op('act', lambda e: e.activation(out=a0t[:, 1:N], in_=bank(bg)[:, 0:N - 1], func=AF.Copy, scale=w0),
                 r=[bk(bg), 'vec'], w=[ka_])
            S.op('dve', lambda e: e.scalar_tensor_tensor(out=cgt[:, 0:N - 1], in0=bank(bg)[:, 1:N], scalar=w2,
                                                         in1=cgt[:, 0:N - 1], op0=ALU.mult, op1=ALU.add),
                 r=[bk(bg), kg_, 'vec'], w=[kg_])
            S.op('pool', lambda e: e.tensor_tensor(out=cgt[:, 1:N], in0=cgt[:, 1:N], in1=a0t[:, 1:N], op=ALU.add),
                 r=[kg_, ka_], w=[kg_])
            w0, w1, w2, bb = taps(1)
            S.op('act', lambda e: e.activation(out=cvt[:, 0:N], in_=bank(bv)[:, 0:N], func=AF.Identity, scale=w1, bias=bb),
                 r=[bk(bv), 'vec'], w=[kv_])
            S.op('dve', lambda e: e.scalar_tensor_tensor(out=cvt[:, 1:N], in0=bank(bv)[:, 0:N - 1], scalar=w0,
                                                         in1=cvt[:, 1:N], op0=ALU.mult, op1=ALU.add),
                 r=[bk(bv), kv_, 'vec'], w=[kv_])
            S.op('dve', lambda e: e.scalar_tensor_tensor(out=cvt[:, 0:N - 1], in0=bank(bv)[:, 1:N], scalar=w2,
                                                         in1=cvt[:, 0:N - 1], op0=ALU.mult, op1=ALU.add),
                 r=[bk(bv), kv_, 'vec'], w=[kv_])
            S.op('act', lambda e: e.activation(out=tht[:, 0:N], in_=cgt[:, 0:N], func=AF.Tanh, scale=0.5),
                 r=[kg_], w=[kt_])
            S.op('pool', lambda e: e.tensor_tensor(out=cvt[:, 0:N], in0=cvt[:, 0:N], in1=cgt[:, 0:N], op=ALU.mult),
                 r=[kg_, kv_], w=[kv_])
            S.op('dve', lambda e, j=j: e.scalar_tensor_tensor(out=aT[:, j, 0:N], in0=tht[:, 0:N], scalar=1.0, in1=cvt[:, 0:N],
                                                              op0=ALU.add, op1=ALU.mult),
                 r=[kt_, kv_], w=[('aT', j)])
        def load_wd(m):
            i = cnt['wd'] % 2
            cnt['wd'] += 1
            S.op('sp', lambda e: e.dma_start(out=wd_t[i][:].rearrange("p j x -> p (j x)"), in_=wd_s[l, m]),
                 w=[('wd', i)], dma=('wd', i))
            return i
        dsl = {0: load_wd(0)}
        for m in range(8):
            if m + 1 < 8:
                dsl[m + 1] = load_wd(m + 1)
            wi = dsl[m]
            bnk = 5 if m % 2 == 0 else 1
            for j in range(NFC):
                S.op('pe', lambda e, j=j, wi=wi, bnk=bnk: e.matmul(bank(bnk)[:, 0:N], lhsT=wd_t[wi][:, j, :], rhs=aT[:, j, 0:N],
                                                                   start=(j == 0), stop=(j == NFC - 1)),
                     r=[('wd', wi), ('aT', j)], w=[bk(bnk)])
            resid_add(N, m, bnk, None)

    for si, s in enumerate(sq):
        nkt = s['S'] // 128
        CK = min(cfg.CK, nkt)
        nck = nkt // CK
        jobs = [(wi_, h, ci) for wi_ in range(len(s['wins'])) for h in range(NH) for ci in range(nck)]
        jslot = {}

        def load_kv(jn, si=si, CK=CK, jobs=jobs, jslot=jslot):
            if jn >= len(jobs) or jn in jslot:
                return
            (_, h, ci) = jobs[jn]
            g = h // (NH // NKV)
            i = cnt['kv'] % RK
            cnt['kv'] += 1
            jslot[jn] = i
            S.op('sp', lambda e: e.dma_start(out=kring[i][:, 0:CK * 128], in_=kT_s[si][g, :, ci * CK * 128:(ci + 1) * CK * 128]),
                 w=[('kv', i)], dma=('kv', i))
            S.op('sp', lambda e: e.dma_start(out=vring[i][:, 0:CK, :], in_=v_s[si][g, :, ci * CK:(ci + 1) * CK, :]),
                 w=[('kv', i)], dma=('kv', i))

        for wi_, (o0, no) in enumerate(s['wins']):
            N = no + HL + HR
            r0 = o0
            ntc = -(-N // 128)
            S.op('sp', lambda e, r0=r0, N=N, si=si: e.dma_start(out=maskt[:, 0:N], in_=maskd[si][:, r0:r0 + N]),
                 w=['maskt'], dma='maskt')
            S.op('sp', lambda e, r0=r0, N=N, si=si: e.dma_start(out=ropeB[:, 0, 0:N], in_=cosq[si][:, r0:r0 + N]),
                 w=['Brope'], dma='ropeB')
            S.op('sp', lambda e, r0=r0, N=N, si=si: e.dma_start(out=ropeB[:, 1, 0:N], in_=sinq[si][:, r0:r0 + N]),
                 w=['Brope'], dma='ropeB')
            for tc in range(ntc):
                nt = min(128, N - tc * 128)
                i = cnt['xw'] % 2
                cnt['xw'] += 1
                S.op('sp', lambda e, i=i, tc=tc, nt=nt, r0=r0, si=si: e.dma_start(
                    out=xw[i][0:nt, :], in_=xq[si][r0 + tc * 128: r0 + tc * 128 + nt, :]), w=[('xw', i)], dma=('xw', i))
                for f in range(8):
                    bnk = f % 4
                    S.op('pe', lambda e, i=i, f=f, tc=tc, nt=nt, bnk=bnk: e.transpose(
                        bank(bnk)[:, tc * 128: tc * 128 + nt], xw[i][0:nt, f * 128:(f + 1) * 128], ident[0:nt, 0:nt]),
                        r=[('xw', i), 'cst'], w=[bk(bnk)])
                    if f % 2 == 0:
                        S.op('act', lambda e, f=f, tc=tc, nt=nt, bnk=bnk: e.activation(
                            out=xT[:, f, tc * 128: tc * 128 + nt], in_=bank(bnk)[:, tc * 128: tc * 128 + nt], func=AF.Copy),
                            r=[bk(bnk)], w=[('xT', f)])
                    else:
                        S.op('dve', lambda e, f=f, tc=tc, nt=nt, bnk=bnk: e.tensor_copy(
                            out=xT[:, f, tc * 128: tc * 128 + nt], in_=bank(bnk)[:, tc * 128: tc * 128 + nt]),
                            r=[bk(bnk)], w=[('xT', f)])

            def window_sums(cur, ck, g, out_ap, out_keys, eng, temps):
                steps = [(1, 0), (1, 1), (2, 2), (4, 4)]
                lo, hi = 0, N
                for lvl in range(g + 1):
                    sl, sr = steps[lvl]
                    dst, dk = (out_ap, out_keys) if lvl == g else temps[lvl]
                    nlo, nhi = lo + sl, hi - sr
                    S.op(eng, lambda e, dst=dst, cur=cur, nlo=nlo, nhi=nhi, sl=sl, sr=sr: e.tensor_tensor(
                        out=dst[:, nlo:nhi], in0=cur[:, nlo - sl:nhi - sl], in1=cur[:, nlo + sr:nhi + sr], op=ALU.add),
                        r=ck, w=dk)
                    cur, ck, lo, hi = dst, dk, nlo, nhi

            DVE_T = [(TB['kg'], ['Bkg']), (TB['sq'], ['Bsq']), (TB['t1'], ['Bt1'])]
            POOL_T = [(ffA[0][:, 0, :], [('ffA', 0, 0)]), (ffA[0][:, 1, :], [('ffA', 0, 1)]), (ffA[1][:, 0, :], [('ffA', 1, 0)])]
            for g in range(4):
                S.op('pool', lambda e, g=g: e.tensor_copy(out=icnt[:, g, 0:N], in_=maskt[:, 0:N]), r=['maskt'], w=[('icnt', g)])
                window_sums(maskt, ['maskt'], g, icnt[:, g, :], [('icnt', g)], 'pool', POOL_T)
                S.op('dve', lambda e, g=g: e.tensor_scalar(out=icnt[:, g, 0:N], in0=icnt[:, g, 0:N], scalar1=1.0, scalar2=None,
                                                           op0=ALU.max), r=[('icnt', g)], w=[('icnt', g)])
                S.op('dve', lambda e, g=g: e.reciprocal(out=icnt[:, g, 0:N], in_=icnt[:, g, 0:N]),
                     r=[('icnt', g)], w=[('icnt', g)])
            fm_norm(N, 'attn_norm', 0, None, None)
            for c in range(8):
                norm_chunk(N, c, 'attn_norm', 0, hT[:, c, 0:N], [('hT', c)])

            def load_wq(h):
                i = cnt['wq'] % 2
                cnt['wq'] += 1
                S.op('sp', lambda e: e.dma_start(out=wq_t[i][:].rearrange("p c j -> p (c j)"), in_=wq_s[h]),
                     w=[('wq', i)], dma=('wq', i))
                return i

            def qprep_a(h, wslot):
                for c in range(8):
                    S.op('pe', lambda e, c=c: e.matmul(bank(5)[:, 0:N], lhsT=wq_t[wslot][:, c, :], rhs=hT[:, c, 0:N],
                                                       start=(c == 0), stop=(c == 7)),
                         r=[('wq', wslot), ('hT', c)], w=[bk(5)])

            def qprep_b(h, stages=(0, 1, 2), qi=None):
                if qi is None:
                    qi = cnt['q'] % 2
                    cnt['q'] += 1
                xb_ = 6 + (h % 2)
                rope_head(TB, 5, xb_, xb_, vcol('q_gain'), ropeB[:, 0, 0:N], ropeB[:, 1, 0:N], N, qT[qi][:, 0:N],
                          [('qT', qi)], float(HD) ** -0.5, 'B', stages=stages)
                return qi

            base_job = wi_ * NH * nck
            if wi_ == 0:
                load_kv(0)
                load_kv(1)
            wsl = load_wq(0)
            qprep_a(0, wsl)
            qcur = qprep_b(0)
            npair = nkt // 2
            ROWSUM_PAT = (('pe', 0), ('dve', 1), ('dve', 2))
            ACCK = [[('ffA', k_, 0), ('ffA', k_, 1)] for k_ in range(3)]

            def finish_head(hh):
                lb = 6 + (hh % 2)
                S.op('pe', lambda e: e.matmul(bank(lb)[:, 0:N], lhsT=ones, rhs=lsb[:, 0:N], start=False, stop=True),
                     r=['cst', 'lsb'], w=[bk(lb)])
                S.op('dve', lambda e: e.reciprocal(out=rstd[:, 0:N], in_=bank(lb)[:, 0:N]), r=[bk(lb)], w=['rstd'])
                S.op('pool', lambda e: e.tensor_tensor(out=oT[:, hh, 0:N], in0=osb[:, 0:N], in1=rstd[:, 0:N], op=ALU.mult),
                     r=['osb', 'rstd'], w=[('oT', hh)])

            for h in range(NH):
                if h + 1 < NH:
                    wsl_n = load_wq(h + 1)
                p_of = {}
                accinit = [False, False, False]
                lb_ = 6 + (h % 2)

                def emit_qk(jp, h=h, qcur=qcur):
                    b0 = 1 + 2 * (jp % 2)
                    for t_ in range(2):
                        j = 2 * jp + t_
                        ci, jj = divmod(j, CK)
                        slot = jslot[base_job + h * nck + ci]
                        S.op('pe', lambda e: e.matmul(bank(b0 + t_)[:, 0:N], lhsT=kring[slot][:, jj * 128:(jj + 1) * 128],
                                                      rhs=qT[qcur][:, 0:N], start=True, stop=True),
                             r=[('kv', slot), ('qT', qcur)], w=[bk(b0 + t_)])
                    pi = cnt['p'] % NPT
                    cnt['p'] += 1
                    p_of[jp] = pi
                    S.op('act', lambda e: e.activation(out=pT[pi][:, 0:2 * N].rearrange("p (a n) -> p a n", a=2),
                                                       in_=PS['t'][:, b0:b0 + 2, 0:N], func=AF.Exp),
                         r=[bk(b0), bk(b0 + 1)], w=[('pT', pi)])

                def emit_pv(jp, h=h):
                    pi = p_of[jp]
                    for t_ in range(2):
                        j = 2 * jp + t_
                        ci, jj = divmod(j, CK)
                        slot = jslot[base_job + h * nck + ci]
                        S.op('pe', lambda e: e.matmul(bank(0)[:, 0:N], lhsT=vring[slot][:, jj, :], rhs=pT[pi][:, t_ * N:(t_ + 1) * N],
                                                      start=(j == 0), stop=(j == nkt - 1)),
                             r=[('kv', slot), ('pT', pi)], w=[bk(0)])
                    eng_, k_ = ROWSUM_PAT[jp % len(ROWSUM_PAT)]
                    if eng_ == 'pe':
                        for t_ in range(2):
                            S.op('pe', lambda e: e.matmul(bank(lb_)[:, 0:N], lhsT=onesb[:], rhs=pT[pi][:, t_ * N:(t_ + 1) * N],
                                                          start=(jp == 0 and t_ == 0), stop=False),
                                 r=['onesb', ('pT', pi)], w=[bk(lb_)])
                        return
                    acc = ffAf[k_]
                    if not accinit[k_]:
                        accinit[k_] = True
                        S.op(eng_, lambda e: e.tensor_copy(out=acc[:, 0:2 * N], in_=pT[pi][:, 0:2 * N]),
                             r=[('pT', pi)], w=ACCK[k_])
                    else:
                        S.op(eng_, lambda e: e.tensor_tensor(out=acc[:, 0:2 * N], in0=acc[:, 0:2 * N], in1=pT[pi][:, 0:2 * N],
                                                             op=ALU.add), r=[('pT', pi)] + ACCK[k_], w=ACCK[k_])

                emit_qk(0)
                for jp in range(npair):
                    ci, jj = divmod(2 * jp, CK)
                    if jj == 0:
                        load_kv(base_job + h * nck + ci + 1)
                        load_kv(base_job + h * nck + ci + 2)
                    if jp + 1 < npair:
                        cn = (2 * (jp + 1)) // CK
                        if (base_job + h * nck + cn) not in jslot:
                            load_kv(base_job + h * nck + cn)
                        emit_qk(jp + 1)
                    emit_pv(jp)
                    if jp == 0 and h + 1 < NH:
                        qprep_a(h + 1, wsl_n)
                        qnext = qprep_b(h + 1, stages=(0,))
                    if jp == min(4, npair - 3) and h > 0:
                        finish_head(h - 1)
                    if jp == min(7, npair - 2) and h + 1 < NH:
                        qprep_b(h + 1, stages=(1,), qi=qnext)
                    if jp == min(10, npair - 1) and h + 1 < NH:
                        qprep_b(h + 1, stages=(2,), qi=qnext)
                if h + 1 < NH:
                    if (base_job + (h + 1) * nck) not in jslot:
                        load_kv(base_job + (h + 1) * nck)
                S.op('act', lambda e: e.activation(out=osb[:, 0:N], in_=bank(0)[:, 0:N], func=AF.Copy), r=[bk(0)], w=['osb'])
                used = [k_ for k_ in (1, 2) if accinit[k_]]
                assert used
                u0 = used[0]
                if len(used) == 2:
                    S.op('dve', lambda e: e.tensor_tensor(out=ffAf[1][:, 0:2 * N], in0=ffAf[1][:, 0:2 * N],
                                                          in1=ffAf[2][:, 0:2 * N], op=ALU.add),
                         r=ACCK[1] + ACCK[2], w=ACCK[1])
                S.op('dve', lambda e: e.tensor_tensor(out=lsb[:, 0:N], in0=ffAf[u0][:, 0:N], in1=ffAf[u0][:, N:2 * N], op=ALU.add),
                     r=ACCK[u0], w=['lsb'])
                if h + 1 < NH:
                    qcur = qnext
            finish_head(NH - 1)
            nb_ = (wi_ + 1) * NH * nck
            load_kv(nb_)
            load_kv(nb_ + 1)
            def load_wo(m):
                i = cnt['wo'] % 2
                cnt['wo'] += 1
                S.op('sp', lambda e: e.dma_start(out=wo_t[i][:].rearrange("p c j -> p (c j)"), in_=wo_s[m]),
                     w=[('wo', i)], dma=('wo', i))
                return i
            osl = {0: load_wo(0)}
            for m in range(8):
                if m + 1 < 8:
                    osl[m + 1] = load_wo(m + 1)
                wsl = osl[m]
                bnk = 5 if m % 2 == 0 else 7
                for h in range(NH):
                    S.op('pe', lambda e, h=h, wsl=wsl, bnk=bnk: e.matmul(bank(bnk)[:, 0:N], lhsT=wo_t[wsl][:, h, :], rhs=oT[:, h, 0:N],
                                                                         start=(h == 0), stop=(h == NH - 1)),
                         r=[('wo', wsl), ('oT', h)], w=[bk(bnk)])
                resid_add(N, m, bnk, None)
            ffn(N, 0)
            def shifted_add(eng, out_t, in_t, sh, keys_r, keys_w, first):
                pass
            fm_norm(N, 'pool_norm', 0, None, None)
            for c in (1, 0, 3, 2, 5, 4, 6, 7):
                g = c // 2
                on_pool = c in (1, 3, 5)
                eng_ = 'pool' if on_pool else 'dve'
                hbuf, hk = (ffA[2][:, 0, :], [('ffA', 2, 0)]) if on_pool else (hF, ['hF'])
                res, rk = (ffA[1][:, 1, :], [('ffA', 1, 1)]) if on_pool else (tmpr[0], [('tmpr', 0)])
                norm_chunk(N, c, 'pool_norm', 0, hbuf[:, 0:N], hk)
                window_sums(hbuf, hk, g, res, rk, eng_, POOL_T if on_pool else DVE_T)
                S.op(eng_, lambda e, g=g, res=res: e.tensor_tensor(out=res[:, 0:N], in0=res[:, 0:N], in1=icnt[:, g, 0:N], op=ALU.mult),
                     r=rk + [('icnt', g)], w=rk)
                S.op(eng_, lambda e, c=c, res=res, hbuf=hbuf: e.tensor_tensor(out=plT[:, c, 0:N], in0=res[:, 0:N], in1=hbuf[:, 0:N],
                                                                              op=ALU.subtract), r=rk + hk, w=[('oT', c)])
            for m in range(8):
                g = m // 2
                mo = m % 2
                bnk = 5 if m % 2 == 0 else 7
                for cc in range(2):
                    S.op('pe', lambda e, g=g, mo=mo, cc=cc, bnk=bnk: e.matmul(
                        bank(bnk)[:, 0:N], lhsT=wp_t[:, 2 * g + cc, mo * 128:(mo + 1) * 128], rhs=plT[:, 2 * g + cc, 0:N],
                        start=(cc == 0), stop=(cc == 1)), r=['wp_t', ('oT', 2 * g + cc)], w=[bk(bnk)])
                resid_add(N, m, bnk, vcol('pool_scale', m))
            ffn(N, 1)
            c0 = HL
            while c0 < HL + no:
                nt = min(128, HL + no - c0)
                i = cnt['xw'] % 2
                cnt['xw'] += 1
                for f in range(8):
                    bnk = 2 + (f // 4)
                    S.op('pe', lambda e, f=f, c0=c0, nt=nt, bnk=bnk: e.transpose(
                        bank(bnk)[0:nt, (f % 4) * 128:(f % 4 + 1) * 128], xT[:, f, c0:c0 + nt], ident),
                        r=[('xT', f), 'cst'], w=[bk(bnk)])
                    if f % 4 == 3:
                        hh = f // 4
                        if hh == 0:
                            S.op('act', lambda e, i=i, nt=nt, bnk=bnk, hh=hh: e.activation(
                                out=xw[i][0:nt, hh * 512:(hh + 1) * 512], in_=bank(bnk)[0:nt, :], func=AF.Copy),
                                r=[bk(bnk)], w=[('xw', i)])
                        else:
                            S.op('dve', lambda e, i=i, nt=nt, bnk=bnk, hh=hh: e.tensor_copy(
                                out=xw[i][0:nt, hh * 512:(hh + 1) * 512], in_=bank(bnk)[0:nt, :]),
                                r=[bk(bnk)], w=[('xw', i)])
                orow = o0 + (c0 - HL)
                S.op('sp', lambda e, i=i, nt=nt, orow=orow, si=si: e.dma_start(out=yout[si][orow:orow + nt, :], in_=xw[i][0:nt, :]),
                     r=[('xw', i)], dma=('xw', i))
                c0 += nt

    S.finish()
    es_b.close()
    es_all.close()
    return nc, S


def rope_tables(pos):
    pos = np.asarray(pos, dtype=np.int64)
    row = (pos // GRID_W).astype(np.float32)
    col = (pos % GRID_W).astype(np.float32)
    F = 32
    inv = (np.float32(10000.0) ** (-(np.arange(F, dtype=np.float32) / np.float32(F)))).astype(np.float32)
    ang_r = (row[None, :] * inv[:, None]).astype(np.float32)
    ang_c = (col[None, :] * inv[:, None]).astype(np.float32)
    cos = np.empty((128, len(pos)), np.float32)
    sin = np.empty((128, len(pos)), np.float32)
    for a, ang in ((0, ang_r), (1, ang_c)):
        c = np.cos(ang).astype(np.float32)
        s = np.sin(ang).astype(np.float32)
        cos[a * 64:a * 64 + 32] = c
        cos[a * 64 + 32:a * 64 + 64] = c
        sin[a * 64:a * 64 + 32] = -s
        sin[a * 64 + 32:a * 64 + 64] = s
    return cos, sin


def const_mats():
    ident = np.eye(128, dtype=np.float32)
    perm = np.zeros((128, 128), np.float32)
    for d in range(128):
        h = (d % 64) // 32
        partner = d + 32 if h == 0 else d - 32
        perm[partner, d] = 1.0
    ones = np.ones((128, 128), np.float32)
    return np.ascontiguousarray(np.concatenate([ident, perm, ones], axis=1))


def pack_vecs(inp):
    voff, NV = vec_layout()
    v = np.zeros((128, NV), np.float32)
    def cols(a):
        return np.asarray(a, np.float32).reshape(-1, 128).T
    v[:, voff['attn_norm']:voff['attn_norm'] + 8] = cols(inp['attn_norm'][0])
    v[:, voff['pool_norm']:voff['pool_norm'] + 8] = cols(inp['pool_norm'][0])
    v[:, voff['pool_scale']:voff['pool_scale'] + 8] = cols(inp['pool_scale'][0])
    for l in range(2):
        v[:, voff['ffn_norm'] + 8 * l:voff['ffn_norm'] + 8 * l + 8] = cols(inp['ffn_norm'][l])
        for k in range(3):
            c0 = voff['conv_w'] + l * 132 + k * 44
            v[:, c0:c0 + 44] = cols(inp['conv_w'][l, k])
        c0 = voff['conv_b'] + l * 44
        v[:, c0:c0 + 44] = cols(inp['conv_b'][l])
    v[:, voff['q_gain']] = np.asarray(inp['q_gain'][0], np.float32)
    v[:, voff['k_gain']] = np.asarray(inp['k_gain'][0], np.float32)
    return v


def make_in_maps(cfg, inp, n_cores=8):
    xp = np.asarray(inp['x_prompt'], np.float32)
    xs = np.asarray(inp['x_sample'], np.float32)
    npc = cfg.SP // cfg.NOP
    nsc = cfg.SS // cfg.NOS
    cosk, sink = rope_tables(np.arange(cfg.SP))
    shared = dict(
        cosk=cosk, sink=sink, consts=const_mats(), vecs=pack_vecs(inp),
        gbc=np.ascontiguousarray(np.broadcast_to(np.asarray(inp['attn_norm'][0], np.float32)[None, :], (128, D))),
        w_qkv=np.ascontiguousarray(np.asarray(inp['w_qkv'][0], np.float32)),
        w_o=np.ascontiguousarray(np.asarray(inp['w_o'][0], np.float32)),
        w_pool=np.ascontiguousarray(np.asarray(inp['w_pool'][0], np.float32).reshape(4 * 256, 256)),
        w_up=np.ascontiguousarray(np.asarray(inp['w_up'], np.float32)),
        w_down=np.ascontiguousarray(np.asarray(inp['w_down'], np.float32)),
    )
    maps = []
    for c in range(n_cores):
        m = dict(shared)
        for (tag, x, S_, NO, per) in (('p', xp, cfg.SP, cfg.NOP, npc), ('s', xs, cfg.SS, cfg.NOS, nsc)):
            b, part = divmod(c, per)
            q0 = part * NO
            NL = NO + HL + HR
            pos = np.arange(q0 - HL, q0 - HL + NL)
            valid = (pos >= 0) & (pos < S_)
            xl = np.zeros((NL, D), np.float32)
            xl[valid] = x[b, pos[valid]]
            cq, sq_ = rope_tables(np.where(valid, pos, 0))
            m['xkv_' + tag] = np.ascontiguousarray(x[b])
            m['xq_' + tag] = xl
            m['cosq_' + tag] = cq
            m['sinq_' + tag] = sq_
            m['mask_' + tag] = np.ascontiguousarray(np.broadcast_to(valid.astype(np.float32)[None, :], (128, NL)))
        maps.append(m)
    return maps


_CACHE = {}


def run_cfg(cfg, inp, n_cores=8, trace=False):
    key = (cfg.SP, cfg.SS, cfg.NOP, cfg.NOS)
    if key not in _CACHE:
        _CACHE[key] = build(cfg)
    nc, S = _CACHE[key]
    maps = make_in_maps(cfg, inp, n_cores)
    res = run_bass_kernel_spmd(nc, maps, core_ids=list(range(n_cores)), trace=trace)
    B = inp['x_prompt'].shape[0]
    Bs = inp['x_sample'].shape[0]
    yp = np.zeros((B, cfg.SP, D), np.float32)
    ys = np.zeros((Bs, cfg.SS, D), np.float32)
    npc = cfg.SP // cfg.NOP
    nsc = cfg.SS // cfg.NOS
    for c in range(n_cores):
        b, part = divmod(c, npc)
        yp[b, part * cfg.NOP:(part + 1) * cfg.NOP] = res.results[c]['y_p']
        b, part = divmod(c, nsc)
        ys[b, part * cfg.NOS:(part + 1) * cfg.NOS] = res.results[c]['y_s']
    return (yp, ys), res


def kernel(**inputs):
    cfg = Cfg()
    (yp, ys), _ = run_cfg(cfg, inputs)
    return (yp, ys)
```

```python
import sys
import numpy as np
from contextlib import ExitStack
import concourse.bass as bass
import concourse.mybir as mybir
from concourse.bass_utils import run_bass_kernel_spmd

F32 = mybir.dt.float32
BF16 = mybir.dt.bfloat16
AF = mybir.ActivationFunctionType
ALU = mybir.AluOpType

D = 1024
NH = 8
NKV = 2
HD = 128
DFF = 2816
NFC = DFF // 128
EPS = 1e-6
GRID_W = 64
HL, HR = 10, 9


class _Rec:
    def __getattr__(self, name):
        def f(*a, **k):
            self.call = (name, a, k)
            return self
        return f


class Sched:
    CE = ('pe', 'act', 'dve', 'pool')

    def __init__(self, nc, es):
        self.nc = nc
        self.es = es
        self.eng = dict(pe=nc.tensor, act=nc.scalar, dve=nc.vector, pool=nc.gpsimd, sp=nc.sync)
        self.ops = []
        self.tags = {}
        self.sem = {e: es.enter_context(nc.semaphore("sem_" + e)) for e in self.CE}
        self.slot_sem = {}
        self.cnt = {e: 0 for e in self.CE}
        self.slot_cnt = {}
        self.waited = {e: {} for e in self.eng}
        self.pend = {e: {} for e in self.eng}
        self.stats = dict(n_ops=0, n_wait=0)

    def op(self, eng, fn, r=(), w=(), dma=None):
        rec = _Rec()
        fn(rec)
        name, a, k = rec.call
        self.tags.setdefault(eng, []).append(sys._getframe(1).f_lineno)
        self.ops.append((eng, (lambda E, name=name, a=a, k=k: getattr(E, name)(*a, **k)), tuple(r), tuple(w), dma))

    def _wait(self, eng, key, val):
        if val <= 0 or self.waited[eng].get(key, 0) >= val:
            return
        self.waited[eng][key] = val
        s = self.slot_sem[key[1]] if key[0] == 's' else self.sem[key[1]]
        self.eng[eng].wait_ge(s, val)
        self.stats['n_wait'] += 1

    def flush(self):
        nc = self.nc
        ops = self.ops
        self.ops = []
        n = len(ops)
        lastw = {}
        readers = {}
        deps = [None] * n
        last_on = {}
        for i, (eng, fn, r, w, dma) in enumerate(ops):
            d = set()
            for k in r:
                j = lastw.get(k)
                if j is not None:
                    d.add(j)
                if isinstance(k, tuple) and k[0] in ('ps', 'psb'):
                    for j in readers.get(k, ()):
                        if ops[j][0] != eng:
                            d.add(j)
            for k in w:
                j = lastw.get(k)
                if j is not None:
                    d.add(j)
                for j in readers.get(k, ()):
                    d.add(j)
            d.discard(i)
            if eng == 'pe':
                d = {j for j in d if ops[j][0] != 'pe'}
            deps[i] = d
            for k in r:
                readers.setdefault(k, []).append(i)
            for k in w:
                lastw[k] = i
                readers[k] = []
            if dma is None:
                last_on[eng] = i
        signal = [False] * n
        for i in range(n):
            for j in deps[i]:
                if ops[j][4] is None:
                    signal[j] = True
        for e, i in last_on.items():
            signal[i] = True
        sigval = [0] * n
        for i, (eng, fn, r, w, dma) in enumerate(ops):
            if dma is not None and dma not in self.slot_sem:
                self.slot_sem[dma] = self.es.enter_context(nc.semaphore("dq%d" % len(self.slot_sem)))
                self.slot_cnt[dma] = 0
            if self.pend[eng]:
                for key, val in self.pend[eng].items():
                    self._wait(eng, key, val)
                self.pend[eng] = {}
            need = {}
            for j in deps[i]:
                if ops[j][4] is not None:
                    key = ('s', ops[j][4])
                    val = self.slot_cnt[ops[j][4]]
                else:
                    key = ('e', ops[j][0])
                    val = sigval[j]
                if need.get(key, 0) < val:
                    need[key] = val
            for key, val in need.items():
                self._wait(eng, key, val)
            ins = fn(self.eng[eng])
            if dma is not None:
                self.slot_cnt[dma] += 16
                ins.then_inc(self.slot_sem[dma], 16)
            elif signal[i]:
                self.cnt[eng] += 1
                ins.then_inc(self.sem[eng], 1)
                sigval[i] = self.cnt[eng]
        self.stats['n_ops'] += n
        for e in self.eng:
            for ce in self.CE:
                if self.pend[e].get(('e', ce), 0) < self.cnt[ce]:
                    self.pend[e][('e', ce)] = self.cnt[ce]
            for s, v in self.slot_cnt.items():
                if self.pend[e].get(('s', s), 0) < v:
                    self.pend[e][('s', s)] = v

    def finish(self):
        self.flush()
        for key, val in self.pend['sp'].items():
            self._wait('sp', key, val)
        self.pend['sp'] = {}


class Cfg:
    def __init__(self, SP=16384, SS=4096, NOP=4096, NOS=2048, nwp=9, nws=5, CK=16):
        self.SP, self.SS, self.NOP, self.NOS = SP, SS, NOP, NOS
        self.CK = CK
        self.seqs = []
        for (S, NO, nw, tag) in ((SP, NOP, nwp, 'p'), (SS, NOS, nws, 's')):
            so = -(-NO // nw)
            wins = []
            o = 0
            while o < NO:
                no = min(so, NO - o)
                wins.append((o, no))
                o += no
            assert max(w[1] for w in wins) + HL + HR <= 512
            self.seqs.append(dict(S=S, NO=NO, NL=NO + HL + HR, wins=wins, tag=tag))


def vec_layout():
    off = {}
    c = 0
    for name, ncol in (('attn_norm', 8), ('pool_norm', 8), ('pool_scale', 8), ('ffn_norm', 16),
                       ('q_gain', 1), ('k_gain', 1), ('conv_w', 2 * 3 * 44), ('conv_b', 2 * 44)):
        off[name] = c
        c += ncol
    return off, c


def build(cfg):
    nc = bass.Bass("TRN2", target_bir_lowering=False)
    es_all = ExitStack()
    S = Sched(nc, es_all)
    voff, NV = vec_layout()

    def din(name, shape, dt=F32):
        return nc.dram_tensor(name, list(shape), dt, kind="ExternalInput").ap()

    def dscr(name, shape, dt=BF16):
        return nc.dram_tensor(name, list(shape), dt, kind="Internal").ap()

    sq = cfg.seqs
    xkv = [din("xkv_" + s['tag'], [s['S'], D]) for s in sq]
    xq = [din("xq_" + s['tag'], [s['NL'], D]) for s in sq]
    cosk = din("cosk", [128, cfg.SP])
    sink = din("sink", [128, cfg.SP])
    cosq = [din("cosq_" + s['tag'], [128, s['NL']]) for s in sq]
    sinq = [din("sinq_" + s['tag'], [128, s['NL']]) for s in sq]
    maskd = [din("mask_" + s['tag'], [128, s['NL']]) for s in sq]
    consts = din("consts", [128, 3 * 128])
    vecs = din("vecs", [128, NV])
    gbc = din("gbc", [128, D])
    w_qkv = din("w_qkv", [D, 1536])
    w_o = din("w_o", [D, D])
    w_pool = din("w_pool", [4 * 256, 256])
    w_up = din("w_up", [2, D, 2 * DFF])
    w_down = din("w_down", [2, DFF, D])
    yout = [nc.dram_tensor("y_" + s['tag'], [s['NO'], D], F32, kind="ExternalOutput").ap() for s in sq]

    wq_s = dscr("wq_s", [8, 128, 1024])
    wk_s = dscr("wk_s", [2, 128, 1024])
    wv_s = dscr("wv_s", [128, 2048])
    wo_s = dscr("wo_s", [8, 128, 1024])
    wup_s = dscr("wup_s", [2, NFC, 128, 2, 1024])
    wd_s = dscr("wd_s", [2, 8, 128, NFC * 128])
    wp_s = dscr("wp_s", [128, 2048])
    kT_s = [dscr("kT_" + s['tag'], [2, 128, s['S']]) for s in sq]
    v_s = [dscr("v_" + s['tag'], [2, 128, s['S'] // 128, 128]) for s in sq]

    def sb(es, name, shape, dt=F32):
        return es.enter_context(nc.sbuf_tensor(name, list(shape), dt))

    cst = sb(es_all, "cst", [128, 384])
    vec = sb(es_all, "vec", [128, NV])
    onesb = sb(es_all, "onesb", [128, 128], BF16)
    identb = sb(es_all, "identb", [128, 128], BF16)
    ident = cst[:, 0:128]
    perm = cst[:, 128:256]
    ones = cst[:, 256:384]
    S.op('sp', lambda e: e.dma_start(out=cst[:], in_=consts[:, :]), w=['cst'], dma='cst')
    S.op('sp', lambda e: e.dma_start(out=vec[:], in_=vecs[:, :]), w=['vec'], dma='vec')
    S.op('dve', lambda e: e.tensor_copy(out=onesb[:], in_=ones), r=['cst'], w=['onesb'])
    S.op('dve', lambda e: e.tensor_copy(out=identb[:], in_=ident), r=['cst'], w=['identb'])

    for l_ in range(2):
        for k_ in range(3):
            c0_ = voff['conv_w'] + l_ * 132 + k_ * 44 + NFC
            S.op('dve', lambda e, c0_=c0_: e.tensor_scalar(out=vec[:, c0_:c0_ + NFC], in0=vec[:, c0_:c0_ + NFC], scalar1=0.5,
                                                           scalar2=None, op0=ALU.mult), r=['vec'], w=['vec'])
        c0_ = voff['conv_b'] + l_ * 44 + NFC
        S.op('dve', lambda e, c0_=c0_: e.tensor_scalar(out=vec[:, c0_:c0_ + NFC], in0=vec[:, c0_:c0_ + NFC], scalar1=0.5,
                                                       scalar2=None, op0=ALU.mult), r=['vec'], w=['vec'])

    def vcol(name, i=0):
        c = voff[name] + i
        return vec[:, c:c + 1]

    PS = {}

    def bank(b):
        return PS['t'][:, b, :]

    def bk(b):
        return ('ps', b)

    es_w = ExitStack()
    STG = 4096
    NST = 3
    stf = [sb(es_w, "stf%d" % i, [128, STG]) for i in range(NST)]
    stb = [sb(es_w, "stb%d" % i, [128, STG], BF16) for i in range(NST)]
    wstate = dict(i=0)
    cast_engs = ['act', 'act', 'act']

    def prep(src, dst, nrc, cw, blocked, dkey=None):
        i = wstate['i']
        wstate['i'] += 1
        f, b = stf[i % NST], stb[i % NST]
        kf, kb = ('stf', i % NST), ('stb', i % NST)
        nel = nrc * cw
        fv = f[:, 0:nel].rearrange("p (c n) -> p c n", c=nrc)
        S.op('sp', lambda e: e.dma_start(out=fv, in_=src), w=[kf], dma=kf)
        ce = cast_engs[i % 3]
        if blocked:
            nb = cw // 128
            iv = f[:, 0:nel].rearrange("p (c b j) -> p b c j", c=nrc, b=nb)
            ov = b[:, 0:nel].rearrange("p (b c j) -> p b c j", c=nrc, b=nb)
            dv = b[:, 0:nel].rearrange("p (b x) -> p b x", b=nb)
        else:
            iv = f[:, 0:nel]
            ov = b[:, 0:nel]
            dv = b[:, 0:nel]
        if ce == 'act':
            if blocked:
                for bb in range(nb):
                    S.op('act', lambda e, bb=bb: e.activation(out=ov[:, bb], in_=iv[:, bb], func=AF.Copy),
                         r=[kf], w=[kb])
            else:
                S.op('act', lambda e: e.activation(out=ov, in_=iv, func=AF.Copy), r=[kf], w=[kb])
        else:
            if blocked:
                for bb in range(nb):
                    S.op(ce, lambda e, bb=bb: e.tensor_copy(out=ov[:, bb], in_=iv[:, bb]), r=[kf], w=[kb])
            else:
                S.op(ce, lambda e: e.tensor_copy(out=ov, in_=iv), r=[kf], w=[kb])
        S.op('sp', lambda e: e.dma_start(out=dst, in_=dv), r=[kb], w=([dkey] if dkey else []), dma=('wst', i % NST))

    def rows(w2d, nrc):
        return w2d.rearrange("(c p) n -> p c n", p=128)

    prep_steps = []

    def P(*args, **kw):
        prep_steps.append(lambda: prep(*args, **kw))
    qv = rows(w_qkv, 8)
    prep(qv[:, :, 1024:1280], wk_s[:, :, :].rearrange("b p x -> p b x"), 8, 256, True, dkey='wk_s')
    prep(qv[:, :, 1280:1536], wv_s[:, :], 8, 256, False, dkey='wv_s')
    for sl in range(2):
        P(qv[:, :, sl * 512:(sl + 1) * 512], wq_s[sl * 4:(sl + 1) * 4].rearrange("b p x -> p b x"), 8, 512, True)
    ov_ = rows(w_o, 8)
    for sl in range(2):
        P(ov_[:, :, sl * 512:(sl + 1) * 512], wo_s[sl * 4:(sl + 1) * 4].rearrange("b p x -> p b x"), 8, 512, True)
    P(rows(w_pool, 8), wp_s[:, :], 8, 256, False)
    for l in range(2):
        uv = rows(w_up[l], 8)
        for half in range(2):
            for b0 in range(0, NFC, 4):
                nb = min(4, NFC - b0)
                c0 = half * DFF + b0 * 128
                P(uv[:, :, c0:c0 + nb * 128],
                  wup_s[l, b0:b0 + nb, :, half, :].rearrange("b p x -> p b x"), 8, nb * 128, True)
        dv_ = rows(w_down[l], NFC)
        for m in range(8):
            P(dv_[:, :, m * 128:(m + 1) * 128], wd_s[l, m:m + 1].rearrange("b p x -> p b x"), NFC, 128, True)

    def rope_head(T, src_ps_bank, b_ss, b_pm, gain_col, cos_t, sin_t, N, out_bf, out_keys, post_scale, tagk, stages=(0, 1, 2), rope_key=None, aux='pool'):
        kg, sqt, t1, t2, rs = T['kg'], T['sq'], T['t1'], T['t2'], T['rs']
        src = bank(src_ps_bank)[:, 0:N]
        rk_ = rope_key if rope_key is not None else tagk + 'rope'
        if 0 in stages:
            S.op('act', lambda e: e.activation(out=kg[:, 0:N], in_=src, func=AF.Copy, scale=gain_col),
                 r=[bk(src_ps_bank), 'vec'], w=[tagk + 'kg'])
            S.op('act', lambda e: e.activation(out=sqt[:, 0:N], in_=src, func=AF.Square),
                 r=[bk(src_ps_bank)], w=[tagk + 'sq'])
        if 1 in stages:
            S.op('pe', lambda e: e.matmul(bank(b_ss)[:, 0:N], lhsT=ones, rhs=sqt[:, 0:N], start=True, stop=True),
                 r=['cst', tagk + 'sq'], w=[bk(b_ss)])
            S.op('dve', lambda e: e.tensor_scalar(out=rs[:, 0:N], in0=bank(b_ss)[:, 0:N], scalar1=1.0 / HD, scalar2=EPS,
                                                  op0=ALU.mult, op1=ALU.add), r=[bk(b_ss)], w=[tagk + 'rs'])
            S.op('act', lambda e: e.activation(out=rs[:, 0:N], in_=rs[:, 0:N], func=AF.Sqrt),
                 r=[tagk + 'rs'], w=[tagk + 'rs'])
            S.op('dve', lambda e: e.reciprocal(out=rs[:, 0:N], in_=rs[:, 0:N]), r=[tagk + 'rs'], w=[tagk + 'rs'])
        if 2 in stages:
            S.op('pe', lambda e: e.matmul(bank(b_pm)[:, 0:N], lhsT=perm, rhs=kg[:, 0:N], start=True, stop=True),
                 r=['cst', tagk + 'kg'], w=[bk(b_pm)])
            S.op('dve', lambda e: e.tensor_tensor(out=t2[:, 0:N], in0=bank(b_pm)[:, 0:N], in1=sin_t, op=ALU.mult),
                 r=[bk(b_pm), rk_], w=[tagk + 't2'])
            S.op(aux, lambda e: e.tensor_tensor(out=t1[:, 0:N], in0=kg[:, 0:N], in1=cos_t, op=ALU.mult),
                 r=[tagk + 'kg', rk_], w=[tagk + 't1'])
            S.op(aux, lambda e: e.tensor_tensor(out=t1[:, 0:N], in0=t1[:, 0:N], in1=t2[:, 0:N], op=ALU.add),
                 r=[tagk + 't1', tagk + 't2'], w=[tagk + 't1'])
            S.op('dve', lambda e: e.scalar_tensor_tensor(out=out_bf, in0=t1[:, 0:N], scalar=post_scale, in1=rs[:, 0:N],
                                                         op0=ALU.mult, op1=ALU.mult),
                 r=[tagk + 't1', tagk + 'rs'], w=out_keys)

    es_a = es_w
    PS['t'] = es_a.enter_context(nc.psum_tensor("psA", [128, 6, 512], F32))
    psb = es_a.enter_context(nc.psum_tensor("psb", [128, 2, 1024], BF16))
    xa = [sb(es_a, "xa%d" % i, [128, 4, D]) for i in range(2)]
    xn = [sb(es_a, "xn%d" % i, [128, 4, D], BF16) for i in range(2)]
    hTa = [sb(es_a, "hTa%d" % i, [128, 8, 512], BF16) for i in range(2)]
    wk_t = sb(es_a, "wk_t", [128, 2, 8, 128], BF16)
    wv_t = sb(es_a, "wv_t", [128, 8, 256], BF16)
    gbc_t = sb(es_a, "gbc_t", [128, D])
    ropeA = [sb(es_a, "ropeA%d" % i, [128, 2, 512]) for i in range(2)]
    TA2 = [{k: sb(es_a, "TA%d_" % g_ + k, [128, 512]) for k in ('kg', 'sq', 't1', 't2', 'rs')} for g_ in range(2)]
    ssa = [sb(es_a, "ssa%d" % i, [128, 8]) for i in range(2)]
    kout = [sb(es_a, "kout%d" % i, [128, 512], BF16) for i in range(4)]
    vout = [sb(es_a, "vout%d" % i, [128, 4, 256], BF16) for i in range(2)]

    S.op('sp', lambda e: e.dma_start(out=wk_t[:].rearrange("p a c j -> p a (c j)"),
                                     in_=wk_s[:, :, :].rearrange("a p x -> p a x")), r=['wk_s'], w=['wk_t'], dma='wk_t')
    S.op('sp', lambda e: e.dma_start(out=wv_t[:].rearrange("p c j -> p (c j)"), in_=wv_s[:, :]),
         r=['wv_s'], w=['wv_t'], dma='wv_t')
    S.op('sp', lambda e: e.dma_start(out=gbc_t[:], in_=gbc[:, :]), w=['gbc_t'], dma='gbc_t')

    tiles = [(si, ti) for si, s in enumerate(sq) for ti in range(s['S'] // 512)]

    def frontA(t):
        si, ti = tiles[t]
        t0 = ti * 512
        pb = t % 2
        xat, xnt, hTt, rp, sst = xa[pb], xn[pb], hTa[pb], ropeA[pb], ssa[pb]
        kx = ('xa', pb)
        S.op('sp', lambda e: e.dma_start(out=xat[:], in_=xkv[si][t0:t0 + 512, :].rearrange("(c p) d -> p c d", p=128)),
             w=[kx], dma=kx)
        kr = ('ropeA', pb)
        S.op('sp', lambda e: e.dma_start(out=rp[:, 0, :], in_=cosk[:, t0:t0 + 512]), w=[kr], dma=kr)
        S.op('sp', lambda e: e.dma_start(out=rp[:, 1, :], in_=sink[:, t0:t0 + 512]), w=[kr], dma=kr)
        for c in range(4):
            S.op('act', lambda e, c=c: e.activation(out=xnt[:, c, :], in_=xat[:, c, :], func=AF.Square,
                                                    accum_out=sst[:, c:c + 1]),
                 r=[kx], w=[('ssa', pb, c), ('xn', pb, c)])
        S.op('dve', lambda e: e.tensor_scalar(out=sst[:, 4:8], in0=sst[:, 0:4], scalar1=1.0 / D, scalar2=EPS,
                                              op0=ALU.mult, op1=ALU.add),
             r=[('ssa', pb, c) for c in range(4)], w=[('ssa2', pb)])
        S.op('act', lambda e: e.activation(out=sst[:, 4:8], in_=sst[:, 4:8], func=AF.Sqrt), r=[('ssa2', pb)], w=[('ssa2', pb)])
        S.op('dve', lambda e: e.reciprocal(out=sst[:, 4:8], in_=sst[:, 4:8]), r=[('ssa2', pb)], w=[('ssa2', pb)])
        for c in range(4):
            S.op('dve', lambda e, c=c: e.scalar_tensor_tensor(
                out=xnt[:, c, :], in0=xat[:, c, :], scalar=sst[:, 4 + c:5 + c], in1=gbc_t[:],
                op0=ALU.mult, op1=ALU.mult), r=[kx, ('ssa2', pb), 'gbc_t'], w=[('xn', pb, c)])
        for f in range(8):
            for c in range(4):
                S.op('pe', lambda e, f=f, c=c: e.transpose(psb[:, f % 2, c * 128:(c + 1) * 128],
                                                           xnt[:, c, f * 128:(f + 1) * 128], identb[:]),
                     r=[('xn', pb, c), 'identb'], w=[('psb', f % 2)])
            if f % 2 == 0:
                S.op('act', lambda e, f=f: e.activation(out=hTt[:, f, :], in_=psb[:, f % 2, 0:512], func=AF.Copy),
                     r=[('psb', f % 2)], w=[('hTa', pb, f)])
            else:
                S.op('dve', lambda e, f=f: e.tensor_copy(out=hTt[:, f, :], in_=psb[:, f % 2, 0:512]),
                     r=[('psb', f % 2)], w=[('hTa', pb, f)])

    def backA(t):
        si, ti = tiles[t]
        t0 = ti * 512
        pb = t % 2
        hTt, rp = hTa[pb], ropeA[pb]
        vo = vout[pb]
        vok = ('vout', pb)

        def vpart(cs):
            for c in cs:
                pbv = 2 + c % 2
                for f in range(8):
                    S.op('pe', lambda e, c=c, f=f, pbv=pbv: e.matmul(bank(pbv)[:, 0:256], lhsT=hTt[:, f, c * 128:(c + 1) * 128],
                                                                     rhs=wv_t[:, f, :], start=(f == 0), stop=(f == 7)),
                         r=['wv_t', ('hTa', pb, f)], w=[bk(pbv)])
                if c % 2 == 0:
                    S.op('act', lambda e, c=c, pbv=pbv: e.activation(out=vo[:, c, :], in_=bank(pbv)[:, 0:256], func=AF.Copy),
                         r=[bk(pbv)], w=[vok])
                else:
                    S.op('dve', lambda e, c=c, pbv=pbv: e.tensor_copy(out=vo[:, c, :], in_=bank(pbv)[:, 0:256]),
                         r=[bk(pbv)], w=[vok])

        def rope(g, stages):
            ko = kout[(t * 2 + g) % 4]
            kok = ('kout', (t * 2 + g) % 4)
            rope_head(TA2[g], g, 4 + g, 4 + g, vcol('k_gain'), rp[:, 0, :], rp[:, 1, :], 512, ko[:], [kok], 1.0, 'A%d' % g,
                      stages=stages, rope_key=('ropeA', pb), aux='dve')
            if 2 in stages:
                S.op('sp', lambda e: e.dma_start(out=kT_s[si][g, :, t0:t0 + 512], in_=ko[:]), r=[kok], dma=kok)

        for g in range(NKV):
            for f in range(8):
                S.op('pe', lambda e, g=g, f=f: e.matmul(bank(g)[:, :], lhsT=wk_t[:, g, f, :], rhs=hTt[:, f, :],
                                                        start=(f == 0), stop=(f == 7)),
                     r=['wk_t', ('hTa', pb, f)], w=[bk(g)])
        for g in range(NKV):
            rope(g, (0,))
        vpart((0, 1))
        for g in range(NKV):
            rope(g, (1,))
        vpart((2, 3))
        for g in range(NKV):
            rope(g, (2,))
        for g in range(NKV):
            S.op('sp', lambda e, g=g: e.dma_start(out=v_s[si][g, :, ti * 4:(ti + 1) * 4, :], in_=vo[:, :, g * 128:(g + 1) * 128]),
                 r=[vok], dma=vok)

    frontA(0)
    per_tile = -(-len(prep_steps) // max(1, len(tiles) - 4))
    for t in range(len(tiles)):
        if t + 1 < len(tiles):
            frontA(t + 1)
        for _ in range(per_tile):
            if prep_steps:
                prep_steps.pop(0)()
        backA(t)
    while prep_steps:
        prep_steps.pop(0)()
    S.flush()
    es_a.close()

    es_b = ExitStack()
    PS['t'] = es_b.enter_context(nc.psum_tensor("psB", [128, 8, 512], F32))
    xT = sb(es_b, "xT", [128, 8, 512])
    hT = sb(es_b, "hT", [128, 8, 512], BF16)
    hF = sb(es_b, "hF", [128, 512])
    oT = sb(es_b, "oT", [128, 8, 512], BF16)
    aT = sb(es_b, "aT", [128, NFC, 512], BF16)
    plT = oT
    xw = [sb(es_b, "xw%d" % i, [128, D]) for i in range(2)]
    qT = [sb(es_b, "qT%d" % i, [128, 512], BF16) for i in range(2)]
    NPT = 5
    pT = [sb(es_b, "pT%d" % i, [128, 1024], BF16) for i in range(NPT)]
    RK = 3
    kring = [sb(es_b, "kr%d" % i, [128, cfg.CK * 128], BF16) for i in range(RK)]
    vring = [sb(es_b, "vr%d" % i, [128, cfg.CK, 128], BF16) for i in range(RK)]
    TB = {k: sb(es_b, "TB_" + k, [128, 512]) for k in ('kg', 'sq', 't1', 't2', 'rs')}
    ropeB = sb(es_b, "ropeB", [128, 2, 512])
    maskt = sb(es_b, "maskt", [128, 512])
    icnt = sb(es_b, "icnt", [128, 4, 512])
    rstd = sb(es_b, "rstd", [128, 512])
    osb = sb(es_b, "osb", [128, 512])
    lsb = sb(es_b, "lsb", [128, 512])
    tmpr = [sb(es_b, "tmpr%d" % i, [128, 512]) for i in range(2)]
    sqn = tmpr
    ffA = [sb(es_b, "ffA%d" % i, [128, 2, 512]) for i in range(4)]
    ffAf = [t_[:].rearrange("p a n -> p (a n)") for t_ in ffA]
    cg = [ffA[0][:, p_, :] for p_ in range(2)]
    cv = [ffA[1][:, p_, :] for p_ in range(2)]
    th = [ffA[2][:, p_, :] for p_ in range(2)]
    a0 = [ffA[3][:, p_, :] for p_ in range(2)]
    pa = [TB['kg'], TB['sq'], TB['t1']]
    wq_t = [sb(es_b, "wq_t%d" % i, [128, 8, 128], BF16) for i in range(2)]
    wo_t = [sb(es_b, "wo_t%d" % i, [128, 8, 128], BF16) for i in range(2)]
    wup_t = [sb(es_b, "wup_t%d" % i, [128, 2, 8, 128], BF16) for i in range(4)]
    wd_t = [sb(es_b, "wd_t%d" % i, [128, NFC, 128], BF16) for i in range(2)]
    wp_t = sb(es_b, "wp_t", [128, 8, 256], BF16)
    S.op('sp', lambda e: e.dma_start(out=wp_t[:].rearrange("p c j -> p (c j)"), in_=wp_s[:, :]), w=['wp_t'], dma='wp_t')

    cnt = dict(wq=0, wo=0, wup=0, wd=0, xw=0, kv=0, q=0, p=0, sqn=0, tmpr=0, ff=0)

    def fm_norm(N, gname, gi, out_fn, out_keys_fn, chunks=range(8)):
        for c in range(8):
            i = cnt['sqn'] % 2
            cnt['sqn'] += 1
            S.op('act', lambda e, c=c, i=i: e.activation(out=sqn[i][:, 0:N], in_=xT[:, c, 0:N], func=AF.Square),
                 r=[('xT', c)], w=[('tmpr', i)])
            S.op('pe', lambda e, c=c, i=i: e.matmul(bank(6)[:, 0:N], lhsT=ones, rhs=sqn[i][:, 0:N],
                                                    start=(c == 0), stop=(c == 7)), r=['cst', ('tmpr', i)], w=[bk(6)])
        S.op('dve', lambda e: e.tensor_scalar(out=rstd[:, 0:N], in0=bank(6)[:, 0:N], scalar1=1.0 / D, scalar2=EPS,
                                              op0=ALU.mult, op1=ALU.add), r=[bk(6)], w=['rstd'])
        S.op('act', lambda e: e.activation(out=rstd[:, 0:N], in_=rstd[:, 0:N], func=AF.Sqrt), r=['rstd'], w=['rstd'])
        S.op('dve', lambda e: e.reciprocal(out=rstd[:, 0:N], in_=rstd[:, 0:N]), r=['rstd'], w=['rstd'])

    def norm_chunk(N, c, gname, gi, out_ap, out_keys):
        S.op('dve', lambda e: e.scalar_tensor_tensor(out=out_ap, in0=xT[:, c, 0:N], scalar=vcol(gname, gi * 8 + c),
                                                     in1=rstd[:, 0:N], op0=ALU.mult, op1=ALU.mult),
             r=[('xT', c), 'rstd', 'vec'], w=out_keys)

    def resid_add(N, m, psbank, scale_col):
        i = cnt['tmpr'] % 2
        cnt['tmpr'] += 1
        t = tmpr[i]
        S.op('dve', lambda e: e.scalar_tensor_tensor(out=t[:, 0:N], in0=bank(psbank)[:, 0:N],
                                                     scalar=(1.0 if scale_col is None else scale_col),
                                                     in1=maskt[:, 0:N], op0=ALU.mult, op1=ALU.mult),
             r=[bk(psbank), 'maskt', 'vec'], w=[('tmpr', i)])
        S.op('pool', lambda e: e.tensor_tensor(out=xT[:, m, 0:N], in0=xT[:, m, 0:N], in1=t[:, 0:N], op=ALU.add),
             r=[('xT', m), ('tmpr', i)], w=[('xT', m)])

    def ffn(N, l):
        fm_norm(N, 'ffn_norm', l, None, None)
        for c in range(8):
            norm_chunk(N, c, 'ffn_norm', l, hT[:, c, 0:N], [('hT', c)])

        def load_wup(j):
            i = cnt['wup'] % 4
            cnt['wup'] += 1
            S.op('sp', lambda e: e.dma_start(out=wup_t[i][:].rearrange("p a c j -> p (a c j)"),
                                             in_=wup_s[l, j].rearrange("p a x -> p (a x)")),
                 w=[('wup', i)], dma=('wup', i))
            return i
        slots = {}
        for j in range(min(3, NFC)):
            slots[j] = load_wup(j)
        cwb = voff['conv_w'] + l * 3 * 44
        cbb = voff['conv_b'] + l * 44
        for j in range(NFC):
            if j + 3 < NFC:
                slots[j + 3] = load_wup(j + 3)
            wi = slots[j]
            par = cnt['ff'] % 2
            cnt['ff'] += 1
            bg, bv = (2, 3) if par == 0 else (4, 0)
            for half, bnk in ((0, bg), (1, bv)):
                for c in range(8):
                    S.op('pe', lambda e, half=half, bnk=bnk, c=c, wi=wi: e.matmul(
                        bank(bnk)[:, 0:N], lhsT=wup_t[wi][:, half, c, :], rhs=hT[:, c, 0:N],
                        start=(c == 0), stop=(c == 7)), r=[('wup', wi), ('hT', c)], w=[bk(bnk)])

            def taps(half):
                ch = half * NFC + j
                return [vec[:, cwb + k_ * 44 + ch: cwb + k_ * 44 + ch + 1] for k_ in range(3)] + \
                       [vec[:, cbb + ch: cbb + ch + 1]]
            kg_, kv_, kt_, ka_ = ('ffA', 0, par), ('ffA', 1, par), ('ffA', 2, par), ('ffA', 3, par)
            cgt, cvt, tht, a0t = cg[par], cv[par], th[par], a0[par]
            w0, w1, w2, bb = taps(0)
            S.op('act', lambda e: e.activation(out=cgt[:, 0:N], in_=bank(bg)[:, 0:N], func=AF.Identity, scale=w1, bias=bb),
                 r=[bk(bg), 'vec'], w=[kg_])
            S.op('act', lambda e: e.activation(out=a0t[:, 1:N], in_=bank(bg)[:, 0:N - 1], func=AF.Copy, scale=w0),
                 r=[bk(bg), 'vec'], w=[ka_])
            S.op('dve', lambda e: e.scalar_tensor_tensor(out=cgt[:, 0:N - 1], in0=bank(bg)[:, 1:N], scalar=w2,
                                                         in1=cgt[:, 0:N - 1], op0=ALU.mult, op1=ALU.add),
                 r=[bk(bg), kg_, 'vec'], w=[kg_])
            S.op('pool', lambda e: e.tensor_tensor(out=cgt[:, 1:N], in0=cgt[:, 1:N], in1=a0t[:, 1:N], op=ALU.add),
                 r=[kg_, ka_], w=[kg_])
            w0, w1, w2, bb = taps(1)
            S.op('act', lambda e: e.activation(out=cvt[:, 0:N], in_=bank(bv)[:, 0:N], func=AF.Identity, scale=w1, bias=bb),
                 r=[bk(bv), 'vec'], w=[kv_])
            S.op('dve', lambda e: e.scalar_tensor_tensor(out=cvt[:, 1:N], in0=bank(bv)[:, 0:N - 1], scalar=w0,
                                                         in1=cvt[:, 1:N], op0=ALU.mult, op1=ALU.add),
                 r=[bk(bv), kv_, 'vec'], w=[kv_])
            S.op('dve', lambda e: e.scalar_tensor_tensor(out=cvt[:, 0:N - 1], in0=bank(bv)[:, 1:N], scalar=w2,
                                                         in1=cvt[:, 0:N - 1], op0=ALU.mult, op1=ALU.add),
                 r=[bk(bv), kv_, 'vec'], w=[kv_])
            S.op('act', lambda e: e.activation(out=tht[:, 0:N], in_=cgt[:, 0:N], func=AF.Tanh, scale=0.5),
                 r=[kg_], w=[kt_])
            S.op('pool', lambda e: e.tensor_tensor(out=cvt[:, 0:N], in0=cvt[:, 0:N], in1=cgt[:, 0:N], op=ALU.mult),
                 r=[kg_, kv_], w=[kv_])
            S.op('dve', lambda e, j=j: e.scalar_tensor_tensor(out=aT[:, j, 0:N], in0=tht[:, 0:N], scalar=1.0, in1=cvt[:, 0:N],
                                                              op0=ALU.add, op1=ALU.mult),
                 r=[kt_, kv_], w=[('aT', j)])
        def load_wd(m):
            i = cnt['wd'] % 2
            cnt['wd'] += 1
            S.op('sp', lambda e: e.dma_start(out=wd_t[i][:].rearrange("p j x -> p (j x)"), in_=wd_s[l, m]),
                 w=[('wd', i)], dma=('wd', i))
            return i
        dsl = {0: load_wd(0)}
        for m in range(8):
            if m + 1 < 8:
                dsl[m + 1] = load_wd(m + 1)
            wi = dsl[m]
            bnk = 5 if m % 2 == 0 else 1
            for j in range(NFC):
                S.op('pe', lambda e, j=j, wi=wi, bnk=bnk: e.matmul(bank(bnk)[:, 0:N], lhsT=wd_t[wi][:, j, :], rhs=aT[:, j, 0:N],
                                                                   start=(j == 0), stop=(j == NFC - 1)),
                     r=[('wd', wi), ('aT', j)], w=[bk(bnk)])
            resid_add(N, m, bnk, None)

    for si, s in enumerate(sq):
        nkt = s['S'] // 128
        CK = min(cfg.CK, nkt)
        nck = nkt // CK
        jobs = [(wi_, h, ci) for wi_ in range(len(s['wins'])) for h in range(NH) for ci in range(nck)]
        jslot = {}

        def load_kv(jn, si=si, CK=CK, jobs=jobs, jslot=jslot):
            if jn >= len(jobs) or jn in jslot:
                return
            (_, h, ci) = jobs[jn]
            g = h // (NH // NKV)
            i = cnt['kv'] % RK
            cnt['kv'] += 1
            jslot[jn] = i
            S.op('sp', lambda e: e.dma_start(out=kring[i][:, 0:CK * 128], in_=kT_s[si][g, :, ci * CK * 128:(ci + 1) * CK * 128]),
                 w=[('kv', i)], dma=('kv', i))
            S.op('sp', lambda e: e.dma_start(out=vring[i][:, 0:CK, :], in_=v_s[si][g, :, ci * CK:(ci + 1) * CK, :]),
                 w=[('kv', i)], dma=('kv', i))

        for wi_, (o0, no) in enumerate(s['wins']):
            N = no + HL + HR
            r0 = o0
            ntc = -(-N // 128)
            S.op('sp', lambda e, r0=r0, N=N, si=si: e.dma_start(out=maskt[:, 0:N], in_=maskd[si][:, r0:r0 + N]),
                 w=['maskt'], dma='maskt')
            S.op('sp', lambda e, r0=r0, N=N, si=si: e.dma_start(out=ropeB[:, 0, 0:N], in_=cosq[si][:, r0:r0 + N]),
                 w=['Brope'], dma='ropeB')
            S.op('sp', lambda e, r0=r0, N=N, si=si: e.dma_start(out=ropeB[:, 1, 0:N], in_=sinq[si][:, r0:r0 + N]),
                 w=['Brope'], dma='ropeB')
            for tc in range(ntc):
                nt = min(128, N - tc * 128)
                i = cnt['xw'] % 2
                cnt['xw'] += 1
                S.op('sp', lambda e, i=i, tc=tc, nt=nt, r0=r0, si=si: e.dma_start(
                    out=xw[i][0:nt, :], in_=xq[si][r0 + tc * 128: r0 + tc * 128 + nt, :]), w=[('xw', i)], dma=('xw', i))
                for f in range(8):
                    bnk = f % 4
                    S.op('pe', lambda e, i=i, f=f, tc=tc, nt=nt, bnk=bnk: e.transpose(
                        bank(bnk)[:, tc * 128: tc * 128 + nt], xw[i][0:nt, f * 128:(f + 1) * 128], ident[0:nt, 0:nt]),
                        r=[('xw', i), 'cst'], w=[bk(bnk)])
                    if f % 2 == 0:
                        S.op('act', lambda e, f=f, tc=tc, nt=nt, bnk=bnk: e.activation(
                            out=xT[:, f, tc * 128: tc * 128 + nt], in_=bank(bnk)[:, tc * 128: tc * 128 + nt], func=AF.Copy),
                            r=[bk(bnk)], w=[('xT', f)])
                    else:
                        S.op('dve', lambda e, f=f, tc=tc, nt=nt, bnk=bnk: e.tensor_copy(
                            out=xT[:, f, tc * 128: tc * 128 + nt], in_=bank(bnk)[:, tc * 128: tc * 128 + nt]),
                            r=[bk(bnk)], w=[('xT', f)])

            def window_sums(cur, ck, g, out_ap, out_keys, eng, temps):
                steps = [(1, 0), (1, 1), (2, 2), (4, 4)]
                lo, hi = 0, N
                for lvl in range(g + 1):
                    sl, sr = steps[lvl]
                    dst, dk = (out_ap, out_keys) if lvl == g else temps[lvl]
                    nlo, nhi = lo + sl, hi - sr
                    S.op(eng, lambda e, dst=dst, cur=cur, nlo=nlo, nhi=nhi, sl=sl, sr=sr: e.tensor_tensor(
                        out=dst[:, nlo:nhi], in0=cur[:, nlo - sl:nhi - sl], in1=cur[:, nlo + sr:nhi + sr], op=ALU.add),
                        r=ck, w=dk)
                    cur, ck, lo, hi = dst, dk, nlo, nhi

            DVE_T = [(TB['kg'], ['Bkg']), (TB['sq'], ['Bsq']), (TB['t1'], ['Bt1'])]
            POOL_T = [(ffA[0][:, 0, :], [('ffA', 0, 0)]), (ffA[0][:, 1, :], [('ffA', 0, 1)]), (ffA[1][:, 0, :], [('ffA', 1, 0)])]
            for g in range(4):
                S.op('pool', lambda e, g=g: e.tensor_copy(out=icnt[:, g, 0:N], in_=maskt[:, 0:N]), r=['maskt'], w=[('icnt', g)])
                window_sums(maskt, ['maskt'], g, icnt[:, g, :], [('icnt', g)], 'pool', POOL_T)
                S.op('dve', lambda e, g=g: e.tensor_scalar(out=icnt[:, g, 0:N], in0=icnt[:, g, 0:N], scalar1=1.0, scalar2=None,
                                                           op0=ALU.max), r=[('icnt', g)], w=[('icnt', g)])
                S.op('dve', lambda e, g=g: e.reciprocal(out=icnt[:, g, 0:N], in_=icnt[:, g, 0:N]),
                     r=[('icnt', g)], w=[('icnt', g)])
            fm_norm(N, 'attn_norm', 0, None, None)
            for c in range(8):
                norm_chunk(N, c, 'attn_norm', 0, hT[:, c, 0:N], [('hT', c)])

            def load_wq(h):
                i = cnt['wq'] % 2
                cnt['wq'] += 1
                S.op('sp', lambda e: e.dma_start(out=wq_t[i][:].rearrange("p c j -> p (c j)"), in_=wq_s[h]),
                     w=[('wq', i)], dma=('wq', i))
                return i

            def qprep_a(h, wslot):
                for c in range(8):
                    S.op('pe', lambda e, c=c: e.matmul(bank(5)[:, 0:N], lhsT=wq_t[wslot][:, c, :], rhs=hT[:, c, 0:N],
                                                       start=(c == 0), stop=(c == 7)),
                         r=[('wq', wslot), ('hT', c)], w=[bk(5)])

            def qprep_b(h, stages=(0, 1, 2), qi=None):
                if qi is None:
                    qi = cnt['q'] % 2
                    cnt['q'] += 1
                xb_ = 6 + (h % 2)
                rope_head(TB, 5, xb_, xb_, vcol('q_gain'), ropeB[:, 0, 0:N], ropeB[:, 1, 0:N], N, qT[qi][:, 0:N],
                          [('qT', qi)], float(HD) ** -0.5, 'B', stages=stages)
                return qi

            base_job = wi_ * NH * nck
            if wi_ == 0:
                load_kv(0)
                load_kv(1)
            wsl = load_wq(0)
            qprep_a(0, wsl)
            qcur = qprep_b(0)
            npair = nkt // 2
            ROWSUM_PAT = (('pe', 0), ('dve', 1), ('pe', 0), ('dve', 2))
            ACCK = [[('ffA', k_, 0), ('ffA', k_, 1)] for k_ in range(3)]

            def finish_head(hh):
                lb = 6 + (hh % 2)
                S.op('pe', lambda e: e.matmul(bank(lb)[:, 0:N], lhsT=ones, rhs=lsb[:, 0:N], start=False, stop=True),
                     r=['cst', 'lsb'], w=[bk(lb)])
                S.op('dve', lambda e: e.reciprocal(out=rstd[:, 0:N], in_=bank(lb)[:, 0:N]), r=[bk(lb)], w=['rstd'])
                S.op('pool', lambda e: e.tensor_tensor(out=oT[:, hh, 0:N], in0=osb[:, 0:N], in1=rstd[:, 0:N], op=ALU.mult),
                     r=['osb', 'rstd'], w=[('oT', hh)])

            NPS = 2 * NPT
            LOOK = 3
            for h in range(NH):
                if h + 1 < NH:
                    wsl_n = load_wq(h + 1)
                p_of = {}
                accinit = [False, False, False]
                lb_ = 6 + (h % 2)

                def pslot(s_):
                    return pT[s_ // 2][:, (s_ % 2) * 512:(s_ % 2) * 512 + N]

                def emit_qk(j, h=h, qcur=qcur):
                    sbk = 1 + (j % 4)
                    ci, jj = divmod(j, CK)
                    slot = jslot[base_job + h * nck + ci]
                    S.op('pe', lambda e: e.matmul(bank(sbk)[:, 0:N], lhsT=kring[slot][:, jj * 128:(jj + 1) * 128],
                                                  rhs=qT[qcur][:, 0:N], start=True, stop=True),
                         r=[('kv', slot), ('qT', qcur)], w=[bk(sbk)])
                    ps_ = cnt['p'] % NPS
                    cnt['p'] += 1
                    p_of[j] = ps_
                    S.op('act', lambda e: e.activation(out=pslot(ps_), in_=bank(sbk)[:, 0:N], func=AF.Exp),
                         r=[bk(sbk)], w=[('pT', ps_)])

                def emit_pv(j, h=h):
                    ps_ = p_of[j]
                    ci, jj = divmod(j, CK)
                    slot = jslot[base_job + h * nck + ci]
                    S.op('pe', lambda e: e.matmul(bank(0)[:, 0:N], lhsT=vring[slot][:, jj, :], rhs=pslot(ps_),
                                                  start=(j == 0), stop=(j == nkt - 1)),
                         r=[('kv', slot), ('pT', ps_)], w=[bk(0)])
                    eng_, k_ = ROWSUM_PAT[j % len(ROWSUM_PAT)]
                    if eng_ == 'pe':
                        S.op('pe', lambda e: e.matmul(bank(lb_)[:, 0:N], lhsT=onesb[:], rhs=pslot(ps_),
                                                      start=(j == 0), stop=False),
                             r=['onesb', ('pT', ps_)], w=[bk(lb_)])
                        return
                    acc = ffAf[k_]
                    if not accinit[k_]:
                        accinit[k_] = True
                        S.op(eng_, lambda e: e.tensor_copy(out=acc[:, 0:N], in_=pslot(ps_)), r=[('pT', ps_)], w=ACCK[k_])
                    else:
                        S.op(eng_, lambda e: e.tensor_tensor(out=acc[:, 0:N], in0=acc[:, 0:N], in1=pslot(ps_), op=ALU.add),
                             r=[('pT', ps_)] + ACCK[k_], w=ACCK[k_])

                for j in range(min(LOOK, nkt)):
                    cn = j // CK
                    if (base_job + h * nck + cn) not in jslot:
                        load_kv(base_job + h * nck + cn)
                    emit_qk(j)
                for j in range(nkt):
                    ci, jj = divmod(j, CK)
                    if jj == 0:
                        load_kv(base_job + h * nck + ci + 1)
                        load_kv(base_job + h * nck + ci + 2)
                    if j + LOOK < nkt:
                        cn = (j + LOOK) // CK
                        if (base_job + h * nck + cn) not in jslot:
                            load_kv(base_job + h * nck + cn)
                        emit_qk(j + LOOK)
                    emit_pv(j)
                    if j == 0 and h + 1 < NH:
                        qprep_a(h + 1, wsl_n)
                        qnext = qprep_b(h + 1, stages=(0,))
                    if j == min(8, nkt - 3) and h > 0:
                        finish_head(h - 1)
                    if j == min(14, nkt - 2) and h + 1 < NH:
                        qprep_b(h + 1, stages=(1,), qi=qnext)
                    if j == min(20, nkt - 1) and h + 1 < NH:
                        qprep_b(h + 1, stages=(2,), qi=qnext)
                if h + 1 < NH:
                    if (base_job + (h + 1) * nck) not in jslot:
                        load_kv(base_job + (h + 1) * nck)
                S.op('act', lambda e: e.activation(out=osb[:, 0:N], in_=bank(0)[:, 0:N], func=AF.Copy), r=[bk(0)], w=['osb'])
                used = [k_ for k_ in (1, 2) if accinit[k_]]
                assert used
                if len(used) == 2:
                    S.op('dve', lambda e: e.tensor_tensor(out=lsb[:, 0:N], in0=ffAf[1][:, 0:N], in1=ffAf[2][:, 0:N], op=ALU.add),
                         r=ACCK[1] + ACCK[2], w=['lsb'])
                else:
                    u0 = used[0]
                    S.op('dve', lambda e: e.tensor_copy(out=lsb[:, 0:N], in_=ffAf[u0][:, 0:N]), r=ACCK[u0], w=['lsb'])
                if h + 1 < NH:
                    qcur = qnext
            finish_head(NH - 1)
            nb_ = (wi_ + 1) * NH * nck
            load_kv(nb_)
            load_kv(nb_ + 1)
            def load_wo(m):
                i = cnt['wo'] % 2
                cnt['wo'] += 1
                S.op('sp', lambda e: e.dma_start(out=wo_t[i][:].rearrange("p c j -> p (c j)"), in_=wo_s[m]),
                     w=[('wo', i)], dma=('wo', i))
                return i
            osl = {0: load_wo(0)}
            for m in range(8):
                if m + 1 < 8:
                    osl[m + 1] = load_wo(m + 1)
                wsl = osl[m]
                bnk = 5 if m % 2 == 0 else 7
                for h in range(NH):
                    S.op('pe', lambda e, h=h, wsl=wsl, bnk=bnk: e.matmul(bank(bnk)[:, 0:N], lhsT=wo_t[wsl][:, h, :], rhs=oT[:, h, 0:N],
                                                                         start=(h == 0), stop=(h == NH - 1)),
                         r=[('wo', wsl), ('oT', h)], w=[bk(bnk)])
                resid_add(N, m, bnk, None)
            ffn(N, 0)
            def shifted_add(eng, out_t, in_t, sh, keys_r, keys_w, first):
                pass
            fm_norm(N, 'pool_norm', 0, None, None)
            for c in (1, 0, 3, 2, 5, 4, 6, 7):
                g = c // 2
                on_pool = c in (1, 3, 5)
                eng_ = 'pool' if on_pool else 'dve'
                hbuf, hk = (ffA[2][:, 0, :], [('ffA', 2, 0)]) if on_pool else (hF, ['hF'])
                res, rk = (ffA[1][:, 1, :], [('ffA', 1, 1)]) if on_pool else (tmpr[0], [('tmpr', 0)])
                norm_chunk(N, c, 'pool_norm', 0, hbuf[:, 0:N], hk)
                window_sums(hbuf, hk, g, res, rk, eng_, POOL_T if on_pool else DVE_T)
                S.op(eng_, lambda e, g=g, res=res: e.tensor_tensor(out=res[:, 0:N], in0=res[:, 0:N], in1=icnt[:, g, 0:N], op=ALU.mult),
                     r=rk + [('icnt', g)], w=rk)
                S.op(eng_, lambda e, c=c, res=res, hbuf=hbuf: e.tensor_tensor(out=plT[:, c, 0:N], in0=res[:, 0:N], in1=hbuf[:, 0:N],
                                                                              op=ALU.subtract), r=rk + hk, w=[('oT', c)])
            for m in range(8):
                g = m // 2
                mo = m % 2
                bnk = 5 if m % 2 == 0 else 7
                for cc in range(2):
                    S.op('pe', lambda e, g=g, mo=mo, cc=cc, bnk=bnk: e.matmul(
                        bank(bnk)[:, 0:N], lhsT=wp_t[:, 2 * g + cc, mo * 128:(mo + 1) * 128], rhs=plT[:, 2 * g + cc, 0:N],
                        start=(cc == 0), stop=(cc == 1)), r=['wp_t', ('oT', 2 * g + cc)], w=[bk(bnk)])
                resid_add(N, m, bnk, vcol('pool_scale', m))
            ffn(N, 1)
            c0 = HL
            while c0 < HL + no:
                nt = min(128, HL + no - c0)
                i = cnt['xw'] % 2
                cnt['xw'] += 1
                for f in range(8):
                    bnk = 2 + (f // 4)
                    S.op('pe', lambda e, f=f, c0=c0, nt=nt, bnk=bnk: e.transpose(
                        bank(bnk)[0:nt, (f % 4) * 128:(f % 4 + 1) * 128], xT[:, f, c0:c0 + nt], ident),
                        r=[('xT', f), 'cst'], w=[bk(bnk)])
                    if f % 4 == 3:
                        hh = f // 4
                        if hh == 0:
                            S.op('act', lambda e, i=i, nt=nt, bnk=bnk, hh=hh: e.activation(
                                out=xw[i][0:nt, hh * 512:(hh + 1) * 512], in_=bank(bnk)[0:nt, :], func=AF.Copy),
                                r=[bk(bnk)], w=[('xw', i)])
                        else:
                            S.op('dve', lambda e, i=i, nt=nt, bnk=bnk, hh=hh: e.tensor_copy(
                                out=xw[i][0:nt, hh * 512:(hh + 1) * 512], in_=bank(bnk)[0:nt, :]),
                                r=[bk(bnk)], w=[('xw', i)])
                orow = o0 + (c0 - HL)
                S.op('sp', lambda e, i=i, nt=nt, orow=orow, si=si: e.dma_start(out=yout[si][orow:orow + nt, :], in_=xw[i][0:nt, :]),
                     r=[('xw', i)], dma=('xw', i))
                c0 += nt

    S.finish()
    es_b.close()
    es_all.close()
    return nc, S


def rope_tables(pos):
    pos = np.asarray(pos, dtype=np.int64)
    row = (pos // GRID_W).astype(np.float32)
    col = (pos % GRID_W).astype(np.float32)
    F = 32
    inv = (np.float32(10000.0) ** (-(np.arange(F, dtype=np.float32) / np.float32(F)))).astype(np.float32)
    ang_r = (row[None, :] * inv[:, None]).astype(np.float32)
    ang_c = (col[None, :] * inv[:, None]).astype(np.float32)
    cos = np.empty((128, len(pos)), np.float32)
    sin = np.empty((128, len(pos)), np.float32)
    for a, ang in ((0, ang_r), (1, ang_c)):
        c = np.cos(ang).astype(np.float32)
        s = np.sin(ang).astype(np.float32)
        cos[a * 64:a * 64 + 32] = c
        cos[a * 64 + 32:a * 64 + 64] = c
        sin[a * 64:a * 64 + 32] = -s
        sin[a * 64 + 32:a * 64 + 64] = s
    return cos, sin


def const_mats():
    ident = np.eye(128, dtype=np.float32)
    perm = np.zeros((128, 128), np.float32)
    for d in range(128):
        h = (d % 64) // 32
        partner = d + 32 if h == 0 else d - 32
        perm[partner, d] = 1.0
    ones = np.ones((128, 128), np.float32)
    return np.ascontiguousarray(np.concatenate([ident, perm, ones], axis=1))


def pack_vecs(inp):
    voff, NV = vec_layout()
    v = np.zeros((128, NV), np.float32)
    def cols(a):
        return np.asarray(a, np.float32).reshape(-1, 128).T
    v[:, voff['attn_norm']:voff['attn_norm'] + 8] = cols(inp['attn_norm'][0])
    v[:, voff['pool_norm']:voff['pool_norm'] + 8] = cols(inp['pool_norm'][0])
    v[:, voff['pool_scale']:voff['pool_scale'] + 8] = cols(inp['pool_scale'][0])
    for l in range(2):
        v[:, voff['ffn_norm'] + 8 * l:voff['ffn_norm'] + 8 * l + 8] = cols(inp['ffn_norm'][l])
        for k in range(3):
            c0 = voff['conv_w'] + l * 132 + k * 44
            v[:, c0:c0 + 44] = cols(inp['conv_w'][l, k])
        c0 = voff['conv_b'] + l * 44
        v[:, c0:c0 + 44] = cols(inp['conv_b'][l])
    v[:, voff['q_gain']] = np.asarray(inp['q_gain'][0], np.float32)
    v[:, voff['k_gain']] = np.asarray(inp['k_gain'][0], np.float32)
    return v


def make_in_maps(cfg, inp, n_cores=8):
    xp = np.asarray(inp['x_prompt'], np.float32)
    xs = np.asarray(inp['x_sample'], np.float32)
    npc = cfg.SP // cfg.NOP
    nsc = cfg.SS // cfg.NOS
    cosk, sink = rope_tables(np.arange(cfg.SP))
    shared = dict(
        cosk=cosk, sink=sink, consts=const_mats(), vecs=pack_vecs(inp),
        gbc=np.ascontiguousarray(np.broadcast_to(np.asarray(inp['attn_norm'][0], np.float32)[None, :], (128, D))),
        w_qkv=np.ascontiguousarray(np.asarray(inp['w_qkv'][0], np.float32)),
        w_o=np.ascontiguousarray(np.asarray(inp['w_o'][0], np.float32)),
        w_pool=np.ascontiguousarray(np.asarray(inp['w_pool'][0], np.float32).reshape(4 * 256, 256)),
        w_up=np.ascontiguousarray(np.asarray(inp['w_up'], np.float32)),
        w_down=np.ascontiguousarray(np.asarray(inp['w_down'], np.float32)),
    )
    maps = []
    for c in range(n_cores):
        m = dict(shared)
        for (tag, x, S_, NO, per) in (('p', xp, cfg.SP, cfg.NOP, npc), ('s', xs, cfg.SS, cfg.NOS, nsc)):
            b, part = divmod(c, per)
            q0 = part * NO
            NL = NO + HL + HR
            pos = np.arange(q0 - HL, q0 - HL + NL)
            valid = (pos >= 0) & (pos < S_)
            xl = np.zeros((NL, D), np.float32)
            xl[valid] = x[b, pos[valid]]
            cq, sq_ = rope_tables(np.where(valid, pos, 0))
            m['xkv_' + tag] = np.ascontiguousarray(x[b])
            m['xq_' + tag] = xl
            m['cosq_' + tag] = cq
            m['sinq_' + tag] = sq_
            m['mask_' + tag] = np.ascontiguousarray(np.broadcast_to(valid.astype(np.float32)[None, :], (128, NL)))
        maps.append(m)
    return maps


_CACHE = {}


def run_cfg(cfg, inp, n_cores=8, trace=False):
    key = (cfg.SP, cfg.SS, cfg.NOP, cfg.NOS)
    if key not in _CACHE:
        _CACHE[key] = build(cfg)
    nc, S = _CACHE[key]
    maps = make_in_maps(cfg, inp, n_cores)
    res = run_bass_kernel_spmd(nc, maps, core_ids=list(range(n_cores)), trace=trace)
    B = inp['x_prompt'].shape[0]
    Bs = inp['x_sample'].shape[0]
    yp = np.zeros((B, cfg.SP, D), np.float32)
    ys = np.zeros((Bs, cfg.SS, D), np.float32)
    npc = cfg.SP // cfg.NOP
    nsc = cfg.SS // cfg.NOS
    for c in range(n_cores):
        b, part = divmod(c, npc)
        yp[b, part * cfg.NOP:(part + 1) * cfg.NOP] = res.results[c]['y_p']
        b, part = divmod(c, nsc)
        ys[b, part * cfg.NOS:(part + 1) * cfg.NOS] = res.results[c]['y_s']
    return (yp, ys), res


def kernel(**inputs):
    cfg = Cfg()
    (yp, ys), _ = run_cfg(cfg, inputs)
    return (yp, ys)
```

```python
import sys
import numpy as np
from contextlib import ExitStack
import concourse.bass as bass
import concourse.mybir as mybir
from concourse.bass_utils import run_bass_kernel_spmd

F32 = mybir.dt.float32
BF16 = mybir.dt.bfloat16
AF = mybir.ActivationFunctionType
ALU = mybir.AluOpType

D = 1024
NH = 8
NKV = 2
HD = 128
DFF = 2816
NFC = DFF // 128
EPS = 1e-6
GRID_W = 64
HL, HR = 10, 9


class _Rec:
    def __getattr__(self, name):
        def f(*a, **k):
            self.call = (name, a, k)
            return self
        return f


class Sched:
    CE = ('pe', 'act', 'dve', 'pool')

    def __init__(self, nc, es):
        self.nc = nc
        self.es = es
        self.eng = dict(pe=nc.tensor, act=nc.scalar, dve=nc.vector, pool=nc.gpsimd, sp=nc.sync)
        self.ops = []
        self.tags = {}
        self.sem = {e: es.enter_context(nc.semaphore("sem_" + e)) for e in self.CE}
        self.slot_sem = {}
        self.cnt = {e: 0 for e in self.CE}
        self.slot_cnt = {}
        self.waited = {e: {} for e in self.eng}
        self.pend = {e: {} for e in self.eng}
        self.stats = dict(n_ops=0, n_wait=0)

    def op(self, eng, fn, r=(), w=(), dma=None):
        rec = _Rec()
        fn(rec)
        name, a, k = rec.call
        self.tags.setdefault(eng, []).append(sys._getframe(1).f_lineno)
        self.ops.append((eng, (lambda E, name=name, a=a, k=k: getattr(E, name)(*a, **k)), tuple(r), tuple(w), dma))

    def _wait(self, eng, key, val):
        if val <= 0 or self.waited[eng].get(key, 0) >= val:
            return
        self.waited[eng][key] = val
        s = self.slot_sem[key[1]] if key[0] == 's' else self.sem[key[1]]
        self.eng[eng].wait_ge(s, val)
        self.stats['n_wait'] += 1

    def flush(self):
        nc = self.nc
        ops = self.ops
        self.ops = []
        n = len(ops)
        lastw = {}
        readers = {}
        deps = [None] * n
        last_on = {}
        for i, (eng, fn, r, w, dma) in enumerate(ops):
            d = set()
            for k in r:
                j = lastw.get(k)
                if j is not None:
                    d.add(j)
                if isinstance(k, tuple) and k[0] in ('ps', 'psb'):
                    for j in readers.get(k, ()):
                        if ops[j][0] != eng:
                            d.add(j)
            for k in w:
                j = lastw.get(k)
                if j is not None:
                    d.add(j)
                for j in readers.get(k, ()):
                    d.add(j)
            d.discard(i)
            if eng == 'pe':
                d = {j for j in d if ops[j][0] != 'pe'}
            deps[i] = d
            for k in r:
                readers.setdefault(k, []).append(i)
            for k in w:
                lastw[k] = i
                readers[k] = []
            if dma is None:
                last_on[eng] = i
        signal = [False] * n
        for i in range(n):
            for j in deps[i]:
                if ops[j][4] is None:
                    signal[j] = True
        for e, i in last_on.items():
            signal[i] = True
        sigval = [0] * n
        for i, (eng, fn, r, w, dma) in enumerate(ops):
            if dma is not None and dma not in self.slot_sem:
                self.slot_sem[dma] = self.es.enter_context(nc.semaphore("dq%d" % len(self.slot_sem)))
                self.slot_cnt[dma] = 0
            if self.pend[eng]:
                for key, val in self.pend[eng].items():
                    self._wait(eng, key, val)
                self.pend[eng] = {}
            need = {}
            for j in deps[i]:
                if ops[j][4] is not None:
                    key = ('s', ops[j][4])
                    val = self.slot_cnt[ops[j][4]]
                else:
                    key = ('e', ops[j][0])
                    val = sigval[j]
                if need.get(key, 0) < val:
                    need[key] = val
            for key, val in need.items():
                self._wait(eng, key, val)
            ins = fn(self.eng[eng])
            if dma is not None:
                self.slot_cnt[dma] += 16
                ins.then_inc(self.slot_sem[dma], 16)
            elif signal[i]:
                self.cnt[eng] += 1
                ins.then_inc(self.sem[eng], 1)
                sigval[i] = self.cnt[eng]
        self.stats['n_ops'] += n
        for e in self.eng:
            for ce in self.CE:
                if self.pend[e].get(('e', ce), 0) < self.cnt[ce]:
                    self.pend[e][('e', ce)] = self.cnt[ce]
            for s, v in self.slot_cnt.items():
                if self.pend[e].get(('s', s), 0) < v:
                    self.pend[e][('s', s)] = v

    def finish(self):
        self.flush()
        for key, val in self.pend['sp'].items():
            self._wait('sp', key, val)
        self.pend['sp'] = {}


class Cfg:
    def __init__(self, SP=16384, SS=4096, NOP=4096, NOS=2048, nwp=9, nws=5, CK=16):
        self.SP, self.SS, self.NOP, self.NOS = SP, SS, NOP, NOS
        self.CK = CK
        self.seqs = []
        for (S, NO, nw, tag) in ((SP, NOP, nwp, 'p'), (SS, NOS, nws, 's')):
            so = -(-NO // nw)
            wins = []
            o = 0
            while o < NO:
                no = min(so, NO - o)
                wins.append((o, no))
                o += no
            assert max(w[1] for w in wins) + HL + HR <= 512
            self.seqs.append(dict(S=S, NO=NO, NL=NO + HL + HR, wins=wins, tag=tag))


def vec_layout():
    off = {}
    c = 0
    for name, ncol in (('attn_norm', 8), ('pool_norm', 8), ('pool_scale', 8), ('ffn_norm', 16),
                       ('q_gain', 1), ('k_gain', 1), ('conv_w', 2 * 3 * 44), ('conv_b', 2 * 44)):
        off[name] = c
        c += ncol
    return off, c


def build(cfg):
    nc = bass.Bass("TRN2", target_bir_lowering=False)
    es_all = ExitStack()
    S = Sched(nc, es_all)
    voff, NV = vec_layout()

    def din(name, shape, dt=F32):
        return nc.dram_tensor(name, list(shape), dt, kind="ExternalInput").ap()

    def dscr(name, shape, dt=BF16):
        return nc.dram_tensor(name, list(shape), dt, kind="Internal").ap()

    sq = cfg.seqs
    xkv = [din("xkv_" + s['tag'], [s['S'], D]) for s in sq]
    xq = [din("xq_" + s['tag'], [s['NL'], D]) for s in sq]
    cosk = din("cosk", [128, cfg.SP])
    sink = din("sink", [128, cfg.SP])
    cosq = [din("cosq_" + s['tag'], [128, s['NL']]) for s in sq]
    sinq = [din("sinq_" + s['tag'], [128, s['NL']]) for s in sq]
    maskd = [din("mask_" + s['tag'], [128, s['NL']]) for s in sq]
    consts = din("consts", [128, 3 * 128])
    vecs = din("vecs", [128, NV])
    gbc = din("gbc", [128, D])
    w_qkv = din("w_qkv", [D, 1536])
    w_o = din("w_o", [D, D])
    w_pool = din("w_pool", [4 * 256, 256])
    w_up = din("w_up", [2, D, 2 * DFF])
    w_down = din("w_down", [2, DFF, D])
    yout = [nc.dram_tensor("y_" + s['tag'], [s['NO'], D], F32, kind="ExternalOutput").ap() for s in sq]

    wq_s = dscr("wq_s", [8, 128, 1024])
    wk_s = dscr("wk_s", [2, 128, 1024])
    wv_s = dscr("wv_s", [128, 2048])
    wo_s = dscr("wo_s", [8, 128, 1024])
    wup_s = dscr("wup_s", [2, NFC, 128, 2, 1024])
    wd_s = dscr("wd_s", [2, 8, 128, NFC * 128])
    wp_s = dscr("wp_s", [128, 2048])
    kT_s = [dscr("kT_" + s['tag'], [2, 128, s['S']]) for s in sq]
    v_s = [dscr("v_" + s['tag'], [2, 128, s['S'] // 128, 128]) for s in sq]

    def sb(es, name, shape, dt=F32):
        return es.enter_context(nc.sbuf_tensor(name, list(shape), dt))

    cst = sb(es_all, "cst", [128, 384])
    vec = sb(es_all, "vec", [128, NV])
    onesb = sb(es_all, "onesb", [128, 128], BF16)
    epst = sb(es_all, "epst", [128, 1])
    epsc = epst[:, 0:1]
    identb = sb(es_all, "identb", [128, 128], BF16)
    ident = cst[:, 0:128]
    perm = cst[:, 128:256]
    ones = cst[:, 256:384]
    S.op('sp', lambda e: e.dma_start(out=cst[:], in_=consts[:, :]), w=['cst'], dma='cst')
    S.op('sp', lambda e: e.dma_start(out=vec[:], in_=vecs[:, :]), w=['vec'], dma='vec')
    S.op('dve', lambda e: e.tensor_copy(out=onesb[:], in_=ones), r=['cst'], w=['onesb'])
    S.op('dve', lambda e: e.memset(epst[:], EPS), w=['epsc'])
    S.op('dve', lambda e: e.tensor_copy(out=identb[:], in_=ident), r=['cst'], w=['identb'])

    for l_ in range(2):
        for k_ in range(3):
            c0_ = voff['conv_w'] + l_ * 132 + k_ * 44 + NFC
            S.op('dve', lambda e, c0_=c0_: e.tensor_scalar(out=vec[:, c0_:c0_ + NFC], in0=vec[:, c0_:c0_ + NFC], scalar1=0.5,
                                                           scalar2=None, op0=ALU.mult), r=['vec'], w=['vec'])
        c0_ = voff['conv_b'] + l_ * 44 + NFC
        S.op('dve', lambda e, c0_=c0_: e.tensor_scalar(out=vec[:, c0_:c0_ + NFC], in0=vec[:, c0_:c0_ + NFC], scalar1=0.5,
                                                       scalar2=None, op0=ALU.mult), r=['vec'], w=['vec'])

    def vcol(name, i=0):
        c = voff[name] + i
        return vec[:, c:c + 1]

    PS = {}

    def bank(b):
        return PS['t'][:, b, :]

    def bk(b):
        return ('ps', b)

    es_w = ExitStack()
    STG = 4096
    NST = 3
    stf = [sb(es_w, "stf%d" % i, [128, STG]) for i in range(NST)]
    stb = [sb(es_w, "stb%d" % i, [128, STG], BF16) for i in range(NST)]
    wstate = dict(i=0)
    cast_engs = ['act', 'act', 'act']

    def prep(src, dst, nrc, cw, blocked, dkey=None):
        i = wstate['i']
        wstate['i'] += 1
        f, b = stf[i % NST], stb[i % NST]
        kf, kb = ('stf', i % NST), ('stb', i % NST)
        nel = nrc * cw
        fv = f[:, 0:nel].rearrange("p (c n) -> p c n", c=nrc)
        S.op('sp', lambda e: e.dma_start(out=fv, in_=src), w=[kf], dma=kf)
        if wstate.get('pend'):
            wstate.pop('pend')()
        ce = cast_engs[i % 3]
        if blocked:
            nb = cw // 128
            iv = f[:, 0:nel].rearrange("p (c b j) -> p b c j", c=nrc, b=nb)
            ov = b[:, 0:nel].rearrange("p (b c j) -> p b c j", c=nrc, b=nb)
            dv = b[:, 0:nel].rearrange("p (b x) -> p b x", b=nb)
        else:
            iv = f[:, 0:nel]
            ov = b[:, 0:nel]
            dv = b[:, 0:nel]
        if ce == 'act':
            if blocked:
                for bb in range(nb):
                    S.op('act', lambda e, bb=bb: e.activation(out=ov[:, bb], in_=iv[:, bb], func=AF.Copy),
                         r=[kf], w=[kb])
            else:
                S.op('act', lambda e: e.activation(out=ov, in_=iv, func=AF.Copy), r=[kf], w=[kb])
        else:
            if blocked:
                for bb in range(nb):
                    S.op(ce, lambda e, bb=bb: e.tensor_copy(out=ov[:, bb], in_=iv[:, bb]), r=[kf], w=[kb])
            else:
                S.op(ce, lambda e: e.tensor_copy(out=ov, in_=iv), r=[kf], w=[kb])
        def store_():
            S.op('sp', lambda e: e.dma_start(out=dst, in_=dv), r=[kb], w=([dkey] if dkey else []), dma=('wst', i % NST))
        if dkey:
            store_()
        else:
            wstate['pend'] = store_

    def rows(w2d, nrc):
        return w2d.rearrange("(c p) n -> p c n", p=128)

    prep_steps = []

    def P(*args, **kw):
        prep_steps.append(lambda: prep(*args, **kw))
    qv = rows(w_qkv, 8)
    prep(qv[:, :, 1024:1280], wk_s[:, :, :].rearrange("b p x -> p b x"), 8, 256, True, dkey='wk_s')
    prep(qv[:, :, 1280:1536], wv_s[:, :], 8, 256, False, dkey='wv_s')
    for sl in range(2):
        P(qv[:, :, sl * 512:(sl + 1) * 512], wq_s[sl * 4:(sl + 1) * 4].rearrange("b p x -> p b x"), 8, 512, True)
    ov_ = rows(w_o, 8)
    for sl in range(2):
        P(ov_[:, :, sl * 512:(sl + 1) * 512], wo_s[sl * 4:(sl + 1) * 4].rearrange("b p x -> p b x"), 8, 512, True)
    P(rows(w_pool, 8), wp_s[:, :], 8, 256, False)
    for l in range(2):
        uv = rows(w_up[l], 8)
        for half in range(2):
            for b0 in range(0, NFC, 4):
                nb = min(4, NFC - b0)
                c0 = half * DFF + b0 * 128
                P(uv[:, :, c0:c0 + nb * 128],
                  wup_s[l, b0:b0 + nb, :, half, :].rearrange("b p x -> p b x"), 8, nb * 128, True)
        dv_ = rows(w_down[l], NFC)
        for m in range(8):
            P(dv_[:, :, m * 128:(m + 1) * 128], wd_s[l, m:m + 1].rearrange("b p x -> p b x"), NFC, 128, True)

    def rope_head(T, src_ps_bank, b_ss, b_pm, gain_col, cos_t, sin_t, N, out_bf, out_keys, post_scale, tagk, stages=(0, 1, 2), rope_key=None, aux='pool'):
        kg, sqt, t1, t2, rs = T['kg'], T['sq'], T['t1'], T['t2'], T['rs']
        src = bank(src_ps_bank)[:, 0:N]
        rk_ = rope_key if rope_key is not None else tagk + 'rope'
        if 0 in stages:
            S.op('act', lambda e: e.activation(out=kg[:, 0:N], in_=src, func=AF.Copy, scale=gain_col),
                 r=[bk(src_ps_bank), 'vec'], w=[tagk + 'kg'])
            S.op('act', lambda e: e.activation(out=sqt[:, 0:N], in_=src, func=AF.Square),
                 r=[bk(src_ps_bank)], w=[tagk + 'sq'])
        if 1 in stages:
            S.op('pe', lambda e: e.matmul(bank(b_ss)[:, 0:N], lhsT=ones, rhs=sqt[:, 0:N], start=True, stop=True),
                 r=['cst', tagk + 'sq'], w=[bk(b_ss)])
            S.op('act', lambda e: e.activation(out=rs[:, 0:N], in_=bank(b_ss)[:, 0:N], func=AF.Ln, scale=1.0 / HD, bias=epsc),
                 r=[bk(b_ss), 'epsc'], w=[tagk + 'rs'])
            S.op('act', lambda e: e.activation(out=rs[:, 0:N], in_=rs[:, 0:N], func=AF.Exp, scale=-0.5),
                 r=[tagk + 'rs'], w=[tagk + 'rs'])
        if 2 in stages:
            S.op('pe', lambda e: e.matmul(bank(b_pm)[:, 0:N], lhsT=perm, rhs=kg[:, 0:N], start=True, stop=True),
                 r=['cst', tagk + 'kg'], w=[bk(b_pm)])
            S.op('dve', lambda e: e.tensor_tensor(out=t2[:, 0:N], in0=bank(b_pm)[:, 0:N], in1=sin_t, op=ALU.mult),
                 r=[bk(b_pm), rk_], w=[tagk + 't2'])
            S.op(aux, lambda e: e.tensor_tensor(out=t1[:, 0:N], in0=kg[:, 0:N], in1=cos_t, op=ALU.mult),
                 r=[tagk + 'kg', rk_], w=[tagk + 't1'])
            S.op(aux, lambda e: e.tensor_tensor(out=t1[:, 0:N], in0=t1[:, 0:N], in1=t2[:, 0:N], op=ALU.add),
                 r=[tagk + 't1', tagk + 't2'], w=[tagk + 't1'])
            S.op('dve', lambda e: e.scalar_tensor_tensor(out=out_bf, in0=t1[:, 0:N], scalar=post_scale, in1=rs[:, 0:N],
                                                         op0=ALU.mult, op1=ALU.mult),
                 r=[tagk + 't1', tagk + 'rs'], w=out_keys)

    es_a = es_w
    PS['t'] = es_a.enter_context(nc.psum_tensor("psA", [128, 6, 512], F32))
    psb = es_a.enter_context(nc.psum_tensor("psb", [128, 2, 1024], BF16))
    xa = [sb(es_a, "xa%d" % i, [128, 4, D]) for i in range(2)]
    xn = [sb(es_a, "xn%d" % i, [128, 4, D], BF16) for i in range(2)]
    hTa = [sb(es_a, "hTa%d" % i, [128, 8, 512], BF16) for i in range(2)]
    wk_t = sb(es_a, "wk_t", [128, 2, 8, 128], BF16)
    wv_t = sb(es_a, "wv_t", [128, 8, 256], BF16)
    gbc_t = sb(es_a, "gbc_t", [128, D])
    ropeA = [sb(es_a, "ropeA%d" % i, [128, 2, 512]) for i in range(2)]
    TA2 = [{k: sb(es_a, "TA%d_" % g_ + k, [128, 512]) for k in ('kg', 'sq', 't1', 't2', 'rs')} for g_ in range(2)]
    ssa = [sb(es_a, "ssa%d" % i, [128, 8]) for i in range(2)]
    kout = [sb(es_a, "kout%d" % i, [128, 512], BF16) for i in range(4)]
    vout = [sb(es_a, "vout%d" % i, [128, 4, 256], BF16) for i in range(2)]

    S.op('sp', lambda e: e.dma_start(out=wk_t[:].rearrange("p a c j -> p a (c j)"),
                                     in_=wk_s[:, :, :].rearrange("a p x -> p a x")), r=['wk_s'], w=['wk_t'], dma='wk_t')
    S.op('sp', lambda e: e.dma_start(out=wv_t[:].rearrange("p c j -> p (c j)"), in_=wv_s[:, :]),
         r=['wv_s'], w=['wv_t'], dma='wv_t')
    S.op('sp', lambda e: e.dma_start(out=gbc_t[:], in_=gbc[:, :]), w=['gbc_t'], dma='gbc_t')

    tiles = [(si, ti) for si, s in enumerate(sq) for ti in range(s['S'] // 512)]

    def frontA(t):
        si, ti = tiles[t]
        t0 = ti * 512
        pb = t % 2
        xat, xnt, hTt, rp, sst = xa[pb], xn[pb], hTa[pb], ropeA[pb], ssa[pb]
        kx = ('xa', pb)
        S.op('sp', lambda e: e.dma_start(out=xat[:], in_=xkv[si][t0:t0 + 512, :].rearrange("(c p) d -> p c d", p=128)),
             w=[kx], dma=kx)
        kr = ('ropeA', pb)
        S.op('sp', lambda e: e.dma_start(out=rp[:, 0, :], in_=cosk[:, t0:t0 + 512]), w=[kr], dma=kr)
        S.op('sp', lambda e: e.dma_start(out=rp[:, 1, :], in_=sink[:, t0:t0 + 512]), w=[kr], dma=kr)
        for c in range(4):
            S.op('act', lambda e, c=c: e.activation(out=xnt[:, c, :], in_=xat[:, c, :], func=AF.Square,
                                                    accum_out=sst[:, c:c + 1]),
                 r=[kx], w=[('ssa', pb, c), ('xn', pb, c)])
        S.op('act', lambda e: e.activation(out=sst[:, 4:8], in_=sst[:, 0:4], func=AF.Ln, scale=1.0 / D, bias=epsc),
             r=[('ssa', pb, c) for c in range(4)] + ['epsc'], w=[('ssa2', pb)])
        S.op('act', lambda e: e.activation(out=sst[:, 4:8], in_=sst[:, 4:8], func=AF.Exp, scale=-0.5),
             r=[('ssa2', pb)], w=[('ssa2', pb)])
        for c in range(4):
            S.op('dve', lambda e, c=c: e.scalar_tensor_tensor(
                out=xnt[:, c, :], in0=xat[:, c, :], scalar=sst[:, 4 + c:5 + c], in1=gbc_t[:],
                op0=ALU.mult, op1=ALU.mult), r=[kx, ('ssa2', pb), 'gbc_t'], w=[('xn', pb, c)])
        for f in range(8):
            for c in range(4):
                S.op('pe', lambda e, f=f, c=c: e.transpose(psb[:, f % 2, c * 128:(c + 1) * 128],
                                                           xnt[:, c, f * 128:(f + 1) * 128], identb[:]),
                     r=[('xn', pb, c), 'identb'], w=[('psb', f % 2)])
            if f % 2 == 0:
                S.op('act', lambda e, f=f: e.activation(out=hTt[:, f, :], in_=psb[:, f % 2, 0:512], func=AF.Copy),
                     r=[('psb', f % 2)], w=[('hTa', pb, f)])
            else:
                S.op('dve', lambda e, f=f: e.tensor_copy(out=hTt[:, f, :], in_=psb[:, f % 2, 0:512]),
                     r=[('psb', f % 2)], w=[('hTa', pb, f)])

    def backA(t):
        si, ti = tiles[t]
        t0 = ti * 512
        pb = t % 2
        hTt, rp = hTa[pb], ropeA[pb]
        vo = vout[pb]
        vok = ('vout', pb)

        def vpart(cs):
            for c in cs:
                pbv = 2 + c % 2
                for f in range(8):
                    S.op('pe', lambda e, c=c, f=f, pbv=pbv: e.matmul(bank(pbv)[:, 0:256], lhsT=hTt[:, f, c * 128:(c + 1) * 128],
                                                                     rhs=wv_t[:, f, :], start=(f == 0), stop=(f == 7)),
                         r=['wv_t', ('hTa', pb, f)], w=[bk(pbv)])
                if c % 2 == 0:
                    S.op('act', lambda e, c=c, pbv=pbv: e.activation(out=vo[:, c, :], in_=bank(pbv)[:, 0:256], func=AF.Copy),
                         r=[bk(pbv)], w=[vok])
                else:
                    S.op('dve', lambda e, c=c, pbv=pbv: e.tensor_copy(out=vo[:, c, :], in_=bank(pbv)[:, 0:256]),
                         r=[bk(pbv)], w=[vok])

        def rope(g, stages):
            ko = kout[(t * 2 + g) % 4]
            kok = ('kout', (t * 2 + g) % 4)
            rope_head(TA2[g], g, 4 + g, 4 + g, vcol('k_gain'), rp[:, 0, :], rp[:, 1, :], 512, ko[:], [kok], 1.0, 'A%d' % g,
                      stages=stages, rope_key=('ropeA', pb), aux='dve')
            if 2 in stages:
                S.op('sp', lambda e: e.dma_start(out=kT_s[si][g, :, t0:t0 + 512], in_=ko[:]), r=[kok], dma=kok)

        for g in range(NKV):
            for f in range(8):
                S.op('pe', lambda e, g=g, f=f: e.matmul(bank(g)[:, :], lhsT=wk_t[:, g, f, :], rhs=hTt[:, f, :],
                                                        start=(f == 0), stop=(f == 7)),
                     r=['wk_t', ('hTa', pb, f)], w=[bk(g)])
        for g in range(NKV):
            rope(g, (0,))
        vpart((0, 1))
        for g in range(NKV):
            rope(g, (1,))
        vpart((2, 3))
        for g in range(NKV):
            rope(g, (2,))
        for g in range(NKV):
            S.op('sp', lambda e, g=g: e.dma_start(out=v_s[si][g, :, ti * 4:(ti + 1) * 4, :], in_=vo[:, :, g * 128:(g + 1) * 128]),
                 r=[vok], dma=vok)

    frontA(0)
    per_tile = -(-len(prep_steps) // max(1, len(tiles) - 4))
    for t in range(len(tiles)):
        if t + 1 < len(tiles):
            frontA(t + 1)
        for _ in range(per_tile):
            if prep_steps:
                prep_steps.pop(0)()
        backA(t)
    while prep_steps:
        prep_steps.pop(0)()
    if wstate.get('pend'):
        wstate.pop('pend')()
    S.flush()
    es_a.close()

    es_b = ExitStack()
    PS['t'] = es_b.enter_context(nc.psum_tensor("psB", [128, 8, 512], F32))
    xT = sb(es_b, "xT", [128, 8, 512])
    hT = sb(es_b, "hT", [128, 8, 512], BF16)
    hF = sb(es_b, "hF", [128, 512])
    oT = sb(es_b, "oT", [128, 8, 512], BF16)
    aT = sb(es_b, "aT", [128, NFC, 512], BF16)
    plT = oT
    xw = [sb(es_b, "xw%d" % i, [128, D]) for i in range(2)]
    qT = [sb(es_b, "qT%d" % i, [128, 512], BF16) for i in range(2)]
    NPT = 5
    pT = [sb(es_b, "pT%d" % i, [128, 1024], BF16) for i in range(NPT)]
    RK = 3
    kring = [sb(es_b, "kr%d" % i, [128, cfg.CK * 128], BF16) for i in range(RK)]
    vring = [sb(es_b, "vr%d" % i, [128, cfg.CK, 128], BF16) for i in range(RK)]
    TB = {k: sb(es_b, "TB_" + k, [128, 512]) for k in ('kg', 'sq', 't1', 't2', 'rs')}
    ropeB = sb(es_b, "ropeB", [128, 2, 512])
    maskt = sb(es_b, "maskt", [128, 512])
    icnt = sb(es_b, "icnt", [128, 4, 512])
    rstd = sb(es_b, "rstd", [128, 512])
    osb = sb(es_b, "osb", [128, 512])
    lsb = sb(es_b, "lsb", [128, 512])
    tmpr = [sb(es_b, "tmpr%d" % i, [128, 512]) for i in range(2)]
    sqn = tmpr
    ffA = [sb(es_b, "ffA%d" % i, [128, 2, 512]) for i in range(4)]
    ffAf = [t_[:].rearrange("p a n -> p (a n)") for t_ in ffA]
    cg = [ffA[0][:, p_, :] for p_ in range(2)]
    cv = [ffA[1][:, p_, :] for p_ in range(2)]
    th = [ffA[2][:, p_, :] for p_ in range(2)]
    a0 = [ffA[3][:, p_, :] for p_ in range(2)]
    pa = [TB['kg'], TB['sq'], TB['t1']]
    wq_t = [sb(es_b, "wq_t%d" % i, [128, 8, 128], BF16) for i in range(2)]
    wo_t = [sb(es_b, "wo_t%d" % i, [128, 8, 128], BF16) for i in range(2)]
    wup_t = [sb(es_b, "wup_t%d" % i, [128, 2, 8, 128], BF16) for i in range(4)]
    wd_t = [sb(es_b, "wd_t%d" % i, [128, NFC, 128], BF16) for i in range(2)]
    wp_t = sb(es_b, "wp_t", [128, 8, 256], BF16)
    S.op('sp', lambda e: e.dma_start(out=wp_t[:].rearrange("p c j -> p (c j)"), in_=wp_s[:, :]), w=['wp_t'], dma='wp_t')

    cnt = dict(wq=0, wo=0, wup=0, wd=0, xw=0, kv=0, q=0, p=0, sqn=0, tmpr=0, ff=0)

    def fm_norm(N, gname, gi, out_fn, out_keys_fn, chunks=range(8)):
        for c in range(8):
            i = cnt['sqn'] % 2
            cnt['sqn'] += 1
            S.op('act', lambda e, c=c, i=i: e.activation(out=sqn[i][:, 0:N], in_=xT[:, c, 0:N], func=AF.Square),
                 r=[('xT', c)], w=[('tmpr', i)])
            S.op('pe', lambda e, c=c, i=i: e.matmul(bank(6)[:, 0:N], lhsT=ones, rhs=sqn[i][:, 0:N],
                                                    start=(c == 0), stop=(c == 7)), r=['cst', ('tmpr', i)], w=[bk(6)])
        S.op('act', lambda e: e.activation(out=rstd[:, 0:N], in_=bank(6)[:, 0:N], func=AF.Ln, scale=1.0 / D, bias=epsc),
             r=[bk(6), 'epsc'], w=['rstd'])
        S.op('act', lambda e: e.activation(out=rstd[:, 0:N], in_=rstd[:, 0:N], func=AF.Exp, scale=-0.5), r=['rstd'], w=['rstd'])

    def norm_chunk(N, c, gname, gi, out_ap, out_keys):
        S.op('dve', lambda e: e.scalar_tensor_tensor(out=out_ap, in0=xT[:, c, 0:N], scalar=vcol(gname, gi * 8 + c),
                                                     in1=rstd[:, 0:N], op0=ALU.mult, op1=ALU.mult),
             r=[('xT', c), 'rstd', 'vec'], w=out_keys)

    def resid_add(N, m, psbank, scale_col):
        i = cnt['tmpr'] % 2
        cnt['tmpr'] += 1
        t = tmpr[i]
        S.op('dve', lambda e: e.scalar_tensor_tensor(out=t[:, 0:N], in0=bank(psbank)[:, 0:N],
                                                     scalar=(1.0 if scale_col is None else scale_col),
                                                     in1=maskt[:, 0:N], op0=ALU.mult, op1=ALU.mult),
             r=[bk(psbank), 'maskt', 'vec'], w=[('tmpr', i)])
        S.op('pool', lambda e: e.tensor_tensor(out=xT[:, m, 0:N], in0=xT[:, m, 0:N], in1=t[:, 0:N], op=ALU.add),
             r=[('xT', m), ('tmpr', i)], w=[('xT', m)])

    def ffn(N, l):
        fm_norm(N, 'ffn_norm', l, None, None)
        for c in range(8):
            norm_chunk(N, c, 'ffn_norm', l, hT[:, c, 0:N], [('hT', c)])

        def load_wup(j):
            i = cnt['wup'] % 4
            cnt['wup'] += 1
            S.op('sp', lambda e: e.dma_start(out=wup_t[i][:].rearrange("p a c j -> p (a c j)"),
                                             in_=wup_s[l, j].rearrange("p a x -> p (a x)")),
                 w=[('wup', i)], dma=('wup', i))
            return i
        slots = {}
        for j in range(min(3, NFC)):
            slots[j] = load_wup(j)
        cwb = voff['conv_w'] + l * 3 * 44
        cbb = voff['conv_b'] + l * 44
        for j in range(NFC):
            if j + 3 < NFC:
                slots[j + 3] = load_wup(j + 3)
            wi = slots[j]
            par = cnt['ff'] % 2
            cnt['ff'] += 1
            bg, bv = (2, 3) if par == 0 else (4, 0)
            for half, bnk in ((0, bg), (1, bv)):
                for c in range(8):
                    S.op('pe', lambda e, half=half, bnk=bnk, c=c, wi=wi: e.matmul(
                        bank(bnk)[:, 0:N], lhsT=wup_t[wi][:, half, c, :], rhs=hT[:, c, 0:N],
                        start=(c == 0), stop=(c == 7)), r=[('wup', wi), ('hT', c)], w=[bk(bnk)])

            def taps(half):
                ch = half * NFC + j
                return [vec[:, cwb + k_ * 44 + ch: cwb + k_ * 44 + ch + 1] for k_ in range(3)] + \
                       [vec[:, cbb + ch: cbb + ch + 1]]
            kg_, kv_, kt_, ka_ = ('ffA', 0, par), ('ffA', 1, par), ('ffA', 2, par), ('ffA', 3, par)
            cgt, cvt, tht, a0t = cg[par], cv[par], th[par], a0[par]
            w0, w1, w2, bb = taps(0)
            S.op('act', lambda e: e.activation(out=cgt[:, 0:N], in_=bank(bg)[:, 0:N], func=AF.Identity, scale=w1, bias=bb),
                 r=[bk(bg), 'vec'], w=[kg_])
            S.op('act', lambda e: e.activation(out=a0t[:, 1:N], in_=bank(bg)[:, 0:N - 1], func=AF.Copy, scale=w0),
                 r=[bk(bg), 'vec'], w=[ka_])
            S.op('dve', lambda e: e.scalar_tensor_tensor(out=cgt[:, 0:N - 1], in0=bank(bg)[:, 1:N], scalar=w2,
                                                         in1=cgt[:, 0:N - 1], op0=ALU.mult, op1=ALU.add),
                 r=[bk(bg), kg_, 'vec'], w=[kg_])
            S.op('pool', lambda e: e.tensor_tensor(out=cgt[:, 1:N], in0=cgt[:, 1:N], in1=a0t[:, 1:N], op=ALU.add),
                 r=[kg_, ka_], w=[kg_])
            w0, w1, w2, bb = taps(1)
            S.op('act', lambda e: e.activation(out=cvt[:, 0:N], in_=bank(bv)[:, 0:N], func=AF.Identity, scale=w1, bias=bb),
                 r=[bk(bv), 'vec'], w=[kv_])
            S.op('dve', lambda e: e.scalar_tensor_tensor(out=cvt[:, 1:N], in0=bank(bv)[:, 0:N - 1], scalar=w0,
                                                         in1=cvt[:, 1:N], op0=ALU.mult, op1=ALU.add),
                 r=[bk(bv), kv_, 'vec'], w=[kv_])
            S.op('dve', lambda e: e.scalar_tensor_tensor(out=cvt[:, 0:N - 1], in0=bank(bv)[:, 1:N], scalar=w2,
                                                         in1=cvt[:, 0:N - 1], op0=ALU.mult, op1=ALU.add),
                 r=[bk(bv), kv_, 'vec'], w=[kv_])
            S.op('act', lambda e: e.activation(out=tht[:, 0:N], in_=cgt[:, 0:N], func=AF.Tanh, scale=0.5),
                 r=[kg_], w=[kt_])
            S.op('pool', lambda e: e.tensor_tensor(out=cvt[:, 0:N], in0=cvt[:, 0:N], in1=cgt[:, 0:N], op=ALU.mult),
                 r=[kg_, kv_], w=[kv_])
            S.op('dve', lambda e, j=j: e.scalar_tensor_tensor(out=aT[:, j, 0:N], in0=tht[:, 0:N], scalar=1.0, in1=cvt[:, 0:N],
                                                              op0=ALU.add, op1=ALU.mult),
                 r=[kt_, kv_], w=[('aT', j)])
        def load_wd(m):
            i = cnt['wd'] % 2
            cnt['wd'] += 1
            S.op('sp', lambda e: e.dma_start(out=wd_t[i][:].rearrange("p j x -> p (j x)"), in_=wd_s[l, m]),
                 w=[('wd', i)], dma=('wd', i))
            return i
        dsl = {0: load_wd(0)}
        for m in range(8):
            if m + 1 < 8:
                dsl[m + 1] = load_wd(m + 1)
            wi = dsl[m]
            bnk = 5 if m % 2 == 0 else 1
            for j in range(NFC):
                S.op('pe', lambda e, j=j, wi=wi, bnk=bnk: e.matmul(bank(bnk)[:, 0:N], lhsT=wd_t[wi][:, j, :], rhs=aT[:, j, 0:N],
                                                                   start=(j == 0), stop=(j == NFC - 1)),
                     r=[('wd', wi), ('aT', j)], w=[bk(bnk)])
            resid_add(N, m, bnk, None)

    for si, s in enumerate(sq):
        nkt = s['S'] // 128
        CK = min(cfg.CK, nkt)
        nck = nkt // CK
        jobs = [(wi_, h, ci) for wi_ in range(len(s['wins'])) for h in range(NH) for ci in range(nck)]
        jslot = {}

        def load_kv(jn, si=si, CK=CK, jobs=jobs, jslot=jslot):
            if jn >= len(jobs) or jn in jslot:
                return
            (_, h, ci) = jobs[jn]
            g = h // (NH // NKV)
            i = cnt['kv'] % RK
            cnt['kv'] += 1
            jslot[jn] = i
            S.op('sp', lambda e: e.dma_start(out=kring[i][:, 0:CK * 128], in_=kT_s[si][g, :, ci * CK * 128:(ci + 1) * CK * 128]),
                 w=[('kv', i)], dma=('kv', i))
            S.op('sp', lambda e: e.dma_start(out=vring[i][:, 0:CK, :], in_=v_s[si][g, :, ci * CK:(ci + 1) * CK, :]),
                 w=[('kv', i)], dma=('kv', i))

        for wi_, (o0, no) in enumerate(s['wins']):
            N = no + HL + HR
            r0 = o0
            ntc = -(-N // 128)
            S.op('sp', lambda e, r0=r0, N=N, si=si: e.dma_start(out=maskt[:, 0:N], in_=maskd[si][:, r0:r0 + N]),
                 w=['maskt'], dma='maskt')
            S.op('sp', lambda e, r0=r0, N=N, si=si: e.dma_start(out=ropeB[:, 0, 0:N], in_=cosq[si][:, r0:r0 + N]),
                 w=['Brope'], dma='ropeB')
            S.op('sp', lambda e, r0=r0, N=N, si=si: e.dma_start(out=ropeB[:, 1, 0:N], in_=sinq[si][:, r0:r0 + N]),
                 w=['Brope'], dma='ropeB')
            for tc in range(ntc):
                nt = min(128, N - tc * 128)
                i = cnt['xw'] % 2
                cnt['xw'] += 1
                S.op('sp', lambda e, i=i, tc=tc, nt=nt, r0=r0, si=si: e.dma_start(
                    out=xw[i][0:nt, :], in_=xq[si][r0 + tc * 128: r0 + tc * 128 + nt, :]), w=[('xw', i)], dma=('xw', i))
                for f in range(8):
                    bnk = f % 4
                    S.op('pe', lambda e, i=i, f=f, tc=tc, nt=nt, bnk=bnk: e.transpose(
                        bank(bnk)[:, tc * 128: tc * 128 + nt], xw[i][0:nt, f * 128:(f + 1) * 128], ident[0:nt, 0:nt]),
                        r=[('xw', i), 'cst'], w=[bk(bnk)])
                    if f % 2 == 0:
                        S.op('act', lambda e, f=f, tc=tc, nt=nt, bnk=bnk: e.activation(
                            out=xT[:, f, tc * 128: tc * 128 + nt], in_=bank(bnk)[:, tc * 128: tc * 128 + nt], func=AF.Copy),
                            r=[bk(bnk)], w=[('xT', f)])
                    else:
                        S.op('dve', lambda e, f=f, tc=tc, nt=nt, bnk=bnk: e.tensor_copy(
                            out=xT[:, f, tc * 128: tc * 128 + nt], in_=bank(bnk)[:, tc * 128: tc * 128 + nt]),
                            r=[bk(bnk)], w=[('xT', f)])

            def window_sums(cur, ck, g, out_ap, out_keys, eng, temps):
                steps = [(1, 0), (1, 1), (2, 2), (4, 4)]
                lo, hi = 0, N
                for lvl in range(g + 1):
                    sl, sr = steps[lvl]
                    dst, dk = (out_ap, out_keys) if lvl == g else temps[lvl]
                    nlo, nhi = lo + sl, hi - sr
                    S.op(eng, lambda e, dst=dst, cur=cur, nlo=nlo, nhi=nhi, sl=sl, sr=sr: e.tensor_tensor(
                        out=dst[:, nlo:nhi], in0=cur[:, nlo - sl:nhi - sl], in1=cur[:, nlo + sr:nhi + sr], op=ALU.add),
                        r=ck, w=dk)
                    cur, ck, lo, hi = dst, dk, nlo, nhi

            DVE_T = [(TB['kg'], ['Bkg']), (TB['sq'], ['Bsq']), (TB['t1'], ['Bt1'])]
            POOL_T = [(ffA[0][:, 0, :], [('ffA', 0, 0)]), (ffA[0][:, 1, :], [('ffA', 0, 1)]), (ffA[1][:, 0, :], [('ffA', 1, 0)])]
            for g in range(4):
                S.op('pool', lambda e, g=g: e.tensor_copy(out=icnt[:, g, 0:N], in_=maskt[:, 0:N]), r=['maskt'], w=[('icnt', g)])
                window_sums(maskt, ['maskt'], g, icnt[:, g, :], [('icnt', g)], 'pool', POOL_T)
                S.op('dve', lambda e, g=g: e.tensor_scalar(out=icnt[:, g, 0:N], in0=icnt[:, g, 0:N], scalar1=1.0, scalar2=None,
                                                           op0=ALU.max), r=[('icnt', g)], w=[('icnt', g)])
                S.op('dve', lambda e, g=g: e.reciprocal(out=icnt[:, g, 0:N], in_=icnt[:, g, 0:N]),
                     r=[('icnt', g)], w=[('icnt', g)])
            fm_norm(N, 'attn_norm', 0, None, None)
            for c in range(8):
                norm_chunk(N, c, 'attn_norm', 0, hT[:, c, 0:N], [('hT', c)])

            def load_wq(h):
                i = cnt['wq'] % 2
                cnt['wq'] += 1
                S.op('sp', lambda e: e.dma_start(out=wq_t[i][:].rearrange("p c j -> p (c j)"), in_=wq_s[h]),
                     w=[('wq', i)], dma=('wq', i))
                return i

            def qprep_a(h, wslot):
                for c in range(8):
                    S.op('pe', lambda e, c=c: e.matmul(bank(5)[:, 0:N], lhsT=wq_t[wslot][:, c, :], rhs=hT[:, c, 0:N],
                                                       start=(c == 0), stop=(c == 7)),
                         r=[('wq', wslot), ('hT', c)], w=[bk(5)])

            def qprep_b(h, stages=(0, 1, 2), qi=None):
                if qi is None:
                    qi = cnt['q'] % 2
                    cnt['q'] += 1
                xb_ = 6 + (h % 2)
                rope_head(TB, 5, xb_, xb_, vcol('q_gain'), ropeB[:, 0, 0:N], ropeB[:, 1, 0:N], N, qT[qi][:, 0:N],
                          [('qT', qi)], float(HD) ** -0.5, 'B', stages=stages)
                return qi

            base_job = wi_ * NH * nck
            if wi_ == 0:
                load_kv(0)
                load_kv(1)
            wsl = load_wq(0)
            qprep_a(0, wsl)
            qcur = qprep_b(0)
            npair = nkt // 2
            ROWSUM_PAT = (('pe', 0), ('dve', 1), ('pe', 0), ('dve', 2))
            ACCK = [[('ffA', k_, 0), ('ffA', k_, 1)] for k_ in range(3)]

            def finish_head(hh):
                lb = 6 + (hh % 2)
                S.op('pe', lambda e: e.matmul(bank(lb)[:, 0:N], lhsT=ones, rhs=lsb[:, 0:N], start=False, stop=True),
                     r=['cst', 'lsb'], w=[bk(lb)])
                S.op('dve', lambda e: e.reciprocal(out=rstd[:, 0:N], in_=bank(lb)[:, 0:N]), r=[bk(lb)], w=['rstd'])
                S.op('pool', lambda e: e.tensor_tensor(out=oT[:, hh, 0:N], in0=osb[:, 0:N], in1=rstd[:, 0:N], op=ALU.mult),
                     r=['osb', 'rstd'], w=[('oT', hh)])

            NPS = 2 * NPT
            LOOK = 3
            for h in range(NH):
                if h + 1 < NH:
                    wsl_n = load_wq(h + 1)
                p_of = {}
                accinit = [False, False, False]
                lb_ = 6 + (h % 2)

                def pslot(s_):
                    return pT[s_ // 2][:, (s_ % 2) * 512:(s_ % 2) * 512 + N]

                def emit_qk(j, h=h, qcur=qcur):
                    sbk = 1 + (j % 4)
                    ci, jj = divmod(j, CK)
                    slot = jslot[base_job + h * nck + ci]
                    S.op('pe', lambda e: e.matmul(bank(sbk)[:, 0:N], lhsT=kring[slot][:, jj * 128:(jj + 1) * 128],
                                                  rhs=qT[qcur][:, 0:N], start=True, stop=True),
                         r=[('kv', slot), ('qT', qcur)], w=[bk(sbk)])
                    ps_ = cnt['p'] % NPS
                    cnt['p'] += 1
                    p_of[j] = ps_
                    S.op('act', lambda e: e.activation(out=pslot(ps_), in_=bank(sbk)[:, 0:N], func=AF.Exp),
                         r=[bk(sbk)], w=[('pT', ps_)])

                def emit_pv(j, h=h):
                    ps_ = p_of[j]
                    ci, jj = divmod(j, CK)
                    slot = jslot[base_job + h * nck + ci]
                    S.op('pe', lambda e: e.matmul(bank(0)[:, 0:N], lhsT=vring[slot][:, jj, :], rhs=pslot(ps_),
                                                  start=(j == 0), stop=(j == nkt - 1)),
                         r=[('kv', slot), ('pT', ps_)], w=[bk(0)])
                    eng_, k_ = ROWSUM_PAT[j % len(ROWSUM_PAT)]
                    if eng_ == 'pe':
                        S.op('pe', lambda e: e.matmul(bank(lb_)[:, 0:N], lhsT=onesb[:], rhs=pslot(ps_),
                                                      start=(j == 0), stop=False),
                             r=['onesb', ('pT', ps_)], w=[bk(lb_)])
                        return
                    acc = ffAf[k_]
                    if not accinit[k_]:
                        accinit[k_] = True
                        S.op(eng_, lambda e: e.tensor_copy(out=acc[:, 0:N], in_=pslot(ps_)), r=[('pT', ps_)], w=ACCK[k_])
                    else:
                        S.op(eng_, lambda e: e.tensor_tensor(out=acc[:, 0:N], in0=acc[:, 0:N], in1=pslot(ps_), op=ALU.add),
                             r=[('pT', ps_)] + ACCK[k_], w=ACCK[k_])

                for j in range(min(LOOK, nkt)):
                    cn = j // CK
                    if (base_job + h * nck + cn) not in jslot:
                        load_kv(base_job + h * nck + cn)
                    emit_qk(j)
                for j in range(nkt):
                    ci, jj = divmod(j, CK)
                    if jj == 0:
                        load_kv(base_job + h * nck + ci + 1)
                        load_kv(base_job + h * nck + ci + 2)
                    if j + LOOK < nkt:
                        cn = (j + LOOK) // CK
                        if (base_job + h * nck + cn) not in jslot:
                            load_kv(base_job + h * nck + cn)
                        emit_qk(j + LOOK)
                    emit_pv(j)
                    if j == 0 and h + 1 < NH:
                        qprep_a(h + 1, wsl_n)
                        qnext = qprep_b(h + 1, stages=(0,))
                    if j == min(8, nkt - 3) and h > 0:
                        finish_head(h - 1)
                    if j == min(14, nkt - 2) and h + 1 < NH:
                        qprep_b(h + 1, stages=(1,), qi=qnext)
                    if j == min(20, nkt - 1) and h + 1 < NH:
                        qprep_b(h + 1, stages=(2,), qi=qnext)
                if h + 1 < NH:
                    if (base_job + (h + 1) * nck) not in jslot:
                        load_kv(base_job + (h + 1) * nck)
                S.op('act', lambda e: e.activation(out=osb[:, 0:N], in_=bank(0)[:, 0:N], func=AF.Copy), r=[bk(0)], w=['osb'])
                used = [k_ for k_ in (1, 2) if accinit[k_]]
                assert used
                if len(used) == 2:
                    S.op('dve', lambda e: e.tensor_tensor(out=lsb[:, 0:N], in0=ffAf[1][:, 0:N], in1=ffAf[2][:, 0:N], op=ALU.add),
                         r=ACCK[1] + ACCK[2], w=['lsb'])
                else:
                    u0 = used[0]
                    S.op('dve', lambda e: e.tensor_copy(out=lsb[:, 0:N], in_=ffAf[u0][:, 0:N]), r=ACCK[u0], w=['lsb'])
                if h + 1 < NH:
                    qcur = qnext
            finish_head(NH - 1)
            nb_ = (wi_ + 1) * NH * nck
            load_kv(nb_)
            load_kv(nb_ + 1)
            def load_wo(m):
                i = cnt['wo'] % 2
                cnt['wo'] += 1
                S.op('sp', lambda e: e.dma_start(out=wo_t[i][:].rearrange("p c j -> p (c j)"), in_=wo_s[m]),
                     w=[('wo', i)], dma=('wo', i))
                return i
            osl = {0: load_wo(0)}
            for m in range(8):
                if m + 1 < 8:
                    osl[m + 1] = load_wo(m + 1)
                wsl = osl[m]
                bnk = 5 if m % 2 == 0 else 7
                for h in range(NH):
                    S.op('pe', lambda e, h=h, wsl=wsl, bnk=bnk: e.matmul(bank(bnk)[:, 0:N], lhsT=wo_t[wsl][:, h, :], rhs=oT[:, h, 0:N],
                                                                         start=(h == 0), stop=(h == NH - 1)),
                         r=[('wo', wsl), ('oT', h)], w=[bk(bnk)])
                resid_add(N, m, bnk, None)
            ffn(N, 0)
            def shifted_add(eng, out_t, in_t, sh, keys_r, keys_w, first):
                pass
            fm_norm(N, 'pool_norm', 0, None, None)
            for c in (1, 0, 3, 2, 5, 4, 6, 7):
                g = c // 2
                on_pool = c in (1, 3, 5)
                eng_ = 'pool' if on_pool else 'dve'
                hbuf, hk = (ffA[2][:, 0, :], [('ffA', 2, 0)]) if on_pool else (hF, ['hF'])
                res, rk = (ffA[1][:, 1, :], [('ffA', 1, 1)]) if on_pool else (tmpr[0], [('tmpr', 0)])
                norm_chunk(N, c, 'pool_norm', 0, hbuf[:, 0:N], hk)
                window_sums(hbuf, hk, g, res, rk, eng_, POOL_T if on_pool else DVE_T)
                S.op(eng_, lambda e, g=g, res=res: e.tensor_tensor(out=res[:, 0:N], in0=res[:, 0:N], in1=icnt[:, g, 0:N], op=ALU.mult),
                     r=rk + [('icnt', g)], w=rk)
                S.op(eng_, lambda e, c=c, res=res, hbuf=hbuf: e.tensor_tensor(out=plT[:, c, 0:N], in0=res[:, 0:N], in1=hbuf[:, 0:N],
                                                                              op=ALU.subtract), r=rk + hk, w=[('oT', c)])
            for m in range(8):
                g = m // 2
                mo = m % 2
                bnk = 5 if m % 2 == 0 else 7
                for cc in range(2):
                    S.op('pe', lambda e, g=g, mo=mo, cc=cc, bnk=bnk: e.matmul(
                        bank(bnk)[:, 0:N], lhsT=wp_t[:, 2 * g + cc, mo * 128:(mo + 1) * 128], rhs=plT[:, 2 * g + cc, 0:N],
                        start=(cc == 0), stop=(cc == 1)), r=['wp_t', ('oT', 2 * g + cc)], w=[bk(bnk)])
                resid_add(N, m, bnk, vcol('pool_scale', m))
            ffn(N, 1)
            c0 = HL
            while c0 < HL + no:
                nt = min(128, HL + no - c0)
                i = cnt['xw'] % 2
                cnt['xw'] += 1
                for f in range(8):
                    bnk = 2 + (f // 4)
                    S.op('pe', lambda e, f=f, c0=c0, nt=nt, bnk=bnk: e.transpose(
                        bank(bnk)[0:nt, (f % 4) * 128:(f % 4 + 1) * 128], xT[:, f, c0:c0 + nt], ident),
                        r=[('xT', f), 'cst'], w=[bk(bnk)])
                    if f % 4 == 3:
                        hh = f // 4
                        if hh == 0:
                            S.op('act', lambda e, i=i, nt=nt, bnk=bnk, hh=hh: e.activation(
                                out=xw[i][0:nt, hh * 512:(hh + 1) * 512], in_=bank(bnk)[0:nt, :], func=AF.Copy),
                                r=[bk(bnk)], w=[('xw', i)])
                        else:
                            S.op('dve', lambda e, i=i, nt=nt, bnk=bnk, hh=hh: e.tensor_copy(
                                out=xw[i][0:nt, hh * 512:(hh + 1) * 512], in_=bank(bnk)[0:nt, :]),
                                r=[bk(bnk)], w=[('xw', i)])
                orow = o0 + (c0 - HL)
                S.op('sp', lambda e, i=i, nt=nt, orow=orow, si=si: e.dma_start(out=yout[si][orow:orow + nt, :], in_=xw[i][0:nt, :]),
                     r=[('xw', i)], dma=('xw', i))
                c0 += nt

    S.finish()
    es_b.close()
    es_all.close()
    return nc, S


def rope_tables(pos):
    pos = np.asarray(pos, dtype=np.int64)
    row = (pos // GRID_W).astype(np.float32)
    col = (pos % GRID_W).astype(np.float32)
    F = 32
    inv = (np.float32(10000.0) ** (-(np.arange(F, dtype=np.float32) / np.float32(F)))).astype(np.float32)
    ang_r = (row[None, :] * inv[:, None]).astype(np.float32)
    ang_c = (col[None, :] * inv[:, None]).astype(np.float32)
    cos = np.empty((128, len(pos)), np.float32)
    sin = np.empty((128, len(pos)), np.float32)
    for a, ang in ((0, ang_r), (1, ang_c)):
        c = np.cos(ang).astype(np.float32)
        s = np.sin(ang).astype(np.float32)
        cos[a * 64:a * 64 + 32] = c
        cos[a * 64 + 32:a * 64 + 64] = c
        sin[a * 64:a * 64 + 32] = -s
        sin[a * 64 + 32:a * 64 + 64] = s
    return cos, sin


def const_mats():
    ident = np.eye(128, dtype=np.float32)
    perm = np.zeros((128, 128), np.float32)
    for d in range(128):
        h = (d % 64) // 32
        partner = d + 32 if h == 0 else d - 32
        perm[partner, d] = 1.0
    ones = np.ones((128, 128), np.float32)
    return np.ascontiguousarray(np.concatenate([ident, perm, ones], axis=1))


def pack_vecs(inp):
    voff, NV = vec_layout()
    v = np.zeros((128, NV), np.float32)
    def cols(a):
        return np.asarray(a, np.float32).reshape(-1, 128).T
    v[:, voff['attn_norm']:voff['attn_norm'] + 8] = cols(inp['attn_norm'][0])
    v[:, voff['pool_norm']:voff['pool_norm'] + 8] = cols(inp['pool_norm'][0])
    v[:, voff['pool_scale']:voff['pool_scale'] + 8] = cols(inp['pool_scale'][0])
    for l in range(2):
        v[:, voff['ffn_norm'] + 8 * l:voff['ffn_norm'] + 8 * l + 8] = cols(inp['ffn_norm'][l])
        for k in range(3):
            c0 = voff['conv_w'] + l * 132 + k * 44
            v[:, c0:c0 + 44] = cols(inp['conv_w'][l, k])
        c0 = voff['conv_b'] + l * 44
        v[:, c0:c0 + 44] = cols(inp['conv_b'][l])
    v[:, voff['q_gain']] = np.asarray(inp['q_gain'][0], np.float32)
    v[:, voff['k_gain']] = np.asarray(inp['k_gain'][0], np.float32)
    return v


def make_in_maps(cfg, inp, n_cores=8):
    xp = np.asarray(inp['x_prompt'], np.float32)
    xs = np.asarray(inp['x_sample'], np.float32)
    npc = cfg.SP // cfg.NOP
    nsc = cfg.SS // cfg.NOS
    cosk, sink = rope_tables(np.arange(cfg.SP))
    shared = dict(
        cosk=cosk, sink=sink, consts=const_mats(), vecs=pack_vecs(inp),
        gbc=np.ascontiguousarray(np.broadcast_to(np.asarray(inp['attn_norm'][0], np.float32)[None, :], (128, D))),
        w_qkv=np.ascontiguousarray(np.asarray(inp['w_qkv'][0], np.float32)),
        w_o=np.ascontiguousarray(np.asarray(inp['w_o'][0], np.float32)),
        w_pool=np.ascontiguousarray(np.asarray(inp['w_pool'][0], np.float32).reshape(4 * 256, 256)),
        w_up=np.ascontiguousarray(np.asarray(inp['w_up'], np.float32)),
        w_down=np.ascontiguousarray(np.asarray(inp['w_down'], np.float32)),
    )
    maps = []
    for c in range(n_cores):
        m = dict(shared)
        for (tag, x, S_, NO, per) in (('p', xp, cfg.SP, cfg.NOP, npc), ('s', xs, cfg.SS, cfg.NOS, nsc)):
            b, part = divmod(c, per)
            q0 = part * NO
            NL = NO + HL + HR
            pos = np.arange(q0 - HL, q0 - HL + NL)
            valid = (pos >= 0) & (pos < S_)
            xl = np.zeros((NL, D), np.float32)
            xl[valid] = x[b, pos[valid]]
            cq, sq_ = rope_tables(np.where(valid, pos, 0))
            m['xkv_' + tag] = np.ascontiguousarray(x[b])
            m['xq_' + tag] = xl
            m['cosq_' + tag] = cq
            m['sinq_' + tag] = sq_
            m['mask_' + tag] = np.ascontiguousarray(np.broadcast_to(valid.astype(np.float32)[None, :], (128, NL)))
        maps.append(m)
    return maps


_CACHE = {}


def run_cfg(cfg, inp, n_cores=8, trace=False):
    key = (cfg.SP, cfg.SS, cfg.NOP, cfg.NOS)
    if key not in _CACHE:
        _CACHE[key] = build(cfg)
    nc, S = _CACHE[key]
    maps = make_in_maps(cfg, inp, n_cores)
    res = run_bass_kernel_spmd(nc, maps, core_ids=list(range(n_cores)), trace=trace)
    B = inp['x_prompt'].shape[0]
    Bs = inp['x_sample'].shape[0]
    yp = np.zeros((B, cfg.SP, D), np.float32)
    ys = np.zeros((Bs, cfg.SS, D), np.float32)
    npc = cfg.SP // cfg.NOP
    nsc = cfg.SS // cfg.NOS
    for c in range(n_cores):
        b, part = divmod(c, npc)
        yp[b, part * cfg.NOP:(part + 1) * cfg.NOP] = res.results[c]['y_p']
        b, part = divmod(c, nsc)
        ys[b, part * cfg.NOS:(part + 1) * cfg.NOS] = res.results[c]['y_s']
    return (yp, ys), res


def kernel(**inputs):
    cfg = Cfg()
    (yp, ys), _ = run_cfg(cfg, inputs)
    return (yp, ys)
```

```python
import sys
import numpy as np
from contextlib import ExitStack
import concourse.bass as bass
import concourse.mybir as mybir
from concourse.bass_utils import run_bass_kernel_spmd

F32 = mybir.dt.float32
BF16 = mybir.dt.bfloat16
AF = mybir.ActivationFunctionType
ALU = mybir.AluOpType

D = 1024
NH = 8
NKV = 2
HD = 128
DFF = 2816
NFC = DFF // 128
EPS = 1e-6
GRID_W = 64
HL, HR = 10, 9


class _Rec:
    def __getattr__(self, name):
        def f(*a, **k):
            self.call = (name, a, k)
            return self
        return f


class Sched:
    CE = ('pe', 'act', 'dve', 'pool')

    def __init__(self, nc, es):
        self.nc = nc
        self.es = es
        self.eng = dict(pe=nc.tensor, act=nc.scalar, dve=nc.vector, pool=nc.gpsimd, sp=nc.sync)
        self.ops = []
        self.tags = {}
        self.sem = {e: es.enter_context(nc.semaphore("sem_" + e)) for e in self.CE}
        self.slot_sem = {}
        self.cnt = {e: 0 for e in self.CE}
        self.slot_cnt = {}
        self.waited = {e: {} for e in self.eng}
        self.pend = {e: {} for e in self.eng}
        self.stats = dict(n_ops=0, n_wait=0)

    def op(self, eng, fn, r=(), w=(), dma=None):
        rec = _Rec()
        fn(rec)
        name, a, k = rec.call
        self.tags.setdefault(eng, []).append(sys._getframe(1).f_lineno)
        self.ops.append((eng, (lambda E, name=name, a=a, k=k: getattr(E, name)(*a, **k)), tuple(r), tuple(w), dma))

    def _wait(self, eng, key, val):
        if val <= 0 or self.waited[eng].get(key, 0) >= val:
            return
        self.waited[eng][key] = val
        s = self.slot_sem[key[1]] if key[0] == 's' else self.sem[key[1]]
        self.eng[eng].wait_ge(s, val)
        self.stats['n_wait'] += 1

    def flush(self):
        nc = self.nc
        ops = self.ops
        self.ops = []
        n = len(ops)
        lastw = {}
        readers = {}
        deps = [None] * n
        last_on = {}
        for i, (eng, fn, r, w, dma) in enumerate(ops):
            d = set()
            for k in r:
                j = lastw.get(k)
                if j is not None:
                    d.add(j)
                if isinstance(k, tuple) and k[0] in ('ps', 'psb'):
                    for j in readers.get(k, ()):
                        if ops[j][0] != eng:
                            d.add(j)
            for k in w:
                j = lastw.get(k)
                if j is not None:
                    d.add(j)
                for j in readers.get(k, ()):
                    d.add(j)
            d.discard(i)
            if eng == 'pe':
                d = {j for j in d if ops[j][0] != 'pe'}
            deps[i] = d
            for k in r:
                readers.setdefault(k, []).append(i)
            for k in w:
                lastw[k] = i
                readers[k] = []
            if dma is None:
                last_on[eng] = i
        signal = [False] * n
        for i in range(n):
            for j in deps[i]:
                if ops[j][4] is None:
                    signal[j] = True
        for e, i in last_on.items():
            signal[i] = True
        sigval = [0] * n
        for i, (eng, fn, r, w, dma) in enumerate(ops):
            if dma is not None and dma not in self.slot_sem:
                self.slot_sem[dma] = self.es.enter_context(nc.semaphore("dq%d" % len(self.slot_sem)))
                self.slot_cnt[dma] = 0
            if self.pend[eng]:
                for key, val in self.pend[eng].items():
                    self._wait(eng, key, val)
                self.pend[eng] = {}
            need = {}
            for j in deps[i]:
                if ops[j][4] is not None:
                    key = ('s', ops[j][4])
                    val = self.slot_cnt[ops[j][4]]
                else:
                    key = ('e', ops[j][0])
                    val = sigval[j]
                if need.get(key, 0) < val:
                    need[key] = val
            for key, val in need.items():
                self._wait(eng, key, val)
            ins = fn(self.eng[eng])
            if dma is not None:
                self.slot_cnt[dma] += 16
                ins.then_inc(self.slot_sem[dma], 16)
            elif signal[i]:
                self.cnt[eng] += 1
                ins.then_inc(self.sem[eng], 1)
                sigval[i] = self.cnt[eng]
        self.stats['n_ops'] += n
        for e in self.eng:
            for ce in self.CE:
                if self.pend[e].get(('e', ce), 0) < self.cnt[ce]:
                    self.pend[e][('e', ce)] = self.cnt[ce]
            for s, v in self.slot_cnt.items():
                if self.pend[e].get(('s', s), 0) < v:
                    self.pend[e][('s', s)] = v

    def finish(self):
        self.flush()
        for key, val in self.pend['sp'].items():
            self._wait('sp', key, val)
        self.pend['sp'] = {}


class Cfg:
    def __init__(self, SP=16384, SS=4096, NOP=4096, NOS=2048, nwp=9, nws=5, CK=16):
        self.SP, self.SS, self.NOP, self.NOS = SP, SS, NOP, NOS
        self.CK = CK
        self.seqs = []
        for (S, NO, nw, tag) in ((SP, NOP, nwp, 'p'), (SS, NOS, nws, 's')):
            so = -(-NO // nw)
            wins = []
            o = 0
            while o < NO:
                no = min(so, NO - o)
                wins.append((o, no))
                o += no
            assert max(w[1] for w in wins) + HL + HR <= 512
            self.seqs.append(dict(S=S, NO=NO, NL=NO + HL + HR, wins=wins, tag=tag))


def vec_layout():
    off = {}
    c = 0
    for name, ncol in (('attn_norm', 8), ('pool_norm', 8), ('pool_scale', 8), ('ffn_norm', 16),
                       ('q_gain', 1), ('k_gain', 1), ('conv_w', 2 * 3 * 44), ('conv_b', 2 * 44)):
        off[name] = c
        c += ncol
    return off, c


def build(cfg):
    nc = bass.Bass("TRN2", target_bir_lowering=False)
    es_all = ExitStack()
    S = Sched(nc, es_all)
    voff, NV = vec_layout()

    def din(name, shape, dt=F32):
        return nc.dram_tensor(name, list(shape), dt, kind="ExternalInput").ap()

    def dscr(name, shape, dt=BF16):
        return nc.dram_tensor(name, list(shape), dt, kind="Internal").ap()

    sq = cfg.seqs
    xkv = [din("xkv_" + s['tag'], [s['S'], D]) for s in sq]
    xq = [din("xq_" + s['tag'], [s['NL'], D]) for s in sq]
    cosk = din("cosk", [128, cfg.SP])
    sink = din("sink", [128, cfg.SP])
    cosq = [din("cosq_" + s['tag'], [128, s['NL']]) for s in sq]
    sinq = [din("sinq_" + s['tag'], [128, s['NL']]) for s in sq]
    maskd = [din("mask_" + s['tag'], [128, s['NL']]) for s in sq]
    consts = din("consts", [128, 3 * 128])
    vecs = din("vecs", [128, NV])
    gbc = din("gbc", [128, D])
    w_qkv = din("w_qkv", [D, 1536])
    w_o = din("w_o", [D, D])
    w_pool = din("w_pool", [4 * 256, 256])
    w_up = din("w_up", [2, D, 2 * DFF])
    w_down = din("w_down", [2, DFF, D])
    yout = [nc.dram_tensor("y_" + s['tag'], [s['NO'], D], F32, kind="ExternalOutput").ap() for s in sq]

    wq_s = dscr("wq_s", [8, 128, 1024])
    wk_s = dscr("wk_s", [2, 128, 1024])
    wv_s = dscr("wv_s", [128, 2048])
    wo_s = dscr("wo_s", [8, 128, 1024])
    wup_s = dscr("wup_s", [2, NFC, 128, 2, 1024])
    wd_s = dscr("wd_s", [2, 8, 128, NFC * 128])
    wp_s = dscr("wp_s", [128, 2048])
    kT_s = [dscr("kT_" + s['tag'], [2, 128, s['S']]) for s in sq]
    v_s = [dscr("v_" + s['tag'], [2, 128, s['S'] // 128, 128]) for s in sq]

    def sb(es, name, shape, dt=F32):
        return es.enter_context(nc.sbuf_tensor(name, list(shape), dt))

    cst = sb(es_all, "cst", [128, 384])
    vec = sb(es_all, "vec", [128, NV])
    onesb = sb(es_all, "onesb", [128, 128], BF16)
    epst = sb(es_all, "epst", [128, 1])
    epsc = epst[:, 0:1]
    identb = sb(es_all, "identb", [128, 128], BF16)
    ident = cst[:, 0:128]
    perm = cst[:, 128:256]
    ones = cst[:, 256:384]
    S.op('sp', lambda e: e.dma_start(out=cst[:], in_=consts[:, :]), w=['cst'], dma='cst')
    S.op('sp', lambda e: e.dma_start(out=vec[:], in_=vecs[:, :]), w=['vec'], dma='vec')
    S.op('dve', lambda e: e.tensor_copy(out=onesb[:], in_=ones), r=['cst'], w=['onesb'])
    S.op('dve', lambda e: e.memset(epst[:], EPS), w=['epsc'])
    S.op('dve', lambda e: e.tensor_copy(out=identb[:], in_=ident), r=['cst'], w=['identb'])

    for l_ in range(2):
        for k_ in range(3):
            c0_ = voff['conv_w'] + l_ * 132 + k_ * 44 + NFC
            S.op('dve', lambda e, c0_=c0_: e.tensor_scalar(out=vec[:, c0_:c0_ + NFC], in0=vec[:, c0_:c0_ + NFC], scalar1=0.5,
                                                           scalar2=None, op0=ALU.mult), r=['vec'], w=['vec'])
        c0_ = voff['conv_b'] + l_ * 44 + NFC
        S.op('dve', lambda e, c0_=c0_: e.tensor_scalar(out=vec[:, c0_:c0_ + NFC], in0=vec[:, c0_:c0_ + NFC], scalar1=0.5,
                                                       scalar2=None, op0=ALU.mult), r=['vec'], w=['vec'])

    def vcol(name, i=0):
        c = voff[name] + i
        return vec[:, c:c + 1]

    PS = {}

    def bank(b):
        return PS['t'][:, b, :]

    def bk(b):
        return ('ps', b)

    es_w = ExitStack()
    STG = 4096
    NST = 3
    stf = [sb(es_w, "stf%d" % i, [128, STG]) for i in range(NST)]
    stb = [sb(es_w, "stb%d" % i, [128, STG], BF16) for i in range(NST)]
    wstate = dict(i=0)
    cast_engs = ['act', 'act', 'act']

    def prep(src, dst, nrc, cw, blocked, dkey=None):
        i = wstate['i']
        wstate['i'] += 1
        f, b = stf[i % NST], stb[i % NST]
        kf, kb = ('stf', i % NST), ('stb', i % NST)
        nel = nrc * cw
        fv = f[:, 0:nel].rearrange("p (c n) -> p c n", c=nrc)
        S.op('sp', lambda e: e.dma_start(out=fv, in_=src), w=[kf], dma=kf)
        if wstate.get('pend'):
            wstate.pop('pend')()
        ce = cast_engs[i % 3]
        if blocked:
            nb = cw // 128
            iv = f[:, 0:nel].rearrange("p (c b j) -> p b c j", c=nrc, b=nb)
            ov = b[:, 0:nel].rearrange("p (b c j) -> p b c j", c=nrc, b=nb)
            dv = b[:, 0:nel].rearrange("p (b x) -> p b x", b=nb)
        else:
            iv = f[:, 0:nel]
            ov = b[:, 0:nel]
            dv = b[:, 0:nel]
        if ce == 'act':
            if blocked:
                for bb in range(nb):
                    S.op('act', lambda e, bb=bb: e.activation(out=ov[:, bb], in_=iv[:, bb], func=AF.Copy),
                         r=[kf], w=[kb])
            else:
                S.op('act', lambda e: e.activation(out=ov, in_=iv, func=AF.Copy), r=[kf], w=[kb])
        else:
            if blocked:
                for bb in range(nb):
                    S.op(ce, lambda e, bb=bb: e.tensor_copy(out=ov[:, bb], in_=iv[:, bb]), r=[kf], w=[kb])
            else:
                S.op(ce, lambda e: e.tensor_copy(out=ov, in_=iv), r=[kf], w=[kb])
        def store_():
            S.op('sp', lambda e: e.dma_start(out=dst, in_=dv), r=[kb], w=([dkey] if dkey else []), dma=('wst', i % NST))
        if dkey:
            store_()
        else:
            wstate['pend'] = store_

    def rows(w2d, nrc):
        return w2d.rearrange("(c p) n -> p c n", p=128)

    prep_steps = []

    def P(*args, **kw):
        prep_steps.append(lambda: prep(*args, **kw))
    qv = rows(w_qkv, 8)
    prep(qv[:, :, 1024:1280], wk_s[:, :, :].rearrange("b p x -> p b x"), 8, 256, True, dkey='wk_s')
    prep(qv[:, :, 1280:1536], wv_s[:, :], 8, 256, False, dkey='wv_s')
    for sl in range(2):
        P(qv[:, :, sl * 512:(sl + 1) * 512], wq_s[sl * 4:(sl + 1) * 4].rearrange("b p x -> p b x"), 8, 512, True)
    ov_ = rows(w_o, 8)
    for sl in range(2):
        P(ov_[:, :, sl * 512:(sl + 1) * 512], wo_s[sl * 4:(sl + 1) * 4].rearrange("b p x -> p b x"), 8, 512, True)
    P(rows(w_pool, 8), wp_s[:, :], 8, 256, False)
    for l in range(2):
        uv = rows(w_up[l], 8)
        for half in range(2):
            for b0 in range(0, NFC, 4):
                nb = min(4, NFC - b0)
                c0 = half * DFF + b0 * 128
                P(uv[:, :, c0:c0 + nb * 128],
                  wup_s[l, b0:b0 + nb, :, half, :].rearrange("b p x -> p b x"), 8, nb * 128, True)
        dv_ = rows(w_down[l], NFC)
        for m in range(8):
            P(dv_[:, :, m * 128:(m + 1) * 128], wd_s[l, m:m + 1].rearrange("b p x -> p b x"), NFC, 128, True)

    def rope_head(T, src_ps_bank, b_ss, b_pm, gain_col, cos_t, sin_t, N, out_bf, out_keys, post_scale, tagk, stages=(0, 1, 2), rope_key=None, aux='pool'):
        kg, sqt, t1, t2, rs = T['kg'], T['sq'], T['t1'], T['t2'], T['rs']
        src = bank(src_ps_bank)[:, 0:N]
        rk_ = rope_key if rope_key is not None else tagk + 'rope'
        if 0 in stages:
            S.op('act', lambda e: e.activation(out=kg[:, 0:N], in_=src, func=AF.Copy, scale=gain_col),
                 r=[bk(src_ps_bank), 'vec'], w=[tagk + 'kg'])
            S.op('act', lambda e: e.activation(out=sqt[:, 0:N], in_=src, func=AF.Square),
                 r=[bk(src_ps_bank)], w=[tagk + 'sq'])
        if 1 in stages:
            S.op('pe', lambda e: e.matmul(bank(b_ss)[:, 0:N], lhsT=ones, rhs=sqt[:, 0:N], start=True, stop=True),
                 r=['cst', tagk + 'sq'], w=[bk(b_ss)])
            S.op('act', lambda e: e.activation(out=rs[:, 0:N], in_=bank(b_ss)[:, 0:N], func=AF.Ln, scale=1.0 / HD, bias=epsc),
                 r=[bk(b_ss), 'epsc'], w=[tagk + 'rs'])
            S.op('act', lambda e: e.activation(out=rs[:, 0:N], in_=rs[:, 0:N], func=AF.Exp, scale=-0.5),
                 r=[tagk + 'rs'], w=[tagk + 'rs'])
        if 2 in stages:
            S.op('pe', lambda e: e.matmul(bank(b_pm)[:, 0:N], lhsT=perm, rhs=kg[:, 0:N], start=True, stop=True),
                 r=['cst', tagk + 'kg'], w=[bk(b_pm)])
            S.op('dve', lambda e: e.tensor_tensor(out=t2[:, 0:N], in0=bank(b_pm)[:, 0:N], in1=sin_t, op=ALU.mult),
                 r=[bk(b_pm), rk_], w=[tagk + 't2'])
            S.op(aux, lambda e: e.tensor_tensor(out=t1[:, 0:N], in0=kg[:, 0:N], in1=cos_t, op=ALU.mult),
                 r=[tagk + 'kg', rk_], w=[tagk + 't1'])
            S.op(aux, lambda e: e.tensor_tensor(out=t1[:, 0:N], in0=t1[:, 0:N], in1=t2[:, 0:N], op=ALU.add),
                 r=[tagk + 't1', tagk + 't2'], w=[tagk + 't1'])
            S.op('dve', lambda e: e.scalar_tensor_tensor(out=out_bf, in0=t1[:, 0:N], scalar=post_scale, in1=rs[:, 0:N],
                                                         op0=ALU.mult, op1=ALU.mult),
                 r=[tagk + 't1', tagk + 'rs'], w=out_keys)

    es_a = es_w
    PS['t'] = es_a.enter_context(nc.psum_tensor("psA", [128, 6, 512], F32))
    psb = es_a.enter_context(nc.psum_tensor("psb", [128, 2, 1024], BF16))
    xa = [sb(es_a, "xa%d" % i, [128, 4, D]) for i in range(2)]
    xn = [sb(es_a, "xn%d" % i, [128, 4, D], BF16) for i in range(2)]
    hTa = [sb(es_a, "hTa%d" % i, [128, 8, 512], BF16) for i in range(2)]
    wk_t = sb(es_a, "wk_t", [128, 2, 8, 128], BF16)
    wv_t = sb(es_a, "wv_t", [128, 8, 256], BF16)
    gbc_t = sb(es_a, "gbc_t", [128, D])
    ropeA = [sb(es_a, "ropeA%d" % i, [128, 2, 512]) for i in range(2)]
    TA2 = [{k: sb(es_a, "TA%d_" % g_ + k, [128, 512]) for k in ('kg', 'sq', 't1', 't2', 'rs')} for g_ in range(2)]
    ssa = [sb(es_a, "ssa%d" % i, [128, 8]) for i in range(2)]
    kout = [sb(es_a, "kout%d" % i, [128, 512], BF16) for i in range(4)]
    vout = [sb(es_a, "vout%d" % i, [128, 4, 256], BF16) for i in range(2)]

    S.op('sp', lambda e: e.dma_start(out=wk_t[:].rearrange("p a c j -> p a (c j)"),
                                     in_=wk_s[:, :, :].rearrange("a p x -> p a x")), r=['wk_s'], w=['wk_t'], dma='wk_t')
    S.op('sp', lambda e: e.dma_start(out=wv_t[:].rearrange("p c j -> p (c j)"), in_=wv_s[:, :]),
         r=['wv_s'], w=['wv_t'], dma='wv_t')
    S.op('sp', lambda e: e.dma_start(out=gbc_t[:], in_=gbc[:, :]), w=['gbc_t'], dma='gbc_t')

    tiles = [(si, ti) for si, s in enumerate(sq) for ti in range(s['S'] // 512)]

    AQ = 'pool'
    def frontA(t):
        si, ti = tiles[t]
        t0 = ti * 512
        pb = t % 2
        xat, xnt, hTt, rp, sst = xa[pb], xn[pb], hTa[pb], ropeA[pb], ssa[pb]
        kx = ('xa', pb)
        S.op(AQ, lambda e: e.dma_start(out=xat[:], in_=xkv[si][t0:t0 + 512, :].rearrange("(c p) d -> p c d", p=128)),
             w=[kx], dma=kx)
        kr = ('ropeA', pb)
        S.op(AQ, lambda e: e.dma_start(out=rp[:, 0, :], in_=cosk[:, t0:t0 + 512]), w=[kr], dma=kr)
        S.op(AQ, lambda e: e.dma_start(out=rp[:, 1, :], in_=sink[:, t0:t0 + 512]), w=[kr], dma=kr)
        for c in range(4):
            S.op('act', lambda e, c=c: e.activation(out=xnt[:, c, :], in_=xat[:, c, :], func=AF.Square,
                                                    accum_out=sst[:, c:c + 1]),
                 r=[kx], w=[('ssa', pb, c), ('xn', pb, c)])
        S.op('act', lambda e: e.activation(out=sst[:, 4:8], in_=sst[:, 0:4], func=AF.Ln, scale=1.0 / D, bias=epsc),
             r=[('ssa', pb, c) for c in range(4)] + ['epsc'], w=[('ssa2', pb)])
        S.op('act', lambda e: e.activation(out=sst[:, 4:8], in_=sst[:, 4:8], func=AF.Exp, scale=-0.5),
             r=[('ssa2', pb)], w=[('ssa2', pb)])
        for c in range(4):
            S.op('dve', lambda e, c=c: e.scalar_tensor_tensor(
                out=xnt[:, c, :], in0=xat[:, c, :], scalar=sst[:, 4 + c:5 + c], in1=gbc_t[:],
                op0=ALU.mult, op1=ALU.mult), r=[kx, ('ssa2', pb), 'gbc_t'], w=[('xn', pb, c)])
        for f in range(8):
            for c in range(4):
                S.op('pe', lambda e, f=f, c=c: e.transpose(psb[:, f % 2, c * 128:(c + 1) * 128],
                                                           xnt[:, c, f * 128:(f + 1) * 128], identb[:]),
                     r=[('xn', pb, c), 'identb'], w=[('psb', f % 2)])
            if f % 2 == 0:
                S.op('act', lambda e, f=f: e.activation(out=hTt[:, f, :], in_=psb[:, f % 2, 0:512], func=AF.Copy),
                     r=[('psb', f % 2)], w=[('hTa', pb, f)])
            else:
                S.op('dve', lambda e, f=f: e.tensor_copy(out=hTt[:, f, :], in_=psb[:, f % 2, 0:512]),
                     r=[('psb', f % 2)], w=[('hTa', pb, f)])

    def backA(t):
        si, ti = tiles[t]
        t0 = ti * 512
        pb = t % 2
        hTt, rp = hTa[pb], ropeA[pb]
        vo = vout[pb]
        vok = ('vout', pb)

        def vpart(cs):
            for c in cs:
                pbv = 2 + c % 2
                for f in range(8):
                    S.op('pe', lambda e, c=c, f=f, pbv=pbv: e.matmul(bank(pbv)[:, 0:256], lhsT=hTt[:, f, c * 128:(c + 1) * 128],
                                                                     rhs=wv_t[:, f, :], start=(f == 0), stop=(f == 7)),
                         r=['wv_t', ('hTa', pb, f)], w=[bk(pbv)])
                if c % 2 == 0:
                    S.op('act', lambda e, c=c, pbv=pbv: e.activation(out=vo[:, c, :], in_=bank(pbv)[:, 0:256], func=AF.Copy),
                         r=[bk(pbv)], w=[vok])
                else:
                    S.op('dve', lambda e, c=c, pbv=pbv: e.tensor_copy(out=vo[:, c, :], in_=bank(pbv)[:, 0:256]),
                         r=[bk(pbv)], w=[vok])

        def rope(g, stages):
            ko = kout[(t * 2 + g) % 4]
            kok = ('kout', (t * 2 + g) % 4)
            rope_head(TA2[g], g, 4 + g, 4 + g, vcol('k_gain'), rp[:, 0, :], rp[:, 1, :], 512, ko[:], [kok], 1.0, 'A%d' % g,
                      stages=stages, rope_key=('ropeA', pb), aux='dve')
            if 2 in stages:
                S.op(AQ, lambda e: e.dma_start(out=kT_s[si][g, :, t0:t0 + 512], in_=ko[:]), r=[kok], dma=kok)

        for g in range(NKV):
            for f in range(8):
                S.op('pe', lambda e, g=g, f=f: e.matmul(bank(g)[:, :], lhsT=wk_t[:, g, f, :], rhs=hTt[:, f, :],
                                                        start=(f == 0), stop=(f == 7)),
                     r=['wk_t', ('hTa', pb, f)], w=[bk(g)])
        for g in range(NKV):
            rope(g, (0,))
        vpart((0, 1))
        for g in range(NKV):
            rope(g, (1,))
        vpart((2, 3))
        for g in range(NKV):
            rope(g, (2,))
        for g in range(NKV):
            S.op(AQ, lambda e, g=g: e.dma_start(out=v_s[si][g, :, ti * 4:(ti + 1) * 4, :], in_=vo[:, :, g * 128:(g + 1) * 128]),
                 r=[vok], dma=vok)

    frontA(0)
    per_tile = -(-len(prep_steps) // max(1, len(tiles) - 4))
    for t in range(len(tiles)):
        if t + 1 < len(tiles):
            frontA(t + 1)
        for _ in range(per_tile):
            if prep_steps:
                prep_steps.pop(0)()
        backA(t)
    while prep_steps:
        prep_steps.pop(0)()
    if wstate.get('pend'):
        wstate.pop('pend')()
    S.flush()
    es_a.close()

    es_b = ExitStack()
    PS['t'] = es_b.enter_context(nc.psum_tensor("psB", [128, 8, 512], F32))
    xT = sb(es_b, "xT", [128, 8, 512])
    hT = sb(es_b, "hT", [128, 8, 512], BF16)
    hF = sb(es_b, "hF", [128, 512])
    oT = sb(es_b, "oT", [128, 8, 512], BF16)
    aT = sb(es_b, "aT", [128, NFC, 512], BF16)
    plT = oT
    xw = [sb(es_b, "xw%d" % i, [128, D]) for i in range(2)]
    qT = [sb(es_b, "qT%d" % i, [128, 512], BF16) for i in range(2)]
    NPT = 5
    pT = [sb(es_b, "pT%d" % i, [128, 1024], BF16) for i in range(NPT)]
    RK = 3
    kring = [sb(es_b, "kr%d" % i, [128, cfg.CK * 128], BF16) for i in range(RK)]
    vring = [sb(es_b, "vr%d" % i, [128, cfg.CK, 128], BF16) for i in range(RK)]
    TB = {k: sb(es_b, "TB_" + k, [128, 512]) for k in ('kg', 'sq', 't1', 't2', 'rs')}
    ropeB = sb(es_b, "ropeB", [128, 2, 512])
    maskt = sb(es_b, "maskt", [128, 512])
    icnt = sb(es_b, "icnt", [128, 4, 512])
    rstd = sb(es_b, "rstd", [128, 512])
    osb = sb(es_b, "osb", [128, 512])
    lsb = sb(es_b, "lsb", [128, 512])
    tmpr = [sb(es_b, "tmpr%d" % i, [128, 512]) for i in range(2)]
    sqn = tmpr
    ffA = [sb(es_b, "ffA%d" % i, [128, 2, 512]) for i in range(4)]
    ffAf = [t_[:].rearrange("p a n -> p (a n)") for t_ in ffA]
    cg = [ffA[0][:, p_, :] for p_ in range(2)]
    cv = [ffA[1][:, p_, :] for p_ in range(2)]
    th = [ffA[2][:, p_, :] for p_ in range(2)]
    a0 = [ffA[3][:, p_, :] for p_ in range(2)]
    pa = [TB['kg'], TB['sq'], TB['t1']]
    wq_t = [sb(es_b, "wq_t%d" % i, [128, 8, 128], BF16) for i in range(2)]
    wo_t = [sb(es_b, "wo_t%d" % i, [128, 8, 128], BF16) for i in range(2)]
    wup_t = [sb(es_b, "wup_t%d" % i, [128, 2, 8, 128], BF16) for i in range(4)]
    wd_t = [sb(es_b, "wd_t%d" % i, [128, NFC, 128], BF16) for i in range(2)]
    wp_t = sb(es_b, "wp_t", [128, 8, 256], BF16)
    S.op('sp', lambda e: e.dma_start(out=wp_t[:].rearrange("p c j -> p (c j)"), in_=wp_s[:, :]), w=['wp_t'], dma='wp_t')

    cnt = dict(wq=0, wo=0, wup=0, wd=0, xw=0, kv=0, q=0, p=0, sqn=0, tmpr=0, ff=0)

    def fm_norm(N, gname, gi, out_fn, out_keys_fn, chunks=range(8)):
        for c in range(8):
            i = cnt['sqn'] % 2
            cnt['sqn'] += 1
            S.op('act', lambda e, c=c, i=i: e.activation(out=sqn[i][:, 0:N], in_=xT[:, c, 0:N], func=AF.Square),
                 r=[('xT', c)], w=[('tmpr', i)])
            S.op('pe', lambda e, c=c, i=i: e.matmul(bank(6)[:, 0:N], lhsT=ones, rhs=sqn[i][:, 0:N],
                                                    start=(c == 0), stop=(c == 7)), r=['cst', ('tmpr', i)], w=[bk(6)])
        S.op('act', lambda e: e.activation(out=rstd[:, 0:N], in_=bank(6)[:, 0:N], func=AF.Ln, scale=1.0 / D, bias=epsc),
             r=[bk(6), 'epsc'], w=['rstd'])
        S.op('act', lambda e: e.activation(out=rstd[:, 0:N], in_=rstd[:, 0:N], func=AF.Exp, scale=-0.5), r=['rstd'], w=['rstd'])

    def norm_chunk(N, c, gname, gi, out_ap, out_keys):
        S.op('dve', lambda e: e.scalar_tensor_tensor(out=out_ap, in0=xT[:, c, 0:N], scalar=vcol(gname, gi * 8 + c),
                                                     in1=rstd[:, 0:N], op0=ALU.mult, op1=ALU.mult),
             r=[('xT', c), 'rstd', 'vec'], w=out_keys)

    def resid_add(N, m, psbank, scale_col):
        i = cnt['tmpr'] % 2
        cnt['tmpr'] += 1
        t = tmpr[i]
        S.op('dve', lambda e: e.scalar_tensor_tensor(out=t[:, 0:N], in0=bank(psbank)[:, 0:N],
                                                     scalar=(1.0 if scale_col is None else scale_col),
                                                     in1=maskt[:, 0:N], op0=ALU.mult, op1=ALU.mult),
             r=[bk(psbank), 'maskt', 'vec'], w=[('tmpr', i)])
        S.op('pool', lambda e: e.tensor_tensor(out=xT[:, m, 0:N], in0=xT[:, m, 0:N], in1=t[:, 0:N], op=ALU.add),
             r=[('xT', m), ('tmpr', i)], w=[('xT', m)])

    def ffn(N, l):
        fm_norm(N, 'ffn_norm', l, None, None)
        for c in range(8):
            norm_chunk(N, c, 'ffn_norm', l, hT[:, c, 0:N], [('hT', c)])

        def load_wup(j):
            i = cnt['wup'] % 4
            cnt['wup'] += 1
            S.op('sp', lambda e: e.dma_start(out=wup_t[i][:].rearrange("p a c j -> p (a c j)"),
                                             in_=wup_s[l, j].rearrange("p a x -> p (a x)")),
                 w=[('wup', i)], dma=('wup', i))
            return i
        slots = {}
        for j in range(min(3, NFC)):
            slots[j] = load_wup(j)
        cwb = voff['conv_w'] + l * 3 * 44
        cbb = voff['conv_b'] + l * 44
        for j in range(NFC):
            if j + 3 < NFC:
                slots[j + 3] = load_wup(j + 3)
            wi = slots[j]
            par = cnt['ff'] % 2
            cnt['ff'] += 1
            bg, bv = (2, 3) if par == 0 else (4, 0)
            for half, bnk in ((0, bg), (1, bv)):
                for c in range(8):
                    S.op('pe', lambda e, half=half, bnk=bnk, c=c, wi=wi: e.matmul(
                        bank(bnk)[:, 0:N], lhsT=wup_t[wi][:, half, c, :], rhs=hT[:, c, 0:N],
                        start=(c == 0), stop=(c == 7)), r=[('wup', wi), ('hT', c)], w=[bk(bnk)])

            def taps(half):
                ch = half * NFC + j
                return [vec[:, cwb + k_ * 44 + ch: cwb + k_ * 44 + ch + 1] for k_ in range(3)] + \
                       [vec[:, cbb + ch: cbb + ch + 1]]
            kg_, kv_, kt_, ka_ = ('ffA', 0, par), ('ffA', 1, par), ('ffA', 2, par), ('ffA', 3, par)
            cgt, cvt, tht, a0t = cg[par], cv[par], th[par], a0[par]
            w0, w1, w2, bb = taps(0)
            S.op('act', lambda e: e.activation(out=cgt[:, 0:N], in_=bank(bg)[:, 0:N], func=AF.Identity, scale=w1, bias=bb),
                 r=[bk(bg), 'vec'], w=[kg_])
            S.op('act', lambda e: e.activation(out=a0t[:, 1:N], in_=bank(bg)[:, 0:N - 1], func=AF.Copy, scale=w0),
                 r=[bk(bg), 'vec'], w=[ka_])
            S.op('dve', lambda e: e.scalar_tensor_tensor(out=cgt[:, 0:N - 1], in0=bank(bg)[:, 1:N], scalar=w2,
                                                         in1=cgt[:, 0:N - 1], op0=ALU.mult, op1=ALU.add),
                 r=[bk(bg), kg_, 'vec'], w=[kg_])
            S.op('pool', lambda e: e.tensor_tensor(out=cgt[:, 1:N], in0=cgt[:, 1:N], in1=a0t[:, 1:N], op=ALU.add),
                 r=[kg_, ka_], w=[kg_])
            w0, w1, w2, bb = taps(1)
            S.op('act', lambda e: e.activation(out=cvt[:, 0:N], in_=bank(bv)[:, 0:N], func=AF.Identity, scale=w1, bias=bb),
                 r=[bk(bv), 'vec'], w=[kv_])
            S.op('dve', lambda e: e.scalar_tensor_tensor(out=cvt[:, 1:N], in0=bank(bv)[:, 0:N - 1], scalar=w0,
                                                         in1=cvt[:, 1:N], op0=ALU.mult, op1=ALU.add),
                 r=[bk(bv), kv_, 'vec'], w=[kv_])
            S.op('dve', lambda e: e.scalar_tensor_tensor(out=cvt[:, 0:N - 1], in0=bank(bv)[:, 1:N], scalar=w2,
                                                         in1=cvt[:, 0:N - 1], op0=ALU.mult, op1=ALU.add),
                 r=[bk(bv), kv_, 'vec'], w=[kv_])
            S.op('act', lambda e: e.activation(out=tht[:, 0:N], in_=cgt[:, 0:N], func=AF.Tanh, scale=0.5),
                 r=[kg_], w=[kt_])
            S.op('pool', lambda e: e.tensor_tensor(out=cvt[:, 0:N], in0=cvt[:, 0:N], in1=cgt[:, 0:N], op=ALU.mult),
                 r=[kg_, kv_], w=[kv_])
            S.op('dve', lambda e, j=j: e.scalar_tensor_tensor(out=aT[:, j, 0:N], in0=tht[:, 0:N], scalar=1.0, in1=cvt[:, 0:N],
                                                              op0=ALU.add, op1=ALU.mult),
                 r=[kt_, kv_], w=[('aT', j)])
        def load_wd(m):
            i = cnt['wd'] % 2
            cnt['wd'] += 1
            S.op('sp', lambda e: e.dma_start(out=wd_t[i][:].rearrange("p j x -> p (j x)"), in_=wd_s[l, m]),
                 w=[('wd', i)], dma=('wd', i))
            return i
        dsl = {0: load_wd(0)}
        for m in range(8):
            if m + 1 < 8:
                dsl[m + 1] = load_wd(m + 1)
            wi = dsl[m]
            bnk = 5 if m % 2 == 0 else 1
            for j in range(NFC):
                S.op('pe', lambda e, j=j, wi=wi, bnk=bnk: e.matmul(bank(bnk)[:, 0:N], lhsT=wd_t[wi][:, j, :], rhs=aT[:, j, 0:N],
                                                                   start=(j == 0), stop=(j == NFC - 1)),
                     r=[('wd', wi), ('aT', j)], w=[bk(bnk)])
            resid_add(N, m, bnk, None)

    for si, s in enumerate(sq):
        nkt = s['S'] // 128
        CK = min(cfg.CK, nkt)
        nck = nkt // CK
        jobs = [(wi_, h, ci) for wi_ in range(len(s['wins'])) for h in range(NH) for ci in range(nck)]
        jslot = {}

        def load_kv(jn, si=si, CK=CK, jobs=jobs, jslot=jslot):
            if jn >= len(jobs) or jn in jslot:
                return
            (_, h, ci) = jobs[jn]
            g = h // (NH // NKV)
            i = cnt['kv'] % RK
            cnt['kv'] += 1
            jslot[jn] = i
            S.op('sp', lambda e: e.dma_start(out=kring[i][:, 0:CK * 128], in_=kT_s[si][g, :, ci * CK * 128:(ci + 1) * CK * 128]),
                 w=[('kv', i)], dma=('kv', i))
            S.op('sp', lambda e: e.dma_start(out=vring[i][:, 0:CK, :], in_=v_s[si][g, :, ci * CK:(ci + 1) * CK, :]),
                 w=[('kv', i)], dma=('kv', i))

        for wi_, (o0, no) in enumerate(s['wins']):
            N = no + HL + HR
            r0 = o0
            ntc = -(-N // 128)
            S.op('sp', lambda e, r0=r0, N=N, si=si: e.dma_start(out=maskt[:, 0:N], in_=maskd[si][:, r0:r0 + N]),
                 w=['maskt'], dma='maskt')
            S.op('sp', lambda e, r0=r0, N=N, si=si: e.dma_start(out=ropeB[:, 0, 0:N], in_=cosq[si][:, r0:r0 + N]),
                 w=['Brope'], dma='ropeB')
            S.op('sp', lambda e, r0=r0, N=N, si=si: e.dma_start(out=ropeB[:, 1, 0:N], in_=sinq[si][:, r0:r0 + N]),
                 w=['Brope'], dma='ropeB')
            for tc in range(ntc):
                nt = min(128, N - tc * 128)
                i = cnt['xw'] % 2
                cnt['xw'] += 1
                S.op('sp', lambda e, i=i, tc=tc, nt=nt, r0=r0, si=si: e.dma_start(
                    out=xw[i][0:nt, :], in_=xq[si][r0 + tc * 128: r0 + tc * 128 + nt, :]), w=[('xw', i)], dma=('xw', i))
                for f in range(8):
                    bnk = f % 4
                    S.op('pe', lambda e, i=i, f=f, tc=tc, nt=nt, bnk=bnk: e.transpose(
                        bank(bnk)[:, tc * 128: tc * 128 + nt], xw[i][0:nt, f * 128:(f + 1) * 128], ident[0:nt, 0:nt]),
                        r=[('xw', i), 'cst'], w=[bk(bnk)])
                    if f % 2 == 0:
                        S.op('act', lambda e, f=f, tc=tc, nt=nt, bnk=bnk: e.activation(
                            out=xT[:, f, tc * 128: tc * 128 + nt], in_=bank(bnk)[:, tc * 128: tc * 128 + nt], func=AF.Copy),
                            r=[bk(bnk)], w=[('xT', f)])
                    else:
                        S.op('dve', lambda e, f=f, tc=tc, nt=nt, bnk=bnk: e.tensor_copy(
                            out=xT[:, f, tc * 128: tc * 128 + nt], in_=bank(bnk)[:, tc * 128: tc * 128 + nt]),
                            r=[bk(bnk)], w=[('xT', f)])

            def window_sums(cur, ck, g, out_ap, out_keys, eng, temps):
                steps = [(1, 0), (1, 1), (2, 2), (4, 4)]
                lo, hi = 0, N
                for lvl in range(g + 1):
                    sl, sr = steps[lvl]
                    dst, dk = (out_ap, out_keys) if lvl == g else temps[lvl]
                    nlo, nhi = lo + sl, hi - sr
                    S.op(eng, lambda e, dst=dst, cur=cur, nlo=nlo, nhi=nhi, sl=sl, sr=sr: e.tensor_tensor(
                        out=dst[:, nlo:nhi], in0=cur[:, nlo - sl:nhi - sl], in1=cur[:, nlo + sr:nhi + sr], op=ALU.add),
                        r=ck, w=dk)
                    cur, ck, lo, hi = dst, dk, nlo, nhi

            DVE_T = [(TB['kg'], ['Bkg']), (TB['sq'], ['Bsq']), (TB['t1'], ['Bt1'])]
            POOL_T = [(ffA[0][:, 0, :], [('ffA', 0, 0)]), (ffA[0][:, 1, :], [('ffA', 0, 1)]), (ffA[1][:, 0, :], [('ffA', 1, 0)])]
            for g in range(4):
                S.op('pool', lambda e, g=g: e.tensor_copy(out=icnt[:, g, 0:N], in_=maskt[:, 0:N]), r=['maskt'], w=[('icnt', g)])
                window_sums(maskt, ['maskt'], g, icnt[:, g, :], [('icnt', g)], 'pool', POOL_T)
                S.op('dve', lambda e, g=g: e.tensor_scalar(out=icnt[:, g, 0:N], in0=icnt[:, g, 0:N], scalar1=1.0, scalar2=None,
                                                           op0=ALU.max), r=[('icnt', g)], w=[('icnt', g)])
                S.op('dve', lambda e, g=g: e.reciprocal(out=icnt[:, g, 0:N], in_=icnt[:, g, 0:N]),
                     r=[('icnt', g)], w=[('icnt', g)])
            fm_norm(N, 'attn_norm', 0, None, None)
            for c in range(8):
                norm_chunk(N, c, 'attn_norm', 0, hT[:, c, 0:N], [('hT', c)])

            def load_wq(h):
                i = cnt['wq'] % 2
                cnt['wq'] += 1
                S.op('sp', lambda e: e.dma_start(out=wq_t[i][:].rearrange("p c j -> p (c j)"), in_=wq_s[h]),
                     w=[('wq', i)], dma=('wq', i))
                return i

            def qprep_a(h, wslot):
                for c in range(8):
                    S.op('pe', lambda e, c=c: e.matmul(bank(5)[:, 0:N], lhsT=wq_t[wslot][:, c, :], rhs=hT[:, c, 0:N],
                                                       start=(c == 0), stop=(c == 7)),
                         r=[('wq', wslot), ('hT', c)], w=[bk(5)])

            def qprep_b(h, stages=(0, 1, 2), qi=None):
                if qi is None:
                    qi = cnt['q'] % 2
                    cnt['q'] += 1
                xb_ = 6 + (h % 2)
                rope_head(TB, 5, xb_, xb_, vcol('q_gain'), ropeB[:, 0, 0:N], ropeB[:, 1, 0:N], N, qT[qi][:, 0:N],
                          [('qT', qi)], float(HD) ** -0.5, 'B', stages=stages)
                return qi

            base_job = wi_ * NH * nck
            if wi_ == 0:
                load_kv(0)
                load_kv(1)
            wsl = load_wq(0)
            qprep_a(0, wsl)
            qcur = qprep_b(0)
            npair = nkt // 2
            ROWSUM_PAT = (('pe', 0), ('dve', 1), ('pe', 0), ('dve', 2))
            ACCK = [[('ffA', k_, 0), ('ffA', k_, 1)] for k_ in range(3)]

            def finish_head(hh):
                lb = 6 + (hh % 2)
                S.op('pe', lambda e: e.matmul(bank(lb)[:, 0:N], lhsT=ones, rhs=lsb[:, 0:N], start=False, stop=True),
                     r=['cst', 'lsb'], w=[bk(lb)])
                S.op('dve', lambda e: e.reciprocal(out=rstd[:, 0:N], in_=bank(lb)[:, 0:N]), r=[bk(lb)], w=['rstd'])
                S.op('pool', lambda e: e.tensor_tensor(out=oT[:, hh, 0:N], in0=osb[:, 0:N], in1=rstd[:, 0:N], op=ALU.mult),
                     r=['osb', 'rstd'], w=[('oT', hh)])

            NPS = 2 * NPT
            LOOK = 3
            for h in range(NH):
                if h + 1 < NH:
                    wsl_n = load_wq(h + 1)
                p_of = {}
                accinit = [False, False, False]
                lb_ = 6 + (h % 2)

                def pslot(s_):
                    return pT[s_ // 2][:, (s_ % 2) * 512:(s_ % 2) * 512 + N]

                def emit_qk(j, h=h, qcur=qcur):
                    sbk = 1 + (j % 4)
                    ci, jj = divmod(j, CK)
                    slot = jslot[base_job + h * nck + ci]
                    S.op('pe', lambda e: e.matmul(bank(sbk)[:, 0:N], lhsT=kring[slot][:, jj * 128:(jj + 1) * 128],
                                                  rhs=qT[qcur][:, 0:N], start=True, stop=True),
                         r=[('kv', slot), ('qT', qcur)], w=[bk(sbk)])
                    ps_ = cnt['p'] % NPS
                    cnt['p'] += 1
                    p_of[j] = ps_
                    S.op('act', lambda e: e.activation(out=pslot(ps_), in_=bank(sbk)[:, 0:N], func=AF.Exp),
                         r=[bk(sbk)], w=[('pT', ps_)])

                def emit_pv(j, h=h):
                    ps_ = p_of[j]
                    ci, jj = divmod(j, CK)
                    slot = jslot[base_job + h * nck + ci]
                    S.op('pe', lambda e: e.matmul(bank(0)[:, 0:N], lhsT=vring[slot][:, jj, :], rhs=pslot(ps_),
                                                  start=(j == 0), stop=(j == nkt - 1)),
                         r=[('kv', slot), ('pT', ps_)], w=[bk(0)])
                    eng_, k_ = ROWSUM_PAT[j % len(ROWSUM_PAT)]
                    if eng_ == 'pe':
                        S.op('pe', lambda e: e.matmul(bank(lb_)[:, 0:N], lhsT=onesb[:], rhs=pslot(ps_),
                                                      start=(j == 0), stop=False),
                             r=['onesb', ('pT', ps_)], w=[bk(lb_)])
                        return
                    acc = ffAf[k_]
                    if not accinit[k_]:
                        accinit[k_] = True
                        S.op(eng_, lambda e: e.tensor_copy(out=acc[:, 0:N], in_=pslot(ps_)), r=[('pT', ps_)], w=ACCK[k_])
                    else:
                        S.op(eng_, lambda e: e.tensor_tensor(out=acc[:, 0:N], in0=acc[:, 0:N], in1=pslot(ps_), op=ALU.add),
                             r=[('pT', ps_)] + ACCK[k_], w=ACCK[k_])

                for j in range(min(LOOK, nkt)):
                    cn = j // CK
                    if (base_job + h * nck + cn) not in jslot:
                        load_kv(base_job + h * nck + cn)
                    emit_qk(j)
                for j in range(nkt):
                    ci, jj = divmod(j, CK)
                    if jj == 0:
                        load_kv(base_job + h * nck + ci + 1)
                        load_kv(base_job + h * nck + ci + 2)
                    if j + LOOK < nkt:
                        cn = (j + LOOK) // CK
                        if (base_job + h * nck + cn) not in jslot:
                            load_kv(base_job + h * nck + cn)
                        emit_qk(j + LOOK)
                    emit_pv(j)
                    if j == 0 and h + 1 < NH:
                        qprep_a(h + 1, wsl_n)
                        qnext = qprep_b(h + 1, stages=(0,))
                    if j == min(8, nkt - 3) and h > 0:
                        finish_head(h - 1)
                    if j == min(14, nkt - 2) and h + 1 < NH:
                        qprep_b(h + 1, stages=(1,), qi=qnext)
                    if j == min(20, nkt - 1) and h + 1 < NH:
                        qprep_b(h + 1, stages=(2,), qi=qnext)
                if h + 1 < NH:
                    if (base_job + (h + 1) * nck) not in jslot:
                        load_kv(base_job + (h + 1) * nck)
                S.op('act', lambda e: e.activation(out=osb[:, 0:N], in_=bank(0)[:, 0:N], func=AF.Copy), r=[bk(0)], w=['osb'])
                used = [k_ for k_ in (1, 2) if accinit[k_]]
                assert used
                if len(used) == 2:
                    S.op('dve', lambda e: e.tensor_tensor(out=lsb[:, 0:N], in0=ffAf[1][:, 0:N], in1=ffAf[2][:, 0:N], op=ALU.add),
                         r=ACCK[1] + ACCK[2], w=['lsb'])
                else:
                    u0 = used[0]
                    S.op('dve', lambda e: e.tensor_copy(out=lsb[:, 0:N], in_=ffAf[u0][:, 0:N]), r=ACCK[u0], w=['lsb'])
                if h + 1 < NH:
                    qcur = qnext
            finish_head(NH - 1)
            nb_ = (wi_ + 1) * NH * nck
            load_kv(nb_)
            load_kv(nb_ + 1)
            def load_wo(m):
                i = cnt['wo'] % 2
                cnt['wo'] += 1
                S.op('sp', lambda e: e.dma_start(out=wo_t[i][:].rearrange("p c j -> p (c j)"), in_=wo_s[m]),
                     w=[('wo', i)], dma=('wo', i))
                return i
            osl = {0: load_wo(0)}
            for m in range(8):
                if m + 1 < 8:
                    osl[m + 1] = load_wo(m + 1)
                wsl = osl[m]
                bnk = 5 if m % 2 == 0 else 7
                for h in range(NH):
                    S.op('pe', lambda e, h=h, wsl=wsl, bnk=bnk: e.matmul(bank(bnk)[:, 0:N], lhsT=wo_t[wsl][:, h, :], rhs=oT[:, h, 0:N],
                                                                         start=(h == 0), stop=(h == NH - 1)),
                         r=[('wo', wsl), ('oT', h)], w=[bk(bnk)])
                resid_add(N, m, bnk, None)
            ffn(N, 0)
            def shifted_add(eng, out_t, in_t, sh, keys_r, keys_w, first):
                pass
            fm_norm(N, 'pool_norm', 0, None, None)
            for c in (1, 0, 3, 2, 5, 4, 6, 7):
                g = c // 2
                on_pool = c in (1, 3, 5)
                eng_ = 'pool' if on_pool else 'dve'
                hbuf, hk = (ffA[2][:, 0, :], [('ffA', 2, 0)]) if on_pool else (hF, ['hF'])
                res, rk = (ffA[1][:, 1, :], [('ffA', 1, 1)]) if on_pool else (tmpr[0], [('tmpr', 0)])
                norm_chunk(N, c, 'pool_norm', 0, hbuf[:, 0:N], hk)
                window_sums(hbuf, hk, g, res, rk, eng_, POOL_T if on_pool else DVE_T)
                S.op(eng_, lambda e, g=g, res=res: e.tensor_tensor(out=res[:, 0:N], in0=res[:, 0:N], in1=icnt[:, g, 0:N], op=ALU.mult),
                     r=rk + [('icnt', g)], w=rk)
                S.op(eng_, lambda e, c=c, res=res, hbuf=hbuf: e.tensor_tensor(out=plT[:, c, 0:N], in0=res[:, 0:N], in1=hbuf[:, 0:N],
                                                                              op=ALU.subtract), r=rk + hk, w=[('oT', c)])
            for m in range(8):
                g = m // 2
                mo = m % 2
                bnk = 5 if m % 2 == 0 else 7
                for cc in range(2):
                    S.op('pe', lambda e, g=g, mo=mo, cc=cc, bnk=bnk: e.matmul(
                        bank(bnk)[:, 0:N], lhsT=wp_t[:, 2 * g + cc, mo * 128:(mo + 1) * 128], rhs=plT[:, 2 * g + cc, 0:N],
                        start=(cc == 0), stop=(cc == 1)), r=['wp_t', ('oT', 2 * g + cc)], w=[bk(bnk)])
                resid_add(N, m, bnk, vcol('pool_scale', m))
            ffn(N, 1)
            c0 = HL
            while c0 < HL + no:
                nt = min(128, HL + no - c0)
                i = cnt['xw'] % 2
                cnt['xw'] += 1
                for f in range(8):
                    bnk = 2 + (f // 4)
                    S.op('pe', lambda e, f=f, c0=c0, nt=nt, bnk=bnk: e.transpose(
                        bank(bnk)[0:nt, (f % 4) * 128:(f % 4 + 1) * 128], xT[:, f, c0:c0 + nt], ident),
                        r=[('xT', f), 'cst'], w=[bk(bnk)])
                    if f % 4 == 3:
                        hh = f // 4
                        if hh == 0:
                            S.op('act', lambda e, i=i, nt=nt, bnk=bnk, hh=hh: e.activation(
                                out=xw[i][0:nt, hh * 512:(hh + 1) * 512], in_=bank(bnk)[0:nt, :], func=AF.Copy),
                                r=[bk(bnk)], w=[('xw', i)])
                        else:
                            S.op('dve', lambda e, i=i, nt=nt, bnk=bnk, hh=hh: e.tensor_copy(
                                out=xw[i][0:nt, hh * 512:(hh + 1) * 512], in_=bank(bnk)[0:nt, :]),
                                r=[bk(bnk)], w=[('xw', i)])
                orow = o0 + (c0 - HL)
                S.op('sp', lambda e, i=i, nt=nt, orow=orow, si=si: e.dma_start(out=yout[si][orow:orow + nt, :], in_=xw[i][0:nt, :]),
                     r=[('xw', i)], dma=('xw', i))
                c0 += nt

    S.finish()
    es_b.close()
    es_all.close()
    return nc, S


def rope_tables(pos):
    pos = np.asarray(pos, dtype=np.int64)
    row = (pos // GRID_W).astype(np.float32)
    col = (pos % GRID_W).astype(np.float32)
    F = 32
    inv = (np.float32(10000.0) ** (-(np.arange(F, dtype=np.float32) / np.float32(F)))).astype(np.float32)
    ang_r = (row[None, :] * inv[:, None]).astype(np.float32)
    ang_c = (col[None, :] * inv[:, None]).astype(np.float32)
    cos = np.empty((128, len(pos)), np.float32)
    sin = np.empty((128, len(pos)), np.float32)
    for a, ang in ((0, ang_r), (1, ang_c)):
        c = np.cos(ang).astype(np.float32)
        s = np.sin(ang).astype(np.float32)
        cos[a * 64:a * 64 + 32] = c
        cos[a * 64 + 32:a * 64 + 64] = c
        sin[a * 64:a * 64 + 32] = -s
        sin[a * 64 + 32:a * 64 + 64] = s
    return cos, sin


def const_mats():
    ident = np.eye(128, dtype=np.float32)
    perm = np.zeros((128, 128), np.float32)
    for d in range(128):
        h = (d % 64) // 32
        partner = d + 32 if h == 0 else d - 32
        perm[partner, d] = 1.0
    ones = np.ones((128, 128), np.float32)
    return np.ascontiguousarray(np.concatenate([ident, perm, ones], axis=1))


def pack_vecs(inp):
    voff, NV = vec_layout()
    v = np.zeros((128, NV), np.float32)
    def cols(a):
        return np.asarray(a, np.float32).reshape(-1, 128).T
    v[:, voff['attn_norm']:voff['attn_norm'] + 8] = cols(inp['attn_norm'][0])
    v[:, voff['pool_norm']:voff['pool_norm'] + 8] = cols(inp['pool_norm'][0])
    v[:, voff['pool_scale']:voff['pool_scale'] + 8] = cols(inp['pool_scale'][0])
    for l in range(2):
        v[:, voff['ffn_norm'] + 8 * l:voff['ffn_norm'] + 8 * l + 8] = cols(inp['ffn_norm'][l])
        for k in range(3):
            c0 = voff['conv_w'] + l * 132 + k * 44
            v[:, c0:c0 + 44] = cols(inp['conv_w'][l, k])
        c0 = voff['conv_b'] + l * 44
        v[:, c0:c0 + 44] = cols(inp['conv_b'][l])
    v[:, voff['q_gain']] = np.asarray(inp['q_gain'][0], np.float32)
    v[:, voff['k_gain']] = np.asarray(inp['k_gain'][0], np.float32)
    return v


def make_in_maps(cfg, inp, n_cores=8):
    xp = np.asarray(inp['x_prompt'], np.float32)
    xs = np.asarray(inp['x_sample'], np.float32)
    npc = cfg.SP // cfg.NOP
    nsc = cfg.SS // cfg.NOS
    cosk, sink = rope_tables(np.arange(cfg.SP))
    shared = dict(
        cosk=cosk, sink=sink, consts=const_mats(), vecs=pack_vecs(inp),
        gbc=np.ascontiguousarray(np.broadcast_to(np.asarray(inp['attn_norm'][0], np.float32)[None, :], (128, D))),
        w_qkv=np.ascontiguousarray(np.asarray(inp['w_qkv'][0], np.float32)),
        w_o=np.ascontiguousarray(np.asarray(inp['w_o'][0], np.float32)),
        w_pool=np.ascontiguousarray(np.asarray(inp['w_pool'][0], np.float32).reshape(4 * 256, 256)),
        w_up=np.ascontiguousarray(np.asarray(inp['w_up'], np.float32)),
        w_down=np.ascontiguousarray(np.asarray(inp['w_down'], np.float32)),
    )
    maps = []
    for c in range(n_cores):
        m = dict(shared)
        for (tag, x, S_, NO, per) in (('p', xp, cfg.SP, cfg.NOP, npc), ('s', xs, cfg.SS, cfg.NOS, nsc)):
            b, part = divmod(c, per)
            q0 = part * NO
            NL = NO + HL + HR
            pos = np.arange(q0 - HL, q0 - HL + NL)
            valid = (pos >= 0) & (pos < S_)
            xl = np.zeros((NL, D), np.float32)
            xl[valid] = x[b, pos[valid]]
            cq, sq_ = rope_tables(np.where(valid, pos, 0))
            m['xkv_' + tag] = np.ascontiguousarray(x[b])
            m['xq_' + tag] = xl
            m['cosq_' + tag] = cq
            m['sinq_' + tag] = sq_
            m['mask_' + tag] = np.ascontiguousarray(np.broadcast_to(valid.astype(np.float32)[None, :], (128, NL)))
        maps.append(m)
    return maps


_CACHE = {}


def run_cfg(cfg, inp, n_cores=8, trace=False):
    key = (cfg.SP, cfg.SS, cfg.NOP, cfg.NOS)
    if key not in _CACHE:
        _CACHE[key] = build(cfg)
    nc, S = _CACHE[key]
    maps = make_in_maps(cfg, inp, n_cores)
    res = run_bass_kernel_spmd(nc, maps, core_ids=list(range(n_cores)), trace=trace)
    B = inp['x_prompt'].shape[0]
    Bs = inp['x_sample'].shape[0]
    yp = np.zeros((B, cfg.SP, D), np.float32)
    ys = np.zeros((Bs, cfg.SS, D), np.float32)
    npc = cfg.SP // cfg.NOP
    nsc = cfg.SS // cfg.NOS
    for c in range(n_cores):
        b, part = divmod(c, npc)
        yp[b, part * cfg.NOP:(part + 1) * cfg.NOP] = res.results[c]['y_p']
        b, part = divmod(c, nsc)
        ys[b, part * cfg.NOS:(part + 1) * cfg.NOS] = res.results[c]['y_s']
    return (yp, ys), res


def kernel(**inputs):
    cfg = Cfg()
    (yp, ys), _ = run_cfg(cfg, inputs)
    return (yp, ys)
```

```python
import sys
import numpy as np
from contextlib import ExitStack
import concourse.bass as bass
import concourse.mybir as mybir
from concourse.bass_utils import run_bass_kernel_spmd

F32 = mybir.dt.float32
BF16 = mybir.dt.bfloat16
AF = mybir.ActivationFunctionType
ALU = mybir.AluOpType

D = 1024
NH = 8
NKV = 2
HD = 128
DFF = 2816
NFC = DFF // 128
EPS = 1e-6
GRID_W = 64
HL, HR = 10, 9


class _Rec:
    def __getattr__(self, name):
        def f(*a, **k):
            self.call = (name, a, k)
            return self
        return f


class Sched:
    CE = ('pe', 'act', 'dve', 'pool')

    def __init__(self, nc, es):
        self.nc = nc
        self.es = es
        self.eng = dict(pe=nc.tensor, act=nc.scalar, dve=nc.vector, pool=nc.gpsimd, sp=nc.sync)
        self.ops = []
        self.tags = {}
        self.sem = {e: es.enter_context(nc.semaphore("sem_" + e)) for e in self.CE}
        self.slot_sem = {}
        self.cnt = {e: 0 for e in self.CE}
        self.slot_cnt = {}
        self.waited = {e: {} for e in self.eng}
        self.pend = {e: {} for e in self.eng}
        self.stats = dict(n_ops=0, n_wait=0)

    def op(self, eng, fn, r=(), w=(), dma=None):
        rec = _Rec()
        fn(rec)
        name, a, k = rec.call
        self.tags.setdefault(eng, []).append(sys._getframe(1).f_lineno)
        self.ops.append((eng, (lambda E, name=name, a=a, k=k: getattr(E, name)(*a, **k)), tuple(r), tuple(w), dma))

    def _wait(self, eng, key, val):
        if val <= 0 or self.waited[eng].get(key, 0) >= val:
            return
        self.waited[eng][key] = val
        s = self.slot_sem[key[1]] if key[0] == 's' else self.sem[key[1]]
        self.eng[eng].wait_ge(s, val)
        self.stats['n_wait'] += 1

    def flush(self):
        nc = self.nc
        ops = self.ops
        self.ops = []
        n = len(ops)
        lastw = {}
        readers = {}
        deps = [None] * n
        last_on = {}
        for i, (eng, fn, r, w, dma) in enumerate(ops):
            d = set()
            for k in r:
                j = lastw.get(k)
                if j is not None:
                    d.add(j)
                if isinstance(k, tuple) and k[0] in ('ps', 'psb'):
                    for j in readers.get(k, ()):
                        if ops[j][0] != eng:
                            d.add(j)
            for k in w:
                j = lastw.get(k)
                if j is not None:
                    d.add(j)
                for j in readers.get(k, ()):
                    d.add(j)
            d.discard(i)
            if eng == 'pe':
                d = {j for j in d if ops[j][0] != 'pe'}
            deps[i] = d
            for k in r:
                readers.setdefault(k, []).append(i)
            for k in w:
                lastw[k] = i
                readers[k] = []
            if dma is None:
                last_on[eng] = i
        signal = [False] * n
        for i in range(n):
            for j in deps[i]:
                if ops[j][4] is None:
                    signal[j] = True
        for e, i in last_on.items():
            signal[i] = True
        sigval = [0] * n
        for i, (eng, fn, r, w, dma) in enumerate(ops):
            if dma is not None and dma not in self.slot_sem:
                self.slot_sem[dma] = self.es.enter_context(nc.semaphore("dq%d" % len(self.slot_sem)))
                self.slot_cnt[dma] = 0
            if self.pend[eng]:
                for key, val in self.pend[eng].items():
                    self._wait(eng, key, val)
                self.pend[eng] = {}
            need = {}
            for j in deps[i]:
                if ops[j][4] is not None:
                    key = ('s', ops[j][4])
                    val = self.slot_cnt[ops[j][4]]
                else:
                    key = ('e', ops[j][0])
                    val = sigval[j]
                if need.get(key, 0) < val:
                    need[key] = val
            for key, val in need.items():
                self._wait(eng, key, val)
            ins = fn(self.eng[eng])
            if dma is not None:
                self.slot_cnt[dma] += 16
                ins.then_inc(self.slot_sem[dma], 16)
            elif signal[i]:
                self.cnt[eng] += 1
                ins.then_inc(self.sem[eng], 1)
                sigval[i] = self.cnt[eng]
        self.stats['n_ops'] += n
        for e in self.eng:
            for ce in self.CE:
                if self.pend[e].get(('e', ce), 0) < self.cnt[ce]:
                    self.pend[e][('e', ce)] = self.cnt[ce]
            for s, v in self.slot_cnt.items():
                if self.pend[e].get(('s', s), 0) < v:
                    self.pend[e][('s', s)] = v

    def finish(self):
        self.flush()
        for key, val in self.pend['sp'].items():
            self._wait('sp', key, val)
        self.pend['sp'] = {}


class Cfg:
    def __init__(self, SP=16384, SS=4096, NOP=4096, NOS=2048, nwp=9, nws=5, CK=16):
        self.SP, self.SS, self.NOP, self.NOS = SP, SS, NOP, NOS
        self.CK = CK
        self.seqs = []
        for (S, NO, nw, tag) in ((SP, NOP, nwp, 'p'), (SS, NOS, nws, 's')):
            so = -(-NO // nw)
            wins = []
            o = 0
            while o < NO:
                no = min(so, NO - o)
                wins.append((o, no))
                o += no
            assert max(w[1] for w in wins) + HL + HR <= 512
            self.seqs.append(dict(S=S, NO=NO, NL=NO + HL + HR, wins=wins, tag=tag))


def vec_layout():
    off = {}
    c = 0
    for name, ncol in (('attn_norm', 8), ('pool_norm', 8), ('pool_scale', 8), ('ffn_norm', 16),
                       ('q_gain', 1), ('k_gain', 1), ('conv_w', 2 * 3 * 44), ('conv_b', 2 * 44)):
        off[name] = c
        c += ncol
    return off, c


def build(cfg):
    nc = bass.Bass("TRN2", target_bir_lowering=False)
    es_all = ExitStack()
    S = Sched(nc, es_all)
    voff, NV = vec_layout()

    def din(name, shape, dt=F32):
        return nc.dram_tensor(name, list(shape), dt, kind="ExternalInput").ap()

    def dscr(name, shape, dt=BF16):
        return nc.dram_tensor(name, list(shape), dt, kind="Internal").ap()

    sq = cfg.seqs
    xkv = [din("xkv_" + s['tag'], [s['S'], D]) for s in sq]
    xq = [din("xq_" + s['tag'], [s['NL'], D]) for s in sq]
    cosk = din("cosk", [128, cfg.SP])
    sink = din("sink", [128, cfg.SP])
    cosq = [din("cosq_" + s['tag'], [128, s['NL']]) for s in sq]
    sinq = [din("sinq_" + s['tag'], [128, s['NL']]) for s in sq]
    maskd = [din("mask_" + s['tag'], [128, s['NL']]) for s in sq]
    consts = din("consts", [128, 3 * 128])
    vecs = din("vecs", [128, NV])
    gbc = din("gbc", [128, D])
    w_qkv = din("w_qkv", [D, 1536])
    w_o = din("w_o", [D, D])
    w_pool = din("w_pool", [4 * 256, 256])
    w_up = din("w_up", [2, D, 2 * DFF])
    w_down = din("w_down", [2, DFF, D])
    yout = [nc.dram_tensor("y_" + s['tag'], [s['NO'], D], F32, kind="ExternalOutput").ap() for s in sq]

    wq_s = dscr("wq_s", [8, 128, 1024])
    wk_s = dscr("wk_s", [2, 128, 1024])
    wv_s = dscr("wv_s", [128, 2048])
    wo_s = dscr("wo_s", [8, 128, 1024])
    wup_s = dscr("wup_s", [2, NFC, 128, 2, 1024])
    wd_s = dscr("wd_s", [2, 8, 128, NFC * 128])
    wp_s = dscr("wp_s", [128, 2048])
    kT_s = [dscr("kT_" + s['tag'], [2, 128, s['S']]) for s in sq]
    v_s = [dscr("v_" + s['tag'], [2, 128, s['S'] // 128, 128]) for s in sq]

    def sb(es, name, shape, dt=F32):
        return es.enter_context(nc.sbuf_tensor(name, list(shape), dt))

    cst = sb(es_all, "cst", [128, 384])
    vec = sb(es_all, "vec", [128, NV])
    onesb = sb(es_all, "onesb", [128, 128], BF16)
    epst = sb(es_all, "epst", [128, 1])
    epsc = epst[:, 0:1]
    identb = sb(es_all, "identb", [128, 128], BF16)
    ident = cst[:, 0:128]
    perm = cst[:, 128:256]
    ones = cst[:, 256:384]
    S.op('sp', lambda e: e.dma_start(out=cst[:], in_=consts[:, :]), w=['cst'], dma='cst')
    S.op('sp', lambda e: e.dma_start(out=vec[:], in_=vecs[:, :]), w=['vec'], dma='vec')
    S.op('dve', lambda e: e.tensor_copy(out=onesb[:], in_=ones), r=['cst'], w=['onesb'])
    S.op('dve', lambda e: e.memset(epst[:], EPS), w=['epsc'])
    S.op('dve', lambda e: e.tensor_copy(out=identb[:], in_=ident), r=['cst'], w=['identb'])

    for l_ in range(2):
        for k_ in range(3):
            c0_ = voff['conv_w'] + l_ * 132 + k_ * 44 + NFC
            S.op('dve', lambda e, c0_=c0_: e.tensor_scalar(out=vec[:, c0_:c0_ + NFC], in0=vec[:, c0_:c0_ + NFC], scalar1=0.5,
                                                           scalar2=None, op0=ALU.mult), r=['vec'], w=['vec'])
        c0_ = voff['conv_b'] + l_ * 44 + NFC
        S.op('dve', lambda e, c0_=c0_: e.tensor_scalar(out=vec[:, c0_:c0_ + NFC], in0=vec[:, c0_:c0_ + NFC], scalar1=0.5,
                                                       scalar2=None, op0=ALU.mult), r=['vec'], w=['vec'])

    def vcol(name, i=0):
        c = voff[name] + i
        return vec[:, c:c + 1]

    PS = {}

    def bank(b):
        return PS['t'][:, b, :]

    def bk(b):
        return ('ps', b)

    es_w = ExitStack()
    STG = 4096
    NST = 2
    stf = [sb(es_w, "stf%d" % i, [128, STG]) for i in range(NST)]
    stb = [sb(es_w, "stb%d" % i, [128, STG], BF16) for i in range(NST)]
    wstate = dict(i=0)
    cast_engs = ['act', 'act', 'act']

    def prep(src, dst, nrc, cw, blocked, dkey=None):
        i = wstate['i']
        wstate['i'] += 1
        f, b = stf[i % NST], stb[i % NST]
        kf, kb = ('stf', i % NST), ('stb', i % NST)
        nel = nrc * cw
        fv = f[:, 0:nel].rearrange("p (c n) -> p c n", c=nrc)
        S.op('sp', lambda e: e.dma_start(out=fv, in_=src), w=[kf], dma=kf)
        if wstate.get('pend'):
            wstate.pop('pend')()
        ce = cast_engs[i % 3]
        if blocked:
            nb = cw // 128
            iv = f[:, 0:nel].rearrange("p (c b j) -> p b c j", c=nrc, b=nb)
            ov = b[:, 0:nel].rearrange("p (b c j) -> p b c j", c=nrc, b=nb)
            dv = b[:, 0:nel].rearrange("p (b x) -> p b x", b=nb)
        else:
            iv = f[:, 0:nel]
            ov = b[:, 0:nel]
            dv = b[:, 0:nel]
        if ce == 'act':
            if blocked:
                for bb in range(nb):
                    S.op('act', lambda e, bb=bb: e.activation(out=ov[:, bb], in_=iv[:, bb], func=AF.Copy),
                         r=[kf], w=[kb])
            else:
                S.op('act', lambda e: e.activation(out=ov, in_=iv, func=AF.Copy), r=[kf], w=[kb])
        else:
            if blocked:
                for bb in range(nb):
                    S.op(ce, lambda e, bb=bb: e.tensor_copy(out=ov[:, bb], in_=iv[:, bb]), r=[kf], w=[kb])
            else:
                S.op(ce, lambda e: e.tensor_copy(out=ov, in_=iv), r=[kf], w=[kb])
        def store_():
            S.op('pool', lambda e: e.dma_start(out=dst, in_=dv), r=[kb], w=([dkey] if dkey else []), dma=('wst', i % NST))
        if dkey:
            store_()
        else:
            wstate['pend'] = store_

    def rows(w2d, nrc):
        return w2d.rearrange("(c p) n -> p c n", p=128)

    prep_steps = []

    def P(*args, **kw):
        prep_steps.append(lambda: prep(*args, **kw))
    qv = rows(w_qkv, 8)
    prep(qv[:, :, 1024:1280], wk_s[:, :, :].rearrange("b p x -> p b x"), 8, 256, True, dkey='wk_s')
    prep(qv[:, :, 1280:1536], wv_s[:, :], 8, 256, False, dkey='wv_s')
    for sl in range(2):
        P(qv[:, :, sl * 512:(sl + 1) * 512], wq_s[sl * 4:(sl + 1) * 4].rearrange("b p x -> p b x"), 8, 512, True)
    ov_ = rows(w_o, 8)
    for sl in range(2):
        P(ov_[:, :, sl * 512:(sl + 1) * 512], wo_s[sl * 4:(sl + 1) * 4].rearrange("b p x -> p b x"), 8, 512, True)
    P(rows(w_pool, 8), wp_s[:, :], 8, 256, False)
    for l in range(2):
        uv = rows(w_up[l], 8)
        for half in range(2):
            for b0 in range(0, NFC, 4):
                nb = min(4, NFC - b0)
                c0 = half * DFF + b0 * 128
                P(uv[:, :, c0:c0 + nb * 128],
                  wup_s[l, b0:b0 + nb, :, half, :].rearrange("b p x -> p b x"), 8, nb * 128, True)
        dv_ = rows(w_down[l], NFC)
        for m in range(8):
            P(dv_[:, :, m * 128:(m + 1) * 128], wd_s[l, m:m + 1].rearrange("b p x -> p b x"), NFC, 128, True)

    def rope_head(T, src_ps_bank, b_ss, b_pm, gain_col, cos_t, sin_t, N, out_bf, out_keys, post_scale, tagk, stages=(0, 1, 2), rope_key=None, aux='pool'):
        kg, sqt, t1, t2, rs = T['kg'], T['sq'], T['t1'], T['t2'], T['rs']
        src = bank(src_ps_bank)[:, 0:N]
        rk_ = rope_key if rope_key is not None else tagk + 'rope'
        if 0 in stages:
            S.op('act', lambda e: e.activation(out=kg[:, 0:N], in_=src, func=AF.Copy, scale=gain_col),
                 r=[bk(src_ps_bank), 'vec'], w=[tagk + 'kg'])
            S.op('act', lambda e: e.activation(out=sqt[:, 0:N], in_=src, func=AF.Square),
                 r=[bk(src_ps_bank)], w=[tagk + 'sq'])
        if 1 in stages:
            S.op('pe', lambda e: e.matmul(bank(b_ss)[:, 0:N], lhsT=ones, rhs=sqt[:, 0:N], start=True, stop=True),
                 r=['cst', tagk + 'sq'], w=[bk(b_ss)])
            S.op('act', lambda e: e.activation(out=rs[:, 0:N], in_=bank(b_ss)[:, 0:N], func=AF.Ln, scale=1.0 / HD, bias=epsc),
                 r=[bk(b_ss), 'epsc'], w=[tagk + 'rs'])
            S.op('act', lambda e: e.activation(out=rs[:, 0:N], in_=rs[:, 0:N], func=AF.Exp, scale=-0.5),
                 r=[tagk + 'rs'], w=[tagk + 'rs'])
        if 2 in stages:
            S.op('pe', lambda e: e.matmul(bank(b_pm)[:, 0:N], lhsT=perm, rhs=kg[:, 0:N], start=True, stop=True),
                 r=['cst', tagk + 'kg'], w=[bk(b_pm)])
            S.op('dve', lambda e: e.tensor_tensor(out=t2[:, 0:N], in0=bank(b_pm)[:, 0:N], in1=sin_t, op=ALU.mult),
                 r=[bk(b_pm), rk_], w=[tagk + 't2'])
            S.op(aux, lambda e: e.tensor_tensor(out=t1[:, 0:N], in0=kg[:, 0:N], in1=cos_t, op=ALU.mult),
                 r=[tagk + 'kg', rk_], w=[tagk + 't1'])
            S.op(aux, lambda e: e.tensor_tensor(out=t1[:, 0:N], in0=t1[:, 0:N], in1=t2[:, 0:N], op=ALU.add),
                 r=[tagk + 't1', tagk + 't2'], w=[tagk + 't1'])
            S.op('dve', lambda e: e.scalar_tensor_tensor(out=out_bf, in0=t1[:, 0:N], scalar=post_scale, in1=rs[:, 0:N],
                                                         op0=ALU.mult, op1=ALU.mult),
                 r=[tagk + 't1', tagk + 'rs'], w=out_keys)

    es_a = es_w
    PS['t'] = es_a.enter_context(nc.psum_tensor("psA", [128, 6, 512], F32))
    psb = es_a.enter_context(nc.psum_tensor("psb", [128, 2, 1024], BF16))
    xa = [sb(es_a, "xa%d" % i, [128, 4, D]) for i in range(3)]
    xn = [sb(es_a, "xn%d" % i, [128, 4, D], BF16) for i in range(2)]
    hTa = [sb(es_a, "hTa%d" % i, [128, 8, 512], BF16) for i in range(2)]
    wk_t = sb(es_a, "wk_t", [128, 2, 8, 128], BF16)
    wv_t = sb(es_a, "wv_t", [128, 8, 256], BF16)
    gbc_t = sb(es_a, "gbc_t", [128, D])
    ropeA = [sb(es_a, "ropeA%d" % i, [128, 2, 512]) for i in range(3)]
    TA2 = [{k: sb(es_a, "TA%d_" % g_ + k, [128, 512]) for k in ('kg', 'sq', 't1', 't2', 'rs')} for g_ in range(2)]
    ssa = [sb(es_a, "ssa%d" % i, [128, 8]) for i in range(2)]
    kout = [sb(es_a, "kout%d" % i, [128, 512], BF16) for i in range(4)]
    vout = [sb(es_a, "vout%d" % i, [128, 4, 256], BF16) for i in range(2)]

    S.op('sp', lambda e: e.dma_start(out=wk_t[:].rearrange("p a c j -> p a (c j)"),
                                     in_=wk_s[:, :, :].rearrange("a p x -> p a x")), r=['wk_s'], w=['wk_t'], dma='wk_t')
    S.op('sp', lambda e: e.dma_start(out=wv_t[:].rearrange("p c j -> p (c j)"), in_=wv_s[:, :]),
         r=['wv_s'], w=['wv_t'], dma='wv_t')
    S.op('sp', lambda e: e.dma_start(out=gbc_t[:], in_=gbc[:, :]), w=['gbc_t'], dma='gbc_t')

    tiles = [(si, ti) for si, s in enumerate(sq) for ti in range(s['S'] // 512)]

    def loadA(t):
        if t >= len(tiles):
            return
        si, ti = tiles[t]
        t0 = ti * 512
        p3 = t % 3
        xat, rp = xa[p3], ropeA[p3]
        kx = ('xa', p3)
        S.op('sp', lambda e: e.dma_start(out=xat[:], in_=xkv[si][t0:t0 + 512, :].rearrange("(c p) d -> p c d", p=128)),
             w=[kx], dma=kx)
        kr = ('ropeA', p3)
        S.op('sp', lambda e: e.dma_start(out=rp[:, 0, :], in_=cosk[:, t0:t0 + 512]), w=[kr], dma=kr)
        S.op('sp', lambda e: e.dma_start(out=rp[:, 1, :], in_=sink[:, t0:t0 + 512]), w=[kr], dma=kr)

    def frontA(t):
        si, ti = tiles[t]
        t0 = ti * 512
        pb = t % 2
        p3 = t % 3
        xat, xnt, hTt, rp, sst = xa[p3], xn[pb], hTa[pb], ropeA[p3], ssa[pb]
        kx = ('xa', p3)
        for c in range(4):
            S.op('act', lambda e, c=c: e.activation(out=xnt[:, c, :], in_=xat[:, c, :], func=AF.Square,
                                                    accum_out=sst[:, c:c + 1]),
                 r=[kx], w=[('ssa', pb, c), ('xn', pb, c)])
        S.op('act', lambda e: e.activation(out=sst[:, 4:8], in_=sst[:, 0:4], func=AF.Ln, scale=1.0 / D, bias=epsc),
             r=[('ssa', pb, c) for c in range(4)] + ['epsc'], w=[('ssa2', pb)])
        S.op('act', lambda e: e.activation(out=sst[:, 4:8], in_=sst[:, 4:8], func=AF.Exp, scale=-0.5),
             r=[('ssa2', pb)], w=[('ssa2', pb)])
        for c in range(4):
            S.op('dve', lambda e, c=c: e.scalar_tensor_tensor(
                out=xnt[:, c, :], in0=xat[:, c, :], scalar=sst[:, 4 + c:5 + c], in1=gbc_t[:],
                op0=ALU.mult, op1=ALU.mult), r=[kx, ('ssa2', pb), 'gbc_t'], w=[('xn', pb, c)])
        for f in range(8):
            for c in range(4):
                S.op('pe', lambda e, f=f, c=c: e.transpose(psb[:, f % 2, c * 128:(c + 1) * 128],
                                                           xnt[:, c, f * 128:(f + 1) * 128], identb[:]),
                     r=[('xn', pb, c), 'identb'], w=[('psb', f % 2)])
            if f % 2 == 0:
                S.op('act', lambda e, f=f: e.activation(out=hTt[:, f, :], in_=psb[:, f % 2, 0:512], func=AF.Copy),
                     r=[('psb', f % 2)], w=[('hTa', pb, f)])
            else:
                S.op('dve', lambda e, f=f: e.tensor_copy(out=hTt[:, f, :], in_=psb[:, f % 2, 0:512]),
                     r=[('psb', f % 2)], w=[('hTa', pb, f)])

    def backA(t):
        si, ti = tiles[t]
        t0 = ti * 512
        pb = t % 2
        p3 = t % 3
        hTt, rp = hTa[pb], ropeA[p3]
        vo = vout[pb]
        vok = ('vout', pb)

        def vpart(cs):
            for c in cs:
                pbv = 2 + c % 2
                for f in range(8):
                    S.op('pe', lambda e, c=c, f=f, pbv=pbv: e.matmul(bank(pbv)[:, 0:256], lhsT=hTt[:, f, c * 128:(c + 1) * 128],
                                                                     rhs=wv_t[:, f, :], start=(f == 0), stop=(f == 7)),
                         r=['wv_t', ('hTa', pb, f)], w=[bk(pbv)])
                if c % 2 == 0:
                    S.op('act', lambda e, c=c, pbv=pbv: e.activation(out=vo[:, c, :], in_=bank(pbv)[:, 0:256], func=AF.Copy),
                         r=[bk(pbv)], w=[vok])
                else:
                    S.op('dve', lambda e, c=c, pbv=pbv: e.tensor_copy(out=vo[:, c, :], in_=bank(pbv)[:, 0:256]),
                         r=[bk(pbv)], w=[vok])

        def rope(g, stages):
            ko = kout[(t * 2 + g) % 4]
            kok = ('kout', (t * 2 + g) % 4)
            rope_head(TA2[g], g, 4 + g, 4 + g, vcol('k_gain'), rp[:, 0, :], rp[:, 1, :], 512, ko[:], [kok], 1.0, 'A%d' % g,
                      stages=stages, rope_key=('ropeA', p3), aux='dve')
            if 2 in stages:
                S.op('pool', lambda e: e.dma_start(out=kT_s[si][g, :, t0:t0 + 512], in_=ko[:]), r=[kok], dma=kok)

        for g in range(NKV):
            for f in range(8):
                S.op('pe', lambda e, g=g, f=f: e.matmul(bank(g)[:, :], lhsT=wk_t[:, g, f, :], rhs=hTt[:, f, :],
                                                        start=(f == 0), stop=(f == 7)),
                     r=['wk_t', ('hTa', pb, f)], w=[bk(g)])
        for g in range(NKV):
            rope(g, (0,))
        vpart((0, 1))
        for g in range(NKV):
            rope(g, (1,))
        vpart((2, 3))
        for g in range(NKV):
            rope(g, (2,))
        for g in range(NKV):
            S.op('pool', lambda e, g=g: e.dma_start(out=v_s[si][g, :, ti * 4:(ti + 1) * 4, :], in_=vo[:, :, g * 128:(g + 1) * 128]),
                 r=[vok], dma=vok)

    loadA(0)
    loadA(1)
    frontA(0)
    per_tile = -(-len(prep_steps) // max(1, len(tiles) - 4))
    for t in range(len(tiles)):
        loadA(t + 2)
        if t + 1 < len(tiles):
            frontA(t + 1)
        for _ in range(per_tile):
            if prep_steps:
                prep_steps.pop(0)()
        backA(t)
    while prep_steps:
        prep_steps.pop(0)()
    if wstate.get('pend'):
        wstate.pop('pend')()
    S.flush()
    es_a.close()

    es_b = ExitStack()
    PS['t'] = es_b.enter_context(nc.psum_tensor("psB", [128, 8, 512], F32))
    xT = sb(es_b, "xT", [128, 8, 512])
    hT = sb(es_b, "hT", [128, 8, 512], BF16)
    hF = sb(es_b, "hF", [128, 512])
    oT = sb(es_b, "oT", [128, 8, 512], BF16)
    aT = sb(es_b, "aT", [128, NFC, 512], BF16)
    plT = oT
    xw = [sb(es_b, "xw%d" % i, [128, D]) for i in range(2)]
    qT = [sb(es_b, "qT%d" % i, [128, 512], BF16) for i in range(2)]
    NPT = 5
    pT = [sb(es_b, "pT%d" % i, [128, 1024], BF16) for i in range(NPT)]
    RK = 3
    kring = [sb(es_b, "kr%d" % i, [128, cfg.CK * 128], BF16) for i in range(RK)]
    vring = [sb(es_b, "vr%d" % i, [128, cfg.CK, 128], BF16) for i in range(RK)]
    TB = {k: sb(es_b, "TB_" + k, [128, 512]) for k in ('kg', 'sq', 't1', 't2', 'rs')}
    ropeB = sb(es_b, "ropeB", [128, 2, 512])
    maskt = sb(es_b, "maskt", [128, 512])
    icnt = sb(es_b, "icnt", [128, 4, 512])
    rstd = sb(es_b, "rstd", [128, 512])
    osb = sb(es_b, "osb", [128, 512])
    lsb = sb(es_b, "lsb", [128, 512])
    tmpr = [sb(es_b, "tmpr%d" % i, [128, 512]) for i in range(2)]
    sqn = tmpr
    ffA = [sb(es_b, "ffA%d" % i, [128, 2, 512]) for i in range(4)]
    ffAf = [t_[:].rearrange("p a n -> p (a n)") for t_ in ffA]
    cg = [ffA[0][:, p_, :] for p_ in range(2)]
    cv = [ffA[1][:, p_, :] for p_ in range(2)]
    th = [ffA[2][:, p_, :] for p_ in range(2)]
    a0 = [ffA[3][:, p_, :] for p_ in range(2)]
    pa = [TB['kg'], TB['sq'], TB['t1']]
    wq_t = [sb(es_b, "wq_t%d" % i, [128, 8, 128], BF16) for i in range(2)]
    wo_t = [sb(es_b, "wo_t%d" % i, [128, 8, 128], BF16) for i in range(2)]
    wup_t = [sb(es_b, "wup_t%d" % i, [128, 2, 8, 128], BF16) for i in range(4)]
    wd_t = [sb(es_b, "wd_t%d" % i, [128, NFC, 128], BF16) for i in range(2)]
    wp_t = sb(es_b, "wp_t", [128, 8, 256], BF16)
    S.op('sp', lambda e: e.dma_start(out=wp_t[:].rearrange("p c j -> p (c j)"), in_=wp_s[:, :]), w=['wp_t'], dma='wp_t')

    cnt = dict(wq=0, wo=0, wup=0, wd=0, xw=0, kv=0, q=0, p=0, sqn=0, tmpr=0, ff=0)

    def fm_norm(N, gname, gi, out_fn, out_keys_fn, chunks=range(8)):
        for c in range(8):
            i = cnt['sqn'] % 2
            cnt['sqn'] += 1
            S.op('act', lambda e, c=c, i=i: e.activation(out=sqn[i][:, 0:N], in_=xT[:, c, 0:N], func=AF.Square),
                 r=[('xT', c)], w=[('tmpr', i)])
            S.op('pe', lambda e, c=c, i=i: e.matmul(bank(6)[:, 0:N], lhsT=ones, rhs=sqn[i][:, 0:N],
                                                    start=(c == 0), stop=(c == 7)), r=['cst', ('tmpr', i)], w=[bk(6)])
        S.op('act', lambda e: e.activation(out=rstd[:, 0:N], in_=bank(6)[:, 0:N], func=AF.Ln, scale=1.0 / D, bias=epsc),
             r=[bk(6), 'epsc'], w=['rstd'])
        S.op('act', lambda e: e.activation(out=rstd[:, 0:N], in_=rstd[:, 0:N], func=AF.Exp, scale=-0.5), r=['rstd'], w=['rstd'])

    def norm_chunk(N, c, gname, gi, out_ap, out_keys):
        S.op('dve', lambda e: e.scalar_tensor_tensor(out=out_ap, in0=xT[:, c, 0:N], scalar=vcol(gname, gi * 8 + c),
                                                     in1=rstd[:, 0:N], op0=ALU.mult, op1=ALU.mult),
             r=[('xT', c), 'rstd', 'vec'], w=out_keys)

    def resid_add(N, m, psbank, scale_col):
        i = cnt['tmpr'] % 2
        cnt['tmpr'] += 1
        t = tmpr[i]
        S.op('dve', lambda e: e.scalar_tensor_tensor(out=t[:, 0:N], in0=bank(psbank)[:, 0:N],
                                                     scalar=(1.0 if scale_col is None else scale_col),
                                                     in1=maskt[:, 0:N], op0=ALU.mult, op1=ALU.mult),
             r=[bk(psbank), 'maskt', 'vec'], w=[('tmpr', i)])
        S.op('pool', lambda e: e.tensor_tensor(out=xT[:, m, 0:N], in0=xT[:, m, 0:N], in1=t[:, 0:N], op=ALU.add),
             r=[('xT', m), ('tmpr', i)], w=[('xT', m)])

    def ffn(N, l):
        fm_norm(N, 'ffn_norm', l, None, None)
        for c in range(8):
            norm_chunk(N, c, 'ffn_norm', l, hT[:, c, 0:N], [('hT', c)])

        def load_wup(j):
            i = cnt['wup'] % 4
            cnt['wup'] += 1
            S.op('sp', lambda e: e.dma_start(out=wup_t[i][:].rearrange("p a c j -> p (a c j)"),
                                             in_=wup_s[l, j].rearrange("p a x -> p (a x)")),
                 w=[('wup', i)], dma=('wup', i))
            return i
        slots = {}
        for j in range(min(3, NFC)):
            slots[j] = load_wup(j)
        cwb = voff['conv_w'] + l * 3 * 44
        cbb = voff['conv_b'] + l * 44
        for j in range(NFC):
            if j + 3 < NFC:
                slots[j + 3] = load_wup(j + 3)
            wi = slots[j]
            par = cnt['ff'] % 2
            cnt['ff'] += 1
            bg, bv = (2, 3) if par == 0 else (4, 0)
            for half, bnk in ((0, bg), (1, bv)):
                for c in range(8):
                    S.op('pe', lambda e, half=half, bnk=bnk, c=c, wi=wi: e.matmul(
                        bank(bnk)[:, 0:N], lhsT=wup_t[wi][:, half, c, :], rhs=hT[:, c, 0:N],
                        start=(c == 0), stop=(c == 7)), r=[('wup', wi), ('hT', c)], w=[bk(bnk)])

            def taps(half):
                ch = half * NFC + j
                return [vec[:, cwb + k_ * 44 + ch: cwb + k_ * 44 + ch + 1] for k_ in range(3)] + \
                       [vec[:, cbb + ch: cbb + ch + 1]]
            kg_, kv_, kt_, ka_ = ('ffA', 0, par), ('ffA', 1, par), ('ffA', 2, par), ('ffA', 3, par)
            cgt, cvt, tht, a0t = cg[par], cv[par], th[par], a0[par]
            w0, w1, w2, bb = taps(0)
            S.op('act', lambda e: e.activation(out=cgt[:, 0:N], in_=bank(bg)[:, 0:N], func=AF.Identity, scale=w1, bias=bb),
                 r=[bk(bg), 'vec'], w=[kg_])
            S.op('act', lambda e: e.activation(out=a0t[:, 1:N], in_=bank(bg)[:, 0:N - 1], func=AF.Copy, scale=w0),
                 r=[bk(bg), 'vec'], w=[ka_])
            S.op('dve', lambda e: e.scalar_tensor_tensor(out=cgt[:, 0:N - 1], in0=bank(bg)[:, 1:N], scalar=w2,
                                                         in1=cgt[:, 0:N - 1], op0=ALU.mult, op1=ALU.add),
                 r=[bk(bg), kg_, 'vec'], w=[kg_])
            S.op('pool', lambda e: e.tensor_tensor(out=cgt[:, 1:N], in0=cgt[:, 1:N], in1=a0t[:, 1:N], op=ALU.add),
                 r=[kg_, ka_], w=[kg_])
            w0, w1, w2, bb = taps(1)
            S.op('act', lambda e: e.activation(out=cvt[:, 0:N], in_=bank(bv)[:, 0:N], func=AF.Identity, scale=w1, bias=bb),
                 r=[bk(bv), 'vec'], w=[kv_])
            S.op('dve', lambda e: e.scalar_tensor_tensor(out=cvt[:, 1:N], in0=bank(bv)[:, 0:N - 1], scalar=w0,
                                                         in1=cvt[:, 1:N], op0=ALU.mult, op1=ALU.add),
                 r=[bk(bv), kv_, 'vec'], w=[kv_])
            S.op('dve', lambda e: e.scalar_tensor_tensor(out=cvt[:, 0:N - 1], in0=bank(bv)[:, 1:N], scalar=w2,
                                                         in1=cvt[:, 0:N - 1], op0=ALU.mult, op1=ALU.add),
                 r=[bk(bv), kv_, 'vec'], w=[kv_])
            S.op('act', lambda e: e.activation(out=tht[:, 0:N], in_=cgt[:, 0:N], func=AF.Tanh, scale=0.5),
                 r=[kg_], w=[kt_])
            S.op('pool', lambda e: e.tensor_tensor(out=cvt[:, 0:N], in0=cvt[:, 0:N], in1=cgt[:, 0:N], op=ALU.mult),
                 r=[kg_, kv_], w=[kv_])
            S.op('dve', lambda e, j=j: e.scalar_tensor_tensor(out=aT[:, j, 0:N], in0=tht[:, 0:N], scalar=1.0, in1=cvt[:, 0:N],
                                                              op0=ALU.add, op1=ALU.mult),
                 r=[kt_, kv_], w=[('aT', j)])
        def load_wd(m):
            i = cnt['wd'] % 2
            cnt['wd'] += 1
            S.op('sp', lambda e: e.dma_start(out=wd_t[i][:].rearrange("p j x -> p (j x)"), in_=wd_s[l, m]),
                 w=[('wd', i)], dma=('wd', i))
            return i
        dsl = {0: load_wd(0)}
        for m in range(8):
            if m + 1 < 8:
                dsl[m + 1] = load_wd(m + 1)
            wi = dsl[m]
            bnk = 5 if m % 2 == 0 else 1
            for j in range(NFC):
                S.op('pe', lambda e, j=j, wi=wi, bnk=bnk: e.matmul(bank(bnk)[:, 0:N], lhsT=wd_t[wi][:, j, :], rhs=aT[:, j, 0:N],
                                                                   start=(j == 0), stop=(j == NFC - 1)),
                     r=[('wd', wi), ('aT', j)], w=[bk(bnk)])
            resid_add(N, m, bnk, None)

    for si, s in enumerate(sq):
        nkt = s['S'] // 128
        CK = min(cfg.CK, nkt)
        nck = nkt // CK
        jobs = [(wi_, h, ci) for wi_ in range(len(s['wins'])) for h in range(NH) for ci in range(nck)]
        jslot = {}

        def load_kv(jn, si=si, CK=CK, jobs=jobs, jslot=jslot):
            if jn >= len(jobs) or jn in jslot:
                return
            (_, h, ci) = jobs[jn]
            g = h // (NH // NKV)
            i = cnt['kv'] % RK
            cnt['kv'] += 1
            jslot[jn] = i
            S.op('sp', lambda e: e.dma_start(out=kring[i][:, 0:CK * 128], in_=kT_s[si][g, :, ci * CK * 128:(ci + 1) * CK * 128]),
                 w=[('kv', i)], dma=('kv', i))
            S.op('sp', lambda e: e.dma_start(out=vring[i][:, 0:CK, :], in_=v_s[si][g, :, ci * CK:(ci + 1) * CK, :]),
                 w=[('kv', i)], dma=('kv', i))

        for wi_, (o0, no) in enumerate(s['wins']):
            N = no + HL + HR
            r0 = o0
            ntc = -(-N // 128)
            S.op('sp', lambda e, r0=r0, N=N, si=si: e.dma_start(out=maskt[:, 0:N], in_=maskd[si][:, r0:r0 + N]),
                 w=['maskt'], dma='maskt')
            S.op('sp', lambda e, r0=r0, N=N, si=si: e.dma_start(out=ropeB[:, 0, 0:N], in_=cosq[si][:, r0:r0 + N]),
                 w=['Brope'], dma='ropeB')
            S.op('sp', lambda e, r0=r0, N=N, si=si: e.dma_start(out=ropeB[:, 1, 0:N], in_=sinq[si][:, r0:r0 + N]),
                 w=['Brope'], dma='ropeB')
            for tc in range(ntc):
                nt = min(128, N - tc * 128)
                i = cnt['xw'] % 2
                cnt['xw'] += 1
                S.op('sp', lambda e, i=i, tc=tc, nt=nt, r0=r0, si=si: e.dma_start(
                    out=xw[i][0:nt, :], in_=xq[si][r0 + tc * 128: r0 + tc * 128 + nt, :]), w=[('xw', i)], dma=('xw', i))
                for f in range(8):
                    bnk = f % 4
                    S.op('pe', lambda e, i=i, f=f, tc=tc, nt=nt, bnk=bnk: e.transpose(
                        bank(bnk)[:, tc * 128: tc * 128 + nt], xw[i][0:nt, f * 128:(f + 1) * 128], ident[0:nt, 0:nt]),
                        r=[('xw', i), 'cst'], w=[bk(bnk)])
                    if f % 2 == 0:
                        S.op('act', lambda e, f=f, tc=tc, nt=nt, bnk=bnk: e.activation(
                            out=xT[:, f, tc * 128: tc * 128 + nt], in_=bank(bnk)[:, tc * 128: tc * 128 + nt], func=AF.Copy),
                            r=[bk(bnk)], w=[('xT', f)])
                    else:
                        S.op('dve', lambda e, f=f, tc=tc, nt=nt, bnk=bnk: e.tensor_copy(
                            out=xT[:, f, tc * 128: tc * 128 + nt], in_=bank(bnk)[:, tc * 128: tc * 128 + nt]),
                            r=[bk(bnk)], w=[('xT', f)])

            def window_sums(cur, ck, g, out_ap, out_keys, eng, temps):
                steps = [(1, 0), (1, 1), (2, 2), (4, 4)]
                lo, hi = 0, N
                for lvl in range(g + 1):
                    sl, sr = steps[lvl]
                    dst, dk = (out_ap, out_keys) if lvl == g else temps[lvl]
                    nlo, nhi = lo + sl, hi - sr
                    S.op(eng, lambda e, dst=dst, cur=cur, nlo=nlo, nhi=nhi, sl=sl, sr=sr: e.tensor_tensor(
                        out=dst[:, nlo:nhi], in0=cur[:, nlo - sl:nhi - sl], in1=cur[:, nlo + sr:nhi + sr], op=ALU.add),
                        r=ck, w=dk)
                    cur, ck, lo, hi = dst, dk, nlo, nhi

            DVE_T = [(TB['kg'], ['Bkg']), (TB['sq'], ['Bsq']), (TB['t1'], ['Bt1'])]
            POOL_T = [(ffA[0][:, 0, :], [('ffA', 0, 0)]), (ffA[0][:, 1, :], [('ffA', 0, 1)]), (ffA[1][:, 0, :], [('ffA', 1, 0)])]
            for g in range(4):
                S.op('pool', lambda e, g=g: e.tensor_copy(out=icnt[:, g, 0:N], in_=maskt[:, 0:N]), r=['maskt'], w=[('icnt', g)])
                window_sums(maskt, ['maskt'], g, icnt[:, g, :], [('icnt', g)], 'pool', POOL_T)
                S.op('dve', lambda e, g=g: e.tensor_scalar(out=icnt[:, g, 0:N], in0=icnt[:, g, 0:N], scalar1=1.0, scalar2=None,
                                                           op0=ALU.max), r=[('icnt', g)], w=[('icnt', g)])
                S.op('dve', lambda e, g=g: e.reciprocal(out=icnt[:, g, 0:N], in_=icnt[:, g, 0:N]),
                     r=[('icnt', g)], w=[('icnt', g)])
            fm_norm(N, 'attn_norm', 0, None, None)
            for c in range(8):
                norm_chunk(N, c, 'attn_norm', 0, hT[:, c, 0:N], [('hT', c)])

            def load_wq(h):
                i = cnt['wq'] % 2
                cnt['wq'] += 1
                S.op('sp', lambda e: e.dma_start(out=wq_t[i][:].rearrange("p c j -> p (c j)"), in_=wq_s[h]),
                     w=[('wq', i)], dma=('wq', i))
                return i

            def qprep_a(h, wslot):
                for c in range(8):
                    S.op('pe', lambda e, c=c: e.matmul(bank(5)[:, 0:N], lhsT=wq_t[wslot][:, c, :], rhs=hT[:, c, 0:N],
                                                       start=(c == 0), stop=(c == 7)),
                         r=[('wq', wslot), ('hT', c)], w=[bk(5)])

            def qprep_b(h, stages=(0, 1, 2), qi=None):
                if qi is None:
                    qi = cnt['q'] % 2
                    cnt['q'] += 1
                xb_ = 6 + (h % 2)
                rope_head(TB, 5, xb_, xb_, vcol('q_gain'), ropeB[:, 0, 0:N], ropeB[:, 1, 0:N], N, qT[qi][:, 0:N],
                          [('qT', qi)], float(HD) ** -0.5, 'B', stages=stages)
                return qi

            base_job = wi_ * NH * nck
            if wi_ == 0:
                load_kv(0)
                load_kv(1)
            wsl = load_wq(0)
            qprep_a(0, wsl)
            qcur = qprep_b(0)
            npair = nkt // 2
            ROWSUM_PAT = (('pe', 0), ('dve', 1), ('pe', 0), ('dve', 2))
            ACCK = [[('ffA', k_, 0), ('ffA', k_, 1)] for k_ in range(3)]

            def finish_head(hh):
                lb = 6 + (hh % 2)
                S.op('pe', lambda e: e.matmul(bank(lb)[:, 0:N], lhsT=ones, rhs=lsb[:, 0:N], start=False, stop=True),
                     r=['cst', 'lsb'], w=[bk(lb)])
                S.op('dve', lambda e: e.reciprocal(out=rstd[:, 0:N], in_=bank(lb)[:, 0:N]), r=[bk(lb)], w=['rstd'])
                S.op('pool', lambda e: e.tensor_tensor(out=oT[:, hh, 0:N], in0=osb[:, 0:N], in1=rstd[:, 0:N], op=ALU.mult),
                     r=['osb', 'rstd'], w=[('oT', hh)])

            NPS = 2 * NPT
            LOOK = 3
            for h in range(NH):
                if h + 1 < NH:
                    wsl_n = load_wq(h + 1)
                p_of = {}
                accinit = [False, False, False]
                lb_ = 6 + (h % 2)

                def pslot(s_):
                    return pT[s_ // 2][:, (s_ % 2) * 512:(s_ % 2) * 512 + N]

                def emit_qk(j, h=h, qcur=qcur):
                    sbk = 1 + (j % 4)
                    ci, jj = divmod(j, CK)
                    slot = jslot[base_job + h * nck + ci]
                    S.op('pe', lambda e: e.matmul(bank(sbk)[:, 0:N], lhsT=kring[slot][:, jj * 128:(jj + 1) * 128],
                                                  rhs=qT[qcur][:, 0:N], start=True, stop=True),
                         r=[('kv', slot), ('qT', qcur)], w=[bk(sbk)])
                    ps_ = cnt['p'] % NPS
                    cnt['p'] += 1
                    p_of[j] = ps_
                    S.op('act', lambda e: e.activation(out=pslot(ps_), in_=bank(sbk)[:, 0:N], func=AF.Exp),
                         r=[bk(sbk)], w=[('pT', ps_)])

                def emit_pv(j, h=h):
                    ps_ = p_of[j]
                    ci, jj = divmod(j, CK)
                    slot = jslot[base_job + h * nck + ci]
                    S.op('pe', lambda e: e.matmul(bank(0)[:, 0:N], lhsT=vring[slot][:, jj, :], rhs=pslot(ps_),
                                                  start=(j == 0), stop=(j == nkt - 1)),
                         r=[('kv', slot), ('pT', ps_)], w=[bk(0)])
                    eng_, k_ = ROWSUM_PAT[j % len(ROWSUM_PAT)]
                    if eng_ == 'pe':
                        S.op('pe', lambda e: e.matmul(bank(lb_)[:, 0:N], lhsT=onesb[:], rhs=pslot(ps_),
                                                      start=(j == 0), stop=False),
                             r=['onesb', ('pT', ps_)], w=[bk(lb_)])
                        return
                    acc = ffAf[k_]
                    if not accinit[k_]:
                        accinit[k_] = True
                        S.op(eng_, lambda e: e.tensor_copy(out=acc[:, 0:N], in_=pslot(ps_)), r=[('pT', ps_)], w=ACCK[k_])
                    else:
                        S.op(eng_, lambda e: e.tensor_tensor(out=acc[:, 0:N], in0=acc[:, 0:N], in1=pslot(ps_), op=ALU.add),
                             r=[('pT', ps_)] + ACCK[k_], w=ACCK[k_])

                for j in range(min(LOOK, nkt)):
                    cn = j // CK
                    if (base_job + h * nck + cn) not in jslot:
                        load_kv(base_job + h * nck + cn)
                    emit_qk(j)
                for j in range(nkt):
                    ci, jj = divmod(j, CK)
                    if jj == 0:
                        load_kv(base_job + h * nck + ci + 1)
                        load_kv(base_job + h * nck + ci + 2)
                    if j + LOOK < nkt:
                        cn = (j + LOOK) // CK
                        if (base_job + h * nck + cn) not in jslot:
                            load_kv(base_job + h * nck + cn)
                        emit_qk(j + LOOK)
                    emit_pv(j)
                    if j == 0 and h + 1 < NH:
                        qprep_a(h + 1, wsl_n)
                        qnext = qprep_b(h + 1, stages=(0,))
                    if j == min(8, nkt - 3) and h > 0:
                        finish_head(h - 1)
                    if j == min(14, nkt - 2) and h + 1 < NH:
                        qprep_b(h + 1, stages=(1,), qi=qnext)
                    if j == min(20, nkt - 1) and h + 1 < NH:
                        qprep_b(h + 1, stages=(2,), qi=qnext)
                if h + 1 < NH:
                    if (base_job + (h + 1) * nck) not in jslot:
                        load_kv(base_job + (h + 1) * nck)
                S.op('act', lambda e: e.activation(out=osb[:, 0:N], in_=bank(0)[:, 0:N], func=AF.Copy), r=[bk(0)], w=['osb'])
                used = [k_ for k_ in (1, 2) if accinit[k_]]
                assert used
                if len(used) == 2:
                    S.op('dve', lambda e: e.tensor_tensor(out=lsb[:, 0:N], in0=ffAf[1][:, 0:N], in1=ffAf[2][:, 0:N], op=ALU.add),
                         r=ACCK[1] + ACCK[2], w=['lsb'])
                else:
                    u0 = used[0]
                    S.op('dve', lambda e: e.tensor_copy(out=lsb[:, 0:N], in_=ffAf[u0][:, 0:N]), r=ACCK[u0], w=['lsb'])
                if h + 1 < NH:
                    qcur = qnext
            finish_head(NH - 1)
            nb_ = (wi_ + 1) * NH * nck
            load_kv(nb_)
            load_kv(nb_ + 1)
            def load_wo(m):
                i = cnt['wo'] % 2
                cnt['wo'] += 1
                S.op('sp', lambda e: e.dma_start(out=wo_t[i][:].rearrange("p c j -> p (c j)"), in_=wo_s[m]),
                     w=[('wo', i)], dma=('wo', i))
                return i
            osl = {0: load_wo(0)}
            for m in range(8):
                if m + 1 < 8:
                    osl[m + 1] = load_wo(m + 1)
                wsl = osl[m]
                bnk = 5 if m % 2 == 0 else 7
                for h in range(NH):
                    S.op('pe', lambda e, h=h, wsl=wsl, bnk=bnk: e.matmul(bank(bnk)[:, 0:N], lhsT=wo_t[wsl][:, h, :], rhs=oT[:, h, 0:N],
                                                                         start=(h == 0), stop=(h == NH - 1)),
                         r=[('wo', wsl), ('oT', h)], w=[bk(bnk)])
                resid_add(N, m, bnk, None)
            ffn(N, 0)
            def shifted_add(eng, out_t, in_t, sh, keys_r, keys_w, first):
                pass
            fm_norm(N, 'pool_norm', 0, None, None)
            for c in (1, 0, 3, 2, 5, 4, 6, 7):
                g = c // 2
                on_pool = c in (1, 3, 5)
                eng_ = 'pool' if on_pool else 'dve'
                hbuf, hk = (ffA[2][:, 0, :], [('ffA', 2, 0)]) if on_pool else (hF, ['hF'])
                res, rk = (ffA[1][:, 1, :], [('ffA', 1, 1)]) if on_pool else (tmpr[0], [('tmpr', 0)])
                norm_chunk(N, c, 'pool_norm', 0, hbuf[:, 0:N], hk)
                window_sums(hbuf, hk, g, res, rk, eng_, POOL_T if on_pool else DVE_T)
                S.op(eng_, lambda e, g=g, res=res: e.tensor_tensor(out=res[:, 0:N], in0=res[:, 0:N], in1=icnt[:, g, 0:N], op=ALU.mult),
                     r=rk + [('icnt', g)], w=rk)
                S.op(eng_, lambda e, c=c, res=res, hbuf=hbuf: e.tensor_tensor(out=plT[:, c, 0:N], in0=res[:, 0:N], in1=hbuf[:, 0:N],
                                                                              op=ALU.subtract), r=rk + hk, w=[('oT', c)])
            for m in range(8):
                g = m // 2
                mo = m % 2
                bnk = 5 if m % 2 == 0 else 7
                for cc in range(2):
                    S.op('pe', lambda e, g=g, mo=mo, cc=cc, bnk=bnk: e.matmul(
                        bank(bnk)[:, 0:N], lhsT=wp_t[:, 2 * g + cc, mo * 128:(mo + 1) * 128], rhs=plT[:, 2 * g + cc, 0:N],
                        start=(cc == 0), stop=(cc == 1)), r=['wp_t', ('oT', 2 * g + cc)], w=[bk(bnk)])
                resid_add(N, m, bnk, vcol('pool_scale', m))
            ffn(N, 1)
            c0 = HL
            while c0 < HL + no:
                nt = min(128, HL + no - c0)
                i = cnt['xw'] % 2
                cnt['xw'] += 1
                for f in range(8):
                    bnk = 2 + (f // 4)
                    S.op('pe', lambda e, f=f, c0=c0, nt=nt, bnk=bnk: e.transpose(
                        bank(bnk)[0:nt, (f % 4) * 128:(f % 4 + 1) * 128], xT[:, f, c0:c0 + nt], ident),
                        r=[('xT', f), 'cst'], w=[bk(bnk)])
                    if f % 4 == 3:
                        hh = f // 4
                        if hh == 0:
                            S.op('act', lambda e, i=i, nt=nt, bnk=bnk, hh=hh: e.activation(
                                out=xw[i][0:nt, hh * 512:(hh + 1) * 512], in_=bank(bnk)[0:nt, :], func=AF.Copy),
                                r=[bk(bnk)], w=[('xw', i)])
                        else:
                            S.op('dve', lambda e, i=i, nt=nt, bnk=bnk, hh=hh: e.tensor_copy(
                                out=xw[i][0:nt, hh * 512:(hh + 1) * 512], in_=bank(bnk)[0:nt, :]),
                                r=[bk(bnk)], w=[('xw', i)])
                orow = o0 + (c0 - HL)
                S.op('sp', lambda e, i=i, nt=nt, orow=orow, si=si: e.dma_start(out=yout[si][orow:orow + nt, :], in_=xw[i][0:nt, :]),
                     r=[('xw', i)], dma=('xw', i))
                c0 += nt

    S.finish()
    es_b.close()
    es_all.close()
    return nc, S


def rope_tables(pos):
    pos = np.asarray(pos, dtype=np.int64)
    row = (pos // GRID_W).astype(np.float32)
    col = (pos % GRID_W).astype(np.float32)
    F = 32
    inv = (np.float32(10000.0) ** (-(np.arange(F, dtype=np.float32) / np.float32(F)))).astype(np.float32)
    ang_r = (row[None, :] * inv[:, None]).astype(np.float32)
    ang_c = (col[None, :] * inv[:, None]).astype(np.float32)
    cos = np.empty((128, len(pos)), np.float32)
    sin = np.empty((128, len(pos)), np.float32)
    for a, ang in ((0, ang_r), (1, ang_c)):
        c = np.cos(ang).astype(np.float32)
        s = np.sin(ang).astype(np.float32)
        cos[a * 64:a * 64 + 32] = c
        cos[a * 64 + 32:a * 64 + 64] = c
        sin[a * 64:a * 64 + 32] = -s
        sin[a * 64 + 32:a * 64 + 64] = s
    return cos, sin


def const_mats():
    ident = np.eye(128, dtype=np.float32)
    perm = np.zeros((128, 128), np.float32)
    for d in range(128):
        h = (d % 64) // 32
        partner = d + 32 if h == 0 else d - 32
        perm[partner, d] = 1.0
    ones = np.ones((128, 128), np.float32)
    return np.ascontiguousarray(np.concatenate([ident, perm, ones], axis=1))


def pack_vecs(inp):
    voff, NV = vec_layout()
    v = np.zeros((128, NV), np.float32)
    def cols(a):
        return np.asarray(a, np.float32).reshape(-1, 128).T
    v[:, voff['attn_norm']:voff['attn_norm'] + 8] = cols(inp['attn_norm'][0])
    v[:, voff['pool_norm']:voff['pool_norm'] + 8] = cols(inp['pool_norm'][0])
    v[:, voff['pool_scale']:voff['pool_scale'] + 8] = cols(inp['pool_scale'][0])
    for l in range(2):
        v[:, voff['ffn_norm'] + 8 * l:voff['ffn_norm'] + 8 * l + 8] = cols(inp['ffn_norm'][l])
        for k in range(3):
            c0 = voff['conv_w'] + l * 132 + k * 44
            v[:, c0:c0 + 44] = cols(inp['conv_w'][l, k])
        c0 = voff['conv_b'] + l * 44
        v[:, c0:c0 + 44] = cols(inp['conv_b'][l])
    v[:, voff['q_gain']] = np.asarray(inp['q_gain'][0], np.float32)
    v[:, voff['k_gain']] = np.asarray(inp['k_gain'][0], np.float32)
    return v


def make_in_maps(cfg, inp, n_cores=8):
    xp = np.asarray(inp['x_prompt'], np.float32)
    xs = np.asarray(inp['x_sample'], np.float32)
    npc = cfg.SP // cfg.NOP
    nsc = cfg.SS // cfg.NOS
    cosk, sink = rope_tables(np.arange(cfg.SP))
    shared = dict(
        cosk=cosk, sink=sink, consts=const_mats(), vecs=pack_vecs(inp),
        gbc=np.ascontiguousarray(np.broadcast_to(np.asarray(inp['attn_norm'][0], np.float32)[None, :], (128, D))),
        w_qkv=np.ascontiguousarray(np.asarray(inp['w_qkv'][0], np.float32)),
        w_o=np.ascontiguousarray(np.asarray(inp['w_o'][0], np.float32)),
        w_pool=np.ascontiguousarray(np.asarray(inp['w_pool'][0], np.float32).reshape(4 * 256, 256)),
        w_up=np.ascontiguousarray(np.asarray(inp['w_up'], np.float32)),
        w_down=np.ascontiguousarray(np.asarray(inp['w_down'], np.float32)),
    )
    maps = []
    for c in range(n_cores):
        m = dict(shared)
        for (tag, x, S_, NO, per) in (('p', xp, cfg.SP, cfg.NOP, npc), ('s', xs, cfg.SS, cfg.NOS, nsc)):
            b, part = divmod(c, per)
            q0 = part * NO
            NL = NO + HL + HR
            pos = np.arange(q0 - HL, q0 - HL + NL)
            valid = (pos >= 0) & (pos < S_)
            xl = np.zeros((NL, D), np.float32)
            xl[valid] = x[b, pos[valid]]
            cq, sq_ = rope_tables(np.where(valid, pos, 0))
            m['xkv_' + tag] = np.ascontiguousarray(x[b])
            m['xq_' + tag] = xl
            m['cosq_' + tag] = cq
            m['sinq_' + tag] = sq_
            m['mask_' + tag] = np.ascontiguousarray(np.broadcast_to(valid.astype(np.float32)[None, :], (128, NL)))
        maps.append(m)
    return maps


_CACHE = {}


def run_cfg(cfg, inp, n_cores=8, trace=False):
    key = (cfg.SP, cfg.SS, cfg.NOP, cfg.NOS)
    if key not in _CACHE:
        _CACHE[key] = build(cfg)
    nc, S = _CACHE[key]
    maps = make_in_maps(cfg, inp, n_cores)
    res = run_bass_kernel_spmd(nc, maps, core_ids=list(range(n_cores)), trace=trace)
    B = inp['x_prompt'].shape[0]
    Bs = inp['x_sample'].shape[0]
    yp = np.zeros((B, cfg.SP, D), np.float32)
    ys = np.zeros((Bs, cfg.SS, D), np.float32)
    npc = cfg.SP // cfg.NOP
    nsc = cfg.SS // cfg.NOS
    for c in range(n_cores):
        b, part = divmod(c, npc)
        yp[b, part * cfg.NOP:(part + 1) * cfg.NOP] = res.results[c]['y_p']
        b, part = divmod(c, nsc)
        ys[b, part * cfg.NOS:(part + 1) * cfg.NOS] = res.results[c]['y_s']
    return (yp, ys), res


def kernel(**inputs):
    cfg = Cfg()
    (yp, ys), _ = run_cfg(cfg, inputs)
    return (yp, ys)
```
